# Optimizing a Trainium2 kernel written in Bass

```python
import jax, jax.numpy as jnp
from jax import lax
import numpy as np

D_MODEL = 1024
BATCH = 16
SEQ = 4096
DEPTH = 4

GRID_W = 64
CTX_LEN = 256
CONV_W = 512
CONV_K = 31
CONV_GROUPS = 8
RWKV_W = D_MODEL - CONV_W
HEAD = 64
N_HEADS = RWKV_W // HEAD
DECAY_RANK = 64
ICLR_RANK = 64
GATE_RANK = 160
SHIFT_K = 3
CONV_IN = 2 * CONV_W
RWKV_IN = 3 * RWKV_W + DECAY_RANK + ICLR_RANK + GATE_RANK
IN_COLS = CONV_IN + RWKV_IN
D_FF = -(-8 * D_MODEL // (3 * 256)) * 256
RMS_EPS = 1e-6
LN_EPS = 1e-5
GN_EPS = 64e-5
DECAY_SCALE = 0.606531

kernel_name = "hybrid_conformer_rwkv7_dit"


def rmsnorm(x, g):
    xf = x.astype(jnp.float32)
    y = xf * lax.rsqrt(jnp.mean(xf * xf, axis=-1, keepdims=True) + RMS_EPS)
    return (y * g).astype(x.dtype)


def modulate(h, shift, scale):
    return h * (1.0 + scale) + shift


def group_norm(x, n_groups, g, b, eps):
    shp = x.shape
    xg = x.astype(jnp.float32).reshape(shp[:-1] + (n_groups, shp[-1] // n_groups))
    mu = jnp.mean(xg, axis=-1, keepdims=True)
    var = jnp.mean(jnp.square(xg - mu), axis=-1, keepdims=True)
    xg = (xg - mu) * lax.rsqrt(var + eps)
    return xg.reshape(shp) * g + b


def dwconv(x, w):
    ch = x.shape[-1]
    return lax.conv_general_dilated(
        x, w[:, None, :].astype(x.dtype), window_strides=(1,), padding="SAME",
        dimension_numbers=("NWC", "WIO", "NWC"), feature_group_count=ch)


def grid_transpose(t, rows, cols):
    b, _, ch = t.shape
    return t.reshape(b, rows, cols, ch).transpose(0, 2, 1, 3).reshape(b, rows * cols, ch)


def to_heads(t):
    return t.reshape(t.shape[:-1] + (N_HEADS, HEAD))


def conformer_conv(u, conv_w, conv_b, cnorm_g, cnorm_b):
    val, gate = jnp.split(u, 2, axis=-1)
    z = val * jax.nn.sigmoid(gate)
    z = dwconv(z, conv_w) + conv_b
    z = group_norm(z, CONV_GROUPS, cnorm_g, cnorm_b, LN_EPS)
    return jax.nn.silu(z)


def rwkv_prep(u, shift_w, decay_b0, decay_up, iclr_b0, iclr_up, k_k, k_a, g_up):
    u = dwconv(u, shift_w).astype(jnp.float32)
    o1 = 3 * RWKV_W
    o2 = o1 + DECAY_RANK
    o3 = o2 + ICLR_RANK
    r, k, v, w_lo, a_lo, g_lo = jnp.split(u, [RWKV_W, 2 * RWKV_W, o1, o2, o3], axis=-1)
    kk = to_heads(k * k_k)
    kk = kk / jnp.maximum(jnp.sqrt(jnp.sum(kk * kk, axis=-1, keepdims=True)), 1e-12)
    dirs = []
    for d in range(2):
        decay = jnp.exp(-DECAY_SCALE * jax.nn.sigmoid(decay_b0[d] + jnp.tanh(w_lo) @ decay_up[d]))
        a = jax.nn.sigmoid(iclr_b0[d] + a_lo @ iclr_up[d])
        k_d = k * (1.0 + (a - 1.0) * k_a)
        dirs.append((to_heads(k_d), to_heads(decay), to_heads(a)))
    g = jax.nn.sigmoid(g_lo) @ g_up
    return to_heads(r), to_heads(k), to_heads(v), kk, dirs, g


def wkv_scan(r, k, v, decay, kk, a, s0, reverse, emit):
    xs = tuple(jnp.swapaxes(t, 0, 1) for t in (r, k, v, decay, kk, kk * a))

    def step(s, inp):
        r_t, k_t, v_t, w_t, kk_t, b_t = inp
        sa = -jnp.einsum("bhvk,bhk->bhv", s, kk_t)
        s = (s * w_t[:, :, None, :] + sa[..., None] * b_t[:, :, None, :]
             + v_t[..., None] * k_t[:, :, None, :])
        y = jnp.einsum("bhvk,bhk->bhv", s, r_t) if emit else None
        return s, y

    s_fin, ys = lax.scan(step, s0, xs, reverse=reverse)
    return (jnp.swapaxes(ys, 0, 1) if emit else None), s_fin


def rwkv_scans(prep, s0_f, s0_b, emit):
    r, k, v, kk, dirs, g = prep
    (k_f, w_f, a_f), (k_b, w_b, a_b) = dirs
    y_f, s_f = wkv_scan(r, k_f, v, w_f, kk, a_f, s0_f, False, emit)
    y_b, s_b = wkv_scan(r, k_b, v, w_b, kk, a_b, s0_b, True, emit)
    y = (y_f + y_b) if emit else None
    return y, s_f, s_b


def rwkv_out(y, prep, r_k, gn_g, gn_b):
    r, k, v, kk, dirs, g = prep
    b, t = y.shape[:2]
    y = group_norm(y.reshape(b, t, RWKV_W), N_HEADS, gn_g, gn_b, GN_EPS)
    bonus = jnp.sum(r * k * to_heads(r_k), axis=-1, keepdims=True) * v
    return (y + bonus.reshape(b, t, RWKV_W)) * g


def swiglu(h, wg, wu, wd):
    return (jax.nn.silu(h @ wg) * (h @ wu)) @ wd


def setup_inputs(seed: int = 0) -> dict:
    key = jax.random.key(seed)
    ks = iter(jax.random.split(key, 40))

    def nrm(shape, scale):
        return jax.random.normal(next(ks), shape, jnp.float32) * scale

    L = DEPTH
    shift_base = jnp.array([0.25, 1.0, 0.25], jnp.float32)[None, :, None]
    return {
        "x": nrm((BATCH, SEQ, D_MODEL), 1.0),
        "c": nrm((BATCH, D_MODEL), 1.0),
        "ctx": nrm((BATCH, CTX_LEN, D_MODEL), 1.0),
        "c_ctx": nrm((D_MODEL,), 1.0),
        "ada_w": nrm((L, D_MODEL, 6 * D_MODEL), 0.5 * D_MODEL ** -0.5),
        "ada_b": nrm((L, 6 * D_MODEL), 0.02),
        "norm1_g": 1.0 + nrm((L, D_MODEL), 0.02),
        "norm2_g": 1.0 + nrm((L, D_MODEL), 0.02),
        "w_in": nrm((L, D_MODEL, IN_COLS), D_MODEL ** -0.5),
        "conv_w": nrm((L, CONV_K, CONV_W), CONV_K ** -0.5),
        "conv_b": nrm((L, CONV_W), 0.02),
        "cnorm_g": 1.0 + nrm((L, CONV_W), 0.02),
        "cnorm_b": nrm((L, CONV_W), 0.02),
        "shift_w": shift_base + nrm((L, SHIFT_K, RWKV_IN), 0.1),
        "decay_b0": jax.random.uniform(next(ks), (L, 2, RWKV_W), jnp.float32, -6.0, 0.0),
        "decay_up": nrm((L, 2, DECAY_RANK, RWKV_W), 0.5 * DECAY_RANK ** -0.5),
        "iclr_b0": nrm((L, 2, RWKV_W), 0.5),
        "iclr_up": nrm((L, 2, ICLR_RANK, RWKV_W), 0.5 * ICLR_RANK ** -0.5),
        "k_k": 0.85 + nrm((L, RWKV_W), 0.05),
        "k_a": 1.0 + nrm((L, RWKV_W), 0.05),
        "g_up": nrm((L, GATE_RANK, RWKV_W), GATE_RANK ** -0.5),
        "r_k": nrm((L, RWKV_W), 0.1),
        "gn_g": 1.0 + nrm((L, RWKV_W), 0.02),
        "gn_b": nrm((L, RWKV_W), 0.02),
        "w_out": nrm((L, D_MODEL, D_MODEL), D_MODEL ** -0.5),
        "ffn_wg": nrm((L, D_MODEL, D_FF), D_MODEL ** -0.5),
        "ffn_wu": nrm((L, D_MODEL, D_FF), D_MODEL ** -0.5),
        "ffn_wd": nrm((L, D_FF, D_MODEL), D_FF ** -0.5),
        "final_g": 1.0 + nrm((D_MODEL,), 0.02),
    }


def reference(x, c, ctx, c_ctx, ada_w, ada_b, norm1_g, norm2_g, w_in, conv_w, conv_b,
              cnorm_g, cnorm_b, shift_w, decay_b0, decay_up, iclr_b0, iclr_up, k_k, k_a,
              g_up, r_k, gn_g, gn_b, w_out, ffn_wg, ffn_wu, ffn_wd, final_g):
    rows = x.shape[1] // GRID_W
    batch = x.shape[0]
    s_zero = jnp.zeros((batch, N_HEADS, HEAD, HEAD), jnp.float32)
    silu_c = jax.nn.silu(c)
    silu_cc = jax.nn.silu(c_ctx)
    for l in range(DEPTH):
        last = l == DEPTH - 1
        col_major = l % 2 == 1
        mod_x = (silu_c @ ada_w[l] + ada_b[l])[:, None, :]
        sh1, sc1, gt1, sh2, sc2, gt2 = jnp.split(mod_x, 6, axis=-1)
        mod_c = silu_cc @ ada_w[l] + ada_b[l]
        csh1, csc1, cgt1, csh2, csc2, cgt2 = jnp.split(mod_c, 6, axis=-1)
        rw = (shift_w[l], decay_b0[l], decay_up[l], iclr_b0[l], iclr_up[l], k_k[l], k_a[l], g_up[l])
        cv = (conv_w[l], conv_b[l], cnorm_g[l], cnorm_b[l])
        post = (r_k[l], gn_g[l], gn_b[l])

        ux = modulate(rmsnorm(x, norm1_g[l]), sh1, sc1) @ w_in[l]
        uc = modulate(rmsnorm(ctx, norm1_g[l]), csh1, csc1) @ w_in[l]
        if col_major:
            ux = grid_transpose(ux, rows, GRID_W)

        prep_c = rwkv_prep(uc[..., CONV_IN:], *rw)
        yc, s_f, s_b = rwkv_scans(prep_c, s_zero, s_zero, emit=not last)
        prep_x = rwkv_prep(ux[..., CONV_IN:], *rw)
        yx, _, _ = rwkv_scans(prep_x, s_f, s_b, emit=True)

        mix_x = jnp.concatenate(
            [conformer_conv(ux[..., :CONV_IN], *cv), rwkv_out(yx, prep_x, *post)],
            axis=-1).astype(x.dtype)
        if col_major:
            mix_x = grid_transpose(mix_x, GRID_W, rows)
        x = x + gt1 * (mix_x @ w_out[l])

        x = x + gt2 * swiglu(modulate(rmsnorm(x, norm2_g[l]), sh2, sc2),
                             ffn_wg[l], ffn_wu[l], ffn_wd[l])

        if not last:
            mix_c = jnp.concatenate(
                [conformer_conv(uc[..., :CONV_IN], *cv), rwkv_out(yc, prep_c, *post)],
                axis=-1).astype(ctx.dtype)
            ctx = ctx + cgt1 * (mix_c @ w_out[l])
            ctx = ctx + cgt2 * swiglu(modulate(rmsnorm(ctx, norm2_g[l]), csh2, csc2),
                                      ffn_wg[l], ffn_wu[l], ffn_wd[l])
    return rmsnorm(x, final_g)
```

```python
import contextlib
import os
import numpy as np
import concourse.bass as bass
import concourse.mybir as mybir
from concourse.bass_utils import run_bass_kernel_spmd

F32 = mybir.dt.float32
BF16 = mybir.dt.bfloat16
F32R = mybir.dt.float32r
AF = mybir.ActivationFunctionType
ALU = mybir.AluOpType

D = 1024
INC = 2848
DFF = 2816
NFC = 22
CK = 31
LAM = -0.606531
INV_BF16 = False
NPK = 281
O_ADAB, O_N1G, O_N2G, O_CW, O_CB, O_CG, O_CBB, O_SW, O_DB0, O_IB0, O_KK, O_KA, O_RK, O_GG, O_GB = (
    0, 48, 56, 64, 188, 192, 196, 200, 245, 253, 261, 265, 269, 273, 277)


class Buf:
    __slots__ = ("name", "t", "lw", "rd", "semi")

    def __init__(self, name, t=None):
        self.name = name
        self.t = t
        self.lw = {}
        self.rd = {}
        self.semi = None


class Ctx:
    def __init__(self, nc, gstack):
        self.nc = nc
        self.gstack = gstack
        self.stack = gstack
        self.engs = {"pe": nc.tensor, "dve": nc.vector, "act": nc.scalar, "pool": nc.gpsimd, "sp": nc.sync}
        self.esem = {}
        self.ecnt = {}
        self.known = {k: {} for k in self.engs}
        for k in self.engs:
            self.esem[k] = gstack.enter_context(nc.semaphore("e_" + k))
            self.ecnt[k] = 0
        self.dpool = []
        self.dnext = 0
        self.ninst = 0
        self.uid = 0

    def sb(self, name, shape, dt=F32):
        self.uid += 1
        return Buf(name, self.stack.enter_context(self.nc.sbuf_tensor("%s_%d" % (name, self.uid), list(shape), dt)))

    def ps(self, name, shape, dt=F32):
        self.uid += 1
        return Buf(name, self.stack.enter_context(self.nc.psum_tensor("%s_%d" % (name, self.uid), list(shape), dt)))

    def _need(self, eng, toks):
        kn = self.known[eng]
        best = {}
        for sem, val in toks:
            k = id(sem)
            if kn.get(k, 0) >= val:
                continue
            if k not in best or best[k][1] < val:
                best[k] = (sem, val)
        for sem, val in best.values():
            self.engs[eng].wait_ge(sem, val)
            kn[id(sem)] = val

    def _deps(self, eng, reads, writes):
        toks = []
        own = self.esem[eng]
        for b in reads:
            for t in b.lw.values():
                if t[0] is own and eng == "pe":
                    continue
                toks.append(t)
        for b in writes:
            for t in b.lw.values():
                if t[0] is not own:
                    toks.append(t)
            for t in b.rd.values():
                if t[0] is not own:
                    toks.append(t)
        self._need(eng, toks)

    def op(self, eng, fn, reads=(), writes=()):
        self._deps(eng, reads, writes)
        ins = fn(self.engs[eng])
        self.ecnt[eng] += 1
        tok = (self.esem[eng], self.ecnt[eng])
        ins.then_inc(self.esem[eng], 1)
        k = id(tok[0])
        for b in writes:
            b.lw[k] = tok
        for b in reads:
            b.rd[k] = tok
        self.ninst += 1
        return ins

    def dma(self, q, out_ap, in_ap, reads=(), writes=(), sembuf=None, slow=False):
        self._deps(q, reads, writes)
        sbf = sembuf
        if sbf.semi is None:
            if self.dnext >= len(self.dpool):
                self.dpool.append([self.gstack.enter_context(self.nc.semaphore("d_%d" % len(self.dpool))), 0])
            sbf.semi = self.dnext
            self.dnext += 1
        ent = self.dpool[sbf.semi]
        ins = self.engs[q].dma_start(out=out_ap, in_=in_ap, allow_slow_non_contiguous=True) if slow else self.engs[q].dma_start(out=out_ap, in_=in_ap)
        ent[1] += 16
        ins.then_inc(ent[0], 16)
        tok = (ent[0], ent[1])
        k = id(tok[0])
        for b in writes:
            b.lw[k] = tok
        for b in reads:
            b.rd[k] = tok
        self.ninst += 1
        return ins

    def presem(self, sbf):
        self.dpool.append([self.gstack.enter_context(self.nc.semaphore("d_%d" % len(self.dpool))), 0])
        sbf.semi = len(self.dpool) - 1
        self.dnext = len(self.dpool)

    def barrier(self):
        toks = [(self.esem[k], self.ecnt[k]) for k in self.engs if self.ecnt[k] > 0]
        toks += [(e[0], e[1]) for e in self.dpool if e[1] > 0]
        for k in self.engs:
            self._need(k, toks)

    @contextlib.contextmanager
    def phase(self):
        old = self.stack
        with contextlib.ExitStack() as st:
            self.stack = st
            self.dnext_save = self.dnext
            yield
            self.barrier()
        self.stack = old
        self.dnext = self.dnext_save


def build(NB, T, TC, L, dbg=False, upto=99):
    R = NB + 1
    TT = TC + T
    ROWS = T // 64
    NCHT = TT // 64
    nc = bass.Bass("TRN2", target_bir_lowering=False)
    KI = "ExternalInput"
    SK = "ExternalOutput" if dbg else "Internal"

    def dr(name, shape, dt=F32, kind=KI):
        return nc.dram_tensor(name, list(shape), dt, kind=kind).ap()

    x_in = dr("x", [NB, T, D])
    ctx_in = dr("ctx", [NB, TC, D])
    cT_in = dr("cT", [128, 8, R])
    pk_in = dr("pk", [L, 128, NPK])
    adab_in = dr("ada_b", [L, 6144])
    adaw_in = dr("ada_w", [L, D, 6144])
    win_in = dr("w_in", [L, D, INC])
    dup_in = dr("decay_up", [L, 2, 64, 512])
    iup_in = dr("iclr_up", [L, 2, 64, 512])
    gup_in = dr("g_up", [L, 160, 512])
    wout_in = dr("w_out", [L, D, D])
    wg_in = dr("ffn_wg", [L, D, DFF])
    wu_in = dr("ffn_wu", [L, D, DFF])
    wd_in = dr("ffn_wd", [L, DFF, D])
    fg_in = dr("fg", [128, D])
    out = dr("out", [NB, T, D], kind="ExternalOutput")
    Xs = dr("Xs", [NB, T, D], kind=SK)
    CXs = dr("CXs", [NB, TC, D], kind=SK)
    UT = dr("UT", [NB, INC, TT], kind=SK)
    MIXT = dr("MIXT", [NB, D, TT], BF16, kind=SK)
    FM = dr("FM", [NB, 2, 4, 512, TT], kind=SK)
    TM = dr("TM", [NB, 2, 2, TT, 512], kind=SK)
    VT = dr("VT", [NB, TT, 512], kind=SK)
    PC = dr("PC", [NB, 2, 512, NCHT], kind=SK)
    BON = dr("BON", [NB, 512, TT], kind=SK)
    GG = dr("GG", [NB, 512, TT], kind=SK)
    YT = dr("YT", [NB, 2, 512, TT], kind=SK)

    gst = contextlib.ExitStack()
    with gst:
        c = Ctx(nc, gst)
        op, dma = c.op, c.dma
        rr = [0]

        def q2():
            rr[0] += 1
            return ("sp", "pool")[rr[0] % 2]

        identf = c.sb("identf", [128, 128])
        identb = c.sb("identb", [128, 128], BF16)
        bones = c.sb("bones", [128, 128])
        MK = [c.sb("mk%d" % d, [64, 2, 64]) for d in range(2)]
        MNT = [c.sb("mnt%d" % d, [64, 64]) for d in range(2)]
        rmask = c.sb("rmask", [128, 512])
        ones1 = c.sb("ones1", [1, 128])
        sel = c.sb("sel", [R, R, 128])
        silc = c.sb("silc", [128, 8, R])
        G1 = c.sb("G1", [128, 8, R]); SH1 = c.sb("SH1", [128, 8, R])
        G2 = c.sb("G2", [128, 8, R]); SH2 = c.sb("SH2", [128, 8, R])
        gtrow = c.sb("gtrow", [R, 2, D])
        pk = c.sb("pk", [128, NPK])
        omka = c.sb("omka", [128, 4])

        def asel(buf, ap, pattern, base, cm):
            op("pool", lambda e: e.affine_select(ap, ap, pattern=pattern, compare_op=ALU.is_ge, fill=0.0,
                                                 base=base, channel_multiplier=cm), reads=[buf], writes=[buf])

        tmpi = c.sb("tmpi", [128, 128])
        op("pool", lambda e: e.memset(identf.t[:], 1.0), writes=[identf])
        asel(identf, identf.t[:], [[1, 128]], 0, -1)
        asel(identf, identf.t[:], [[-1, 128]], 0, 1)
        op("dve", lambda e: e.tensor_copy(identb.t[:], identf.t[:]), reads=[identf], writes=[identb])
        op("pool", lambda e: e.memset(bones.t[:], 0.0), writes=[bones])
        op("pool", lambda e: e.memset(bones.t[0:64, 0:64], 1.0), writes=[bones])
        op("pool", lambda e: e.memset(bones.t[64:128, 64:128], 1.0), writes=[bones])
        for d in range(2):
            op("pool", lambda e: e.memset(MK[d].t[:], 1.0), writes=[MK[d]])
            op("pool", lambda e: e.memset(MNT[d].t[:], 1.0), writes=[MNT[d]])
        asel(MK[0], MK[0].t[:, 0, :], [[1, 64]], -1, -1)
        asel(MK[0], MK[0].t[:, 1, :], [[1, 64]], 0, -1)
        asel(MK[1], MK[1].t[:, 0, :], [[-1, 64]], -1, 1)
        asel(MK[1], MK[1].t[:, 1, :], [[-1, 64]], 0, 1)
        asel(MNT[0], MNT[0].t[:], [[-1, 64]], -1, 1)
        asel(MNT[1], MNT[1].t[:], [[1, 64]], -1, -1)
        op("pool", lambda e: e.memset(rmask.t[:], 1.0), writes=[rmask])
        op("pool", lambda e: e.memset(rmask.t[:].rearrange("p (c t) -> p c t", t=64)[:, :, 0:1], 0.0), writes=[rmask])
        op("pool", lambda e: e.memset(ones1.t[:], 1.0), writes=[ones1])
        op("dve", lambda e: e.tensor_copy(sel.t[:], identf.t[0:R, 0:R].unsqueeze(2).broadcast_to([R, R, 128])),
           reads=[identf], writes=[sel])
        c.presem(pk)
        dma("sp", silc.t[:], cT_in, writes=[silc], sembuf=silc)
        op("act", lambda e: e.activation(silc.t[:], silc.t[:], AF.Silu), reads=[silc], writes=[silc])

        def x_rows(src_b, l, i, latent):
            if (not latent) or l % 2 == 0:
                return [(0, 128, src_b[i * 128:(i + 1) * 128, :])]
            ncol = 128 // ROWS
            v = src_b.rearrange("(r c) d -> c r d", c=64)
            return [(cl * ROWS, ROWS, v[i * ncol + cl]) for cl in range(ncol)]

        def rstd_of(ss, rs):
            op("act", lambda e: e.activation(rs.t[:], ss.t[:], AF.Sqrt, bias=1e-6, scale=1.0), reads=[ss], writes=[rs])
            op("dve", lambda e: e.reciprocal(rs.t[:], rs.t[:]), reads=[rs], writes=[rs])

        for l in range(L):
            last = l == L - 1
            with (c.phase() if upto >= 0 else contextlib.nullcontext(False)) as _ph0:
              if upto >= 0:
                dma("sp", pk.t[:], pk_in[l], writes=[pk], sembuf=pk)
                adab = c.sb("adab", [1, 6144])
                dma("pool", adab.t[:], adab_in[l:l + 1, :], writes=[adab], sembuf=adab)
                aw = [c.sb("aw%d" % i, [128, 8, 1024]) for i in range(2)]
                modT = c.ps("modT", [128, 48, R])
                mrow = c.ps("mrow", [R, 2, 512])
                modsb = c.sb("modsb", [128, 48, R])
                awv = adaw_in[l].rearrange("(kc p) n -> p kc n", p=128)
                for jb in range(6):
                    a = aw[jb % 2]
                    for kc in range(8):
                        dma(q2(), a.t[:, kc, :], awv[:, kc, jb * 1024:(jb + 1) * 1024], writes=[a], sembuf=a)
                    for j in range(8):
                        for kc in range(8):
                            op("pe", lambda e: e.matmul(modT.t[:, jb * 8 + j, :], a.t[:, kc, j * 128:(j + 1) * 128],
                                                        silc.t[:, kc, :], start=(kc == 0), stop=(kc == 7)),
                               reads=[a, silc], writes=[modT])
                    if jb in (2, 5):
                        wh = 0 if jb == 2 else 1
                        for nt in range(2):
                            for kc in range(8):
                                op("pe", lambda e: e.matmul(mrow.t[:, nt, :], silc.t[:, kc, :],
                                                            a.t[:, kc, nt * 512:(nt + 1) * 512], start=(kc == 0), stop=False),
                                   reads=[a, silc], writes=[mrow])
                            op("pe", lambda e: e.matmul(mrow.t[:, nt, :], ones1.t[0:1, 0:R],
                                                        adab.t[0:1, jb * 1024 + nt * 512: jb * 1024 + (nt + 1) * 512],
                                                        start=False, stop=True), reads=[adab, ones1], writes=[mrow])
                        op("dve", lambda e: e.tensor_copy(gtrow.t[:, wh, :], mrow.t[:].rearrange("p a b -> p (a b)")),
                           reads=[mrow], writes=[gtrow])
                op("dve", lambda e: e.tensor_tensor(modsb.t[:], modT.t[:],
                                                    pk.t[:, O_ADAB:O_ADAB + 48].unsqueeze(2).broadcast_to([128, 48, R]), ALU.add),
                   reads=[modT, pk], writes=[modsb])
                for (Gx, SHx, o_sh, o_sc, o_g) in ((G1, SH1, 0, 8, O_N1G), (G2, SH2, 24, 32, O_N2G)):
                    op("dve", lambda e: e.tensor_copy(SHx.t[:], modsb.t[:, o_sh:o_sh + 8, :]), reads=[modsb], writes=[SHx])
                    op("dve", lambda e: e.tensor_scalar(Gx.t[:], modsb.t[:, o_sc:o_sc + 8, :], 1.0, None, ALU.add),
                       reads=[modsb], writes=[Gx])
                    op("dve", lambda e: e.tensor_tensor(Gx.t[:], Gx.t[:],
                                                        pk.t[:, o_g:o_g + 8].unsqueeze(2).broadcast_to([128, 8, R]), ALU.mult),
                       reads=[Gx, pk], writes=[Gx])
                op("dve", lambda e: e.tensor_scalar(omka.t[:], pk.t[:, O_KA:O_KA + 4], -1.0, 1.0, ALU.mult, ALU.add),
                   reads=[pk], writes=[omka])

            def make_gt(which, bcp):
                res = []
                for r in range(R):
                    g = c.sb("gt%d" % r, [128, D])
                    for nt in range(2):
                        p = bcp[nt]
                        op("pe", lambda e: e.matmul(p.t[:], sel.t[0:R, r, :], gtrow.t[0:R, which, nt * 512:(nt + 1) * 512],
                                                    start=True, stop=True), reads=[sel, gtrow], writes=[p])
                        op("act", lambda e: e.copy(g.t[:, nt * 512:(nt + 1) * 512], p.t[:]), reads=[p], writes=[g])
                    res.append(g)
                return res

            def norm_transpose(xt, r, Gx, SHx, hT, col0, ss, rs, xn, pT):
                op("pool", lambda e: e.memset(ss.t[:], 0.0), writes=[ss])
                op("act", lambda e: e.activation(xn.t[:], xt.t[:], AF.Square, scale=1.0 / 32.0, accum_out=ss.t[:, 0:1]),
                   reads=[xt, ss], writes=[xn, ss])
                rstd_of(ss, rs)
                op("act", lambda e: e.activation(xn.t[:], xt.t[:], AF.Copy, scale=rs.t[:, 0:1]), reads=[xt, rs], writes=[xn])
                for kc in range(8):
                    op("pe", lambda e: e.transpose(pT.t[:, kc * 128:(kc + 1) * 128], xn.t[:, kc * 128:(kc + 1) * 128], identb.t[:]),
                       reads=[xn, identb], writes=[pT])
                for kc in range(8):
                    if kc % 2 == 0:
                        op("act", lambda e: e.activation(hT.t[:, kc, col0:col0 + 128], pT.t[:, kc * 128:(kc + 1) * 128], AF.Identity,
                                                         bias=SHx.t[:, kc, r:r + 1], scale=Gx.t[:, kc, r:r + 1]),
                           reads=[pT, SHx, Gx], writes=[hT])
                    else:
                        op("dve", lambda e: e.tensor_scalar(hT.t[:, kc, col0:col0 + 128], pT.t[:, kc * 128:(kc + 1) * 128],
                                                            Gx.t[:, kc, r:r + 1], SHx.t[:, kc, r:r + 1], ALU.mult, ALU.add),
                           reads=[pT, SHx, Gx], writes=[hT])

            def load_w_bf16(dst, src_ap, nk, ncols, wst, cwid=1424):
                cnt = 0
                for kc in range(nk):
                    for c0 in range(0, ncols, cwid):
                        w = min(cwid, ncols - c0)
                        s = wst[cnt % len(wst)]
                        dma(q2(), s.t[:, 0:w], src_ap[kc * 128:(kc + 1) * 128, c0:c0 + w], writes=[s], sembuf=s)
                        e_ = ("pool", "dve", "act")[cnt % 3]
                        if e_ == "act":
                            op("act", lambda e: e.copy(dst.t[:, kc, c0:c0 + w], s.t[:, 0:w]), reads=[s], writes=[dst])
                        else:
                            op(e_, lambda e: e.tensor_copy(dst.t[:, kc, c0:c0 + w], s.t[:, 0:w]), reads=[s], writes=[dst])
                        cnt += 1

            with (c.phase() if upto >= 1 else contextlib.nullcontext(False)) as _ph1:
              if upto >= 1:
                wb = c.sb("winb", [128, 8, INC], BF16)
                wst = [c.sb("wst%d" % i, [128, 1424]) for i in range(2)]
                load_w_bf16(wb, win_in[l], 8, INC, wst)
                SEG = min(T, 2048)
                hTs = [c.sb("hT%d" % i, [128, 8, SEG], BF16) for i in range(2)]
                xts = [c.sb("xt%d" % i, [128, D]) for i in range(2)]
                xns = [c.sb("xn%d" % i, [128, D], BF16) for i in range(2)]
                sss = [c.sb("ss%d" % i, [128, 1]) for i in range(2)]
                rss = [c.sb("rs%d" % i, [128, 1]) for i in range(2)]
                pTs = [c.ps("pT%d" % i, [128, D], BF16) for i in range(2)]
                pms = [c.ps("pm%d" % i, [128, 512]) for i in range(4)]
                stg = [c.sb("stg%d" % i, [128, SEG]) for i in range(2)]
                segl = []
                for b in range(NB):
                    xsrc = (x_in if l == 0 else Xs)[b]
                    csrc = (ctx_in if l == 0 else CXs)[b]
                    for (src, Ts, off, r, latent) in ((csrc, TC, 0, NB, False), (xsrc, T, TC, b, True)):
                        for s0 in range(0, Ts, SEG):
                            segl.append((b, src, off, r, latent, s0, min(SEG, Ts - s0)))
                cnt1 = {"tile": 0, "mm": 0}

                def prep_gen(si):
                    (b, src, off, r, latent, s0, S) = segl[si]
                    hT = hTs[si % 2]
                    for i in range(S // 128):
                        k = cnt1["tile"] % 2
                        cnt1["tile"] += 1
                        for (p0, npart, ap) in x_rows(src, l, s0 // 128 + i, latent):
                            dma("sp", xts[k].t[p0:p0 + npart, :], ap, writes=[xts[k]], sembuf=xts[k])
                        norm_transpose(xts[k], r, G1, SH1, hT, i * 128, sss[k], rss[k], xns[k], pTs[k])
                        yield

                def mm_gen(si):
                    (b, src, off, r, latent, s0, S) = segl[si]
                    hT = hTs[si % 2]
                    for j in range(23):
                        cw = 128 if j < 22 else 32
                        sg = stg[j % 2]
                        for nt in range(S // 512 if S >= 512 else 1):
                            n = min(512, S)
                            pm = pms[cnt1["mm"] % 4]
                            cnt1["mm"] += 1
                            for kc in range(8):
                                op("pe", lambda e: e.matmul(pm.t[0:cw, 0:n], wb.t[:, kc, j * 128:j * 128 + cw],
                                                            hT.t[:, kc, nt * 512:nt * 512 + n], start=(kc == 0), stop=(kc == 7)),
                                   reads=[wb, hT], writes=[pm])
                            if cnt1["mm"] % 2:
                                op("act", lambda e: e.copy(sg.t[0:cw, nt * 512:nt * 512 + n], pm.t[0:cw, 0:n]), reads=[pm], writes=[sg])
                            else:
                                op("dve", lambda e: e.tensor_copy(sg.t[0:cw, nt * 512:nt * 512 + n], pm.t[0:cw, 0:n]), reads=[pm], writes=[sg])
                            yield
                        dma("sp", UT[b, j * 128:j * 128 + cw, off + s0:off + s0 + S], sg.t[0:cw, 0:S], reads=[sg], sembuf=sg)

                for _ in prep_gen(0):
                    pass
                for si in range(len(segl)):
                    mg = mm_gen(si)
                    pg_ = prep_gen(si + 1) if si + 1 < len(segl) else None
                    nst = 0
                    for _ in mg:
                        nst += 1
                        if pg_ is not None and nst % 4 == 0:
                            try:
                                next(pg_)
                            except StopIteration:
                                pg_ = None
                    if pg_ is not None:
                        for _ in pg_:
                            pass

            with (c.phase() if upto >= 2 else contextlib.nullcontext(False)) as _ph2:
              if upto >= 2:
                zps = [c.sb("zp%d" % i, [128, TT + 60], F32R) for i in range(2)]
                zer = c.sb("zer", [128, 32])
                op("pool", lambda e: e.memset(zer.t[:], 0.0), writes=[zer])
                for zp in zps:
                    for (z0, zn) in ((0, 15), (15 + TC, 30), (TC + 45 + T, 15)):
                        op("pool", lambda e: e.tensor_copy(zp.t[:, z0:z0 + zn], zer.t[:, 0:zn]), reads=[zer], writes=[zp])
                dg = c.sb("dg", [128, 4 * CK, 128], F32R)
                for i in range(4 * CK):
                    e_ = ("dve", "pool")[i % 2]
                    op(e_, lambda e: e.tensor_scalar(dg.t[:, i, :], identf.t[:], pk.t[:, O_CW + i:O_CW + i + 1], None, ALU.mult),
                       reads=[identf, pk], writes=[dg])
                gates = [c.sb("gate%d" % i, [128, TT]) for i in range(1)]
                vals = [c.sb("val%d" % i, [128, TT]) for i in range(1)]
                mixst = c.sb("mixst", [128, TT], BF16)
                NT2 = 3

                def mk2(name):
                    return [c.sb("%s%d" % (name, i), [128, 512]) for i in range(NT2)]
                accs, sqts, mts, dtts, msqs, vars_ = mk2("acc"), mk2("sqt"), mk2("mt"), mk2("dtt"), mk2("msq"), mk2("var")
                pcv = [c.ps("pcv%d" % i, [128, 512]) for i in range(3)]
                pAB = [c.ps("pAB%d" % i, [128, 512]) for i in range(4)]
                tiles = [(0 + s, 0 + s, min(512, TC - s)) for s in range(0, TC, 512)]
                tiles += [(TC + 30 + s, TC + s, 512) for s in range(0, T, 512)]
                itn = 0
                ibc = 0
                its2 = [(b, cc) for b in range(NB) for cc in range(4)]

                def prep2(i2):
                    (b, cc) = its2[i2]
                    zp = zps[i2 % 2]
                    gate, val = gates[0], vals[0]
                    dma("sp", gate.t[:], UT[b, 512 + cc * 128:512 + (cc + 1) * 128, :], writes=[gate], sembuf=gate)
                    dma("sp", val.t[:], UT[b, cc * 128:(cc + 1) * 128, :], writes=[val], sembuf=val)
                    op("act", lambda e: e.activation(gate.t[:], gate.t[:], AF.Sigmoid), reads=[gate], writes=[gate])
                    op("pool", lambda e: e.tensor_tensor(zp.t[:, 15:15 + TC], val.t[:, 0:TC], gate.t[:, 0:TC], ALU.mult),
                       reads=[val, gate], writes=[zp])
                    hT_ = T // 2
                    op("dve", lambda e: e.tensor_tensor(zp.t[:, TC + 45:TC + 45 + hT_], val.t[:, TC:TC + hT_], gate.t[:, TC:TC + hT_], ALU.mult),
                       reads=[val, gate], writes=[zp])
                    op("pool", lambda e: e.tensor_tensor(zp.t[:, TC + 45 + hT_:TC + 45 + T], val.t[:, TC + hT_:TT], gate.t[:, TC + hT_:TT], ALU.mult),
                       reads=[val, gate], writes=[zp])
                prep2(0)
                if True:
                    for i2 in range(len(its2)):
                        (b, cc) = its2[i2]
                        zp = zps[i2 % 2]
                        tcount = 0
                        for (a0, o0, n) in tiles:
                            tcount += 1
                            if tcount == min(3, len(tiles)) and i2 + 1 < len(its2):
                                prep2(i2 + 1)
                            itn += 1
                            k = itn % NT2
                            acc, sqt, mt, dtt, msq, var = accs[k], sqts[k], mts[k], dtts[k], msqs[k], vars_[k]
                            pc_ = pcv[itn % 3]
                            for j in range(CK):
                                op("pe", lambda e: e.matmul(pc_.t[:, 0:n], dg.t[:, cc * CK + j, :], zp.t[:, a0 + j:a0 + j + n],
                                                            start=(j == 0), stop=(j == CK - 1)), reads=[dg, zp], writes=[pc_])
                            op("act", lambda e: e.activation(acc.t[:, 0:n], pc_.t[:, 0:n], AF.Identity, bias=pk.t[:, O_CB + cc:O_CB + cc + 1], scale=1.0),
                               reads=[pc_, pk], writes=[acc])
                            op("act", lambda e: e.activation(sqt.t[:, 0:n], acc.t[:, 0:n], AF.Square), reads=[acc], writes=[sqt])
                            pA, pB = pAB[(itn * 2) % 4], pAB[(itn * 2 + 1) % 4]
                            op("pe", lambda e: e.matmul(pA.t[:, 0:n], bones.t[:], acc.t[:, 0:n], start=True, stop=True),
                               reads=[bones, acc], writes=[pA])
                            op("pe", lambda e: e.matmul(pB.t[:, 0:n], bones.t[:], sqt.t[:, 0:n], start=True, stop=True),
                               reads=[bones, sqt], writes=[pB])
                            op("act", lambda e: e.activation(mt.t[:, 0:n], pA.t[:, 0:n], AF.Copy, scale=1.0 / 64.0), reads=[pA], writes=[mt])
                            op("dve", lambda e: e.tensor_tensor(dtt.t[:, 0:n], acc.t[:, 0:n], mt.t[:, 0:n], ALU.subtract),
                               reads=[acc, mt], writes=[dtt])
                            op("pool", lambda e: e.tensor_tensor(msq.t[:, 0:n], mt.t[:, 0:n], mt.t[:, 0:n], ALU.mult), reads=[mt], writes=[msq])
                            op("dve", lambda e: e.scalar_tensor_tensor(var.t[:, 0:n], pB.t[:, 0:n], 1.0 / 64.0, msq.t[:, 0:n],
                                                                       ALU.mult, ALU.subtract), reads=[pB, msq], writes=[var])
                            op("act", lambda e: e.activation(var.t[:, 0:n], var.t[:, 0:n], AF.Sqrt, bias=1e-5, scale=1.0), reads=[var], writes=[var])
                            op("dve", lambda e: e.reciprocal(var.t[:, 0:n], var.t[:, 0:n]), reads=[var], writes=[var])
                            op("pool", lambda e: e.tensor_tensor(dtt.t[:, 0:n], dtt.t[:, 0:n], var.t[:, 0:n], ALU.mult), reads=[dtt, var], writes=[dtt])
                            op("act", lambda e: e.activation(mixst.t[:, o0:o0 + n], dtt.t[:, 0:n], AF.Silu,
                                                             bias=pk.t[:, O_CBB + cc:O_CBB + cc + 1], scale=pk.t[:, O_CG + cc:O_CG + cc + 1]),
                               reads=[dtt, pk], writes=[mixst])
                        dma("sp", MIXT[b, cc * 128:(cc + 1) * 128, :], mixst.t[:], reads=[mixst], sembuf=mixst)

            with (c.phase() if upto >= 3 else contextlib.nullcontext(False)) as _ph3:
              if upto >= 3:
                dup = c.sb("dup", [64, 2, 512]); iup = c.sb("iup", [128, 2, 512])
                gup1 = c.sb("gup1", [128, 512]); gup2 = c.sb("gup2", [32, 512])
                for d in range(2):
                    dma("sp", dup.t[:, d, :], dup_in[l, d], writes=[dup], sembuf=dup)
                    dma("sp", iup.t[64:128, d, :], iup_in[l, d], writes=[iup], sembuf=iup)
                dma("pool", gup1.t[:], gup_in[l, 0:128, :], writes=[gup1], sembuf=gup1)
                dma("pool", gup2.t[:], gup_in[l, 128:160, :], writes=[gup2], sembuf=gup2)
                S3 = 512
                NS = 2
                pp = [c.ps("pp%d" % i, [128, 512]) for i in range(8)]
                pi = [0]

                def nps():
                    pi[0] += 1
                    return pp[pi[0] % 8]

                def mk(name, shape=None, dt=F32):
                    return [c.sb("%s%d" % (name, i), shape or [128, S3], dt) for i in range(NS)]
                inb = {k_: mk("in" + k_, [128, S3 + 2]) for k_ in ("r", "k", "v", "wa", "g1", "g2")}
                B_ = {k_: mk(k_) for k_ in ("r", "k", "v", "wa", "g1", "g2", "gst", "rk", "bon", "kkr", "sq", "rn", "kk",
                                             "sig", "a", "t1", "kd", "bd", "cs", "ex", "rem", "cb", "E1", "E2", "E3", "E4",
                                             "kh", "bh")}
                for k_ in ("o0", "o1", "o2", "o3"):
                    B_[k_] = mk(k_, None, F32)
                tms = {k_: mk("tm" + k_, [128, 4, 128]) for k_ in ("v", "kh", "bh")}
                pcs = mk("pcs", [128, 8])
                it = 0

                def shift(dst, ib, ch, S, g0, first, lastseg, b, rows=128):
                    f0 = 1024 + ch * 128
                    lo = g0 - (0 if first else 1)
                    hi = g0 + S + (0 if lastseg else 1)
                    if first:
                        op("pool", lambda e: e.memset(ib.t[0:rows, 0:1], 0.0), writes=[ib])
                    if lastseg:
                        op("pool", lambda e: e.memset(ib.t[0:rows, S + 1:S + 2], 0.0), writes=[ib])
                    dma("sp", ib.t[0:rows, (1 if first else 0):(1 if first else 0) + hi - lo], UT[b, f0:f0 + rows, lo:hi], writes=[ib], sembuf=ib)
                    sw = O_SW + ch * 3
                    op("act", lambda e: e.activation(dst.t[0:rows, 0:S], ib.t[0:rows, 0:S], AF.Copy, scale=pk.t[0:rows, sw:sw + 1]),
                       reads=[ib, pk], writes=[dst])
                    for j in (1, 2):
                        op("dve", lambda e: e.scalar_tensor_tensor(dst.t[0:rows, 0:S], ib.t[0:rows, j:j + S], pk.t[0:rows, sw + j:sw + j + 1],
                                                                   dst.t[0:rows, 0:S], ALU.mult, ALU.add), reads=[ib, pk, dst], writes=[dst])

                def hp_task(b, g0, S, first, lastseg, hp, ks, wa, g1, g2):
                    NCS = S // 64
                    pend = []

                    def st(dst_ap, src_ap, buf):
                        pend.append((dst_ap, src_ap, buf))

                    def flush():
                        for (dst_ap, src_ap, buf) in pend:
                            dma("sp", dst_ap, src_ap, reads=[buf], sembuf=buf)
                        del pend[:]
                    X = {k_: v_[ks] for k_, v_ in B_.items()}
                    r_, k_b, v_ = X["r"], X["k"], X["v"]
                    shift(r_, inb["r"][ks], hp, S, g0, first, lastseg, b)
                    yield
                    shift(k_b, inb["k"][ks], 4 + hp, S, g0, first, lastseg, b)
                    yield
                    shift(v_, inb["v"][ks], 8 + hp, S, g0, first, lastseg, b)
                    fs = slice(hp * 128, (hp + 1) * 128)
                    p = nps()
                    op("pe", lambda e: e.matmul(p.t[:, 0:S], gup1.t[:, fs], g1.t[:, 0:S], start=True, stop=False), reads=[gup1, g1], writes=[p])
                    op("pe", lambda e: e.matmul(p.t[:, 0:S], gup2.t[0:32, fs], g2.t[0:32, 0:S], start=False, stop=True), reads=[gup2, g2], writes=[p])
                    yield
                    gs = X["gst"]
                    op("act", lambda e: e.copy(gs.t[:, 0:S], p.t[:, 0:S]), reads=[p], writes=[gs])
                    st(GG[b, fs, g0:g0 + S], gs.t[:, 0:S], gs)
                    rk = X["rk"]
                    op("dve", lambda e: e.scalar_tensor_tensor(rk.t[:, 0:S], r_.t[:, 0:S], pk.t[:, O_RK + hp:O_RK + hp + 1], k_b.t[:, 0:S],
                                                               ALU.mult, ALU.mult), reads=[r_, k_b, pk], writes=[rk])
                    kkr, sq, rn, kk = X["kkr"], X["sq"], X["rn"], X["kk"]
                    op("pool", lambda e: e.tensor_scalar(kkr.t[:, 0:S], k_b.t[:, 0:S], pk.t[:, O_KK + hp:O_KK + hp + 1], None, ALU.mult),
                       reads=[k_b, pk], writes=[kkr])
                    yield
                    flush()
                    p1 = nps()
                    op("pe", lambda e: e.matmul(p1.t[:, 0:S], bones.t[:], rk.t[:, 0:S], start=True, stop=True), reads=[bones, rk], writes=[p1])
                    op("act", lambda e: e.activation(sq.t[:, 0:S], kkr.t[:, 0:S], AF.Square), reads=[kkr], writes=[sq])
                    yield
                    bon = X["bon"]
                    op("dve", lambda e: e.tensor_tensor(bon.t[:, 0:S], p1.t[:, 0:S], v_.t[:, 0:S], ALU.mult), reads=[p1, v_], writes=[bon])
                    st(BON[b, fs, g0:g0 + S], bon.t[:, 0:S], bon)
                    p2 = nps()
                    op("pe", lambda e: e.matmul(p2.t[:, 0:S], bones.t[:], sq.t[:, 0:S], start=True, stop=True), reads=[bones, sq], writes=[p2])
                    yield
                    flush()
                    op("act", lambda e: e.activation(rn.t[:, 0:S], p2.t[:, 0:S], AF.Sqrt), reads=[p2], writes=[rn])
                    yield
                    op("dve", lambda e: e.tensor_scalar(rn.t[:, 0:S], rn.t[:, 0:S], 1e-12, None, ALU.max), reads=[rn], writes=[rn])
                    yield
                    op("dve", lambda e: e.reciprocal(rn.t[:, 0:S], rn.t[:, 0:S]), reads=[rn], writes=[rn])
                    yield
                    op("pool", lambda e: e.tensor_tensor(kk.t[:, 0:S], kkr.t[:, 0:S], rn.t[:, 0:S], ALU.mult), reads=[kkr, rn], writes=[kk])

                    def transp_out(src, tmb, dst_ap):
                        p = nps()
                        nb_ = S // 128
                        for tb in range(nb_):
                            op("pe", lambda e: e.transpose(p.t[:, tb * 128:(tb + 1) * 128], src.t[:, tb * 128:(tb + 1) * 128], identf.t[:]),
                               reads=[src, identf], writes=[p])
                        yield
                        op("act", lambda e: e.copy(tmb.t[:, 0:nb_, :], p.t[:, 0:nb_ * 128].rearrange("p (a f) -> p a f", f=128)),
                           reads=[p], writes=[tmb])
                        st(dst_ap.rearrange("(tb p) f -> p tb f", p=128), tmb.t[:, 0:nb_, :], tmb)
                    yield from transp_out(v_, tms["v"][ks], VT[b, g0:g0 + S, fs])
                    for d in range(2):
                        sig, a_, t1, kd, bd = X["sig"], X["a"], X["t1"], X["kd"], X["bd"]
                        pa_ = nps()
                        op("pe", lambda e: e.matmul(pa_.t[:, 0:S], dup.t[0:64, d, fs], wa.t[0:64, 0:S], start=True, stop=True), reads=[dup, wa], writes=[pa_])
                        pb_ = nps()
                        op("pe", lambda e: e.matmul(pb_.t[:, 0:S], iup.t[64:128, d, fs], wa.t[64:128, 0:S], start=True, stop=True), reads=[iup, wa], writes=[pb_])
                        yield
                        flush()
                        op("act", lambda e: e.activation(sig.t[:, 0:S], pa_.t[:, 0:S], AF.Sigmoid, bias=pk.t[:, O_DB0 + d * 4 + hp:O_DB0 + d * 4 + hp + 1]),
                           reads=[pa_, pk], writes=[sig])
                        op("act", lambda e: e.activation(a_.t[:, 0:S], pb_.t[:, 0:S], AF.Sigmoid, bias=pk.t[:, O_IB0 + d * 4 + hp:O_IB0 + d * 4 + hp + 1]),
                           reads=[pb_, pk], writes=[a_])
                        yield
                        cs, ex, rem, cb = X["cs"], X["ex"], X["rem"], X["cb"]
                        op("dve", lambda e: e.tensor_tensor_scan(cs.t[:, 0:S], rmask.t[:, 0:S], sig.t[:, 0:S], 0.0, ALU.mult, ALU.add),
                           reads=[rmask, sig], writes=[cs])
                        op("pool", lambda e: e.tensor_tensor(bd.t[:, 0:S], kk.t[:, 0:S], a_.t[:, 0:S], ALU.mult), reads=[kk, a_], writes=[bd])
                        yield
                        op("act", lambda e: e.activation(t1.t[:, 0:S], a_.t[:, 0:S], AF.Identity, bias=omka.t[:, hp:hp + 1],
                                                         scale=pk.t[:, O_KA + hp:O_KA + hp + 1]), reads=[a_, pk, omka], writes=[t1])
                        op("pool", lambda e: e.tensor_tensor(ex.t[:, 0:S], cs.t[:, 0:S], sig.t[:, 0:S], ALU.subtract), reads=[cs, sig], writes=[ex])
                        csv = cs.t[:, 0:S].rearrange("p (c t) -> p c t", t=64)
                        pc_ = pcs[ks]
                        op("act", lambda e: e.activation(pc_.t[:, 0:NCS], csv[:, :, 63], AF.Exp, scale=LAM), reads=[cs], writes=[pc_])
                        st(PC[b, d, fs, g0 // 64:g0 // 64 + NCS], pc_.t[:, 0:NCS], pc_)
                        yield
                        flush()
                        op("dve", lambda e: e.tensor_tensor(rem.t[:, 0:S].rearrange("p (c t) -> p c t", t=64),
                                                            csv[:, :, 63:64].broadcast_to([128, NCS, 64]), csv, ALU.subtract),
                           reads=[cs], writes=[rem])
                        op("pool", lambda e: e.tensor_tensor(kd.t[:, 0:S], t1.t[:, 0:S], k_b.t[:, 0:S], ALU.mult), reads=[t1, k_b], writes=[kd])
                        yield
                        if d == 0:
                            incl, excl, aft = cs, ex, rem
                        else:
                            op("pool", lambda e: e.tensor_tensor(cb.t[:, 0:S], rem.t[:, 0:S], sig.t[:, 0:S], ALU.add), reads=[rem, sig], writes=[cb])
                            incl, excl, aft = cb, rem, ex
                            yield
                        E1, E2, E3, E4 = X["E1"], X["E2"], X["E3"], X["E4"]
                        op("act", lambda e: e.activation(E3.t[:, 0:S], incl.t[:, 0:S], AF.Exp, scale=-LAM), reads=[incl], writes=[E3])
                        op("act", lambda e: e.activation(E4.t[:, 0:S], aft.t[:, 0:S], AF.Exp, scale=LAM), reads=[aft], writes=[E4])
                        yield
                        op("act", lambda e: e.activation(E2.t[:, 0:S], excl.t[:, 0:S], AF.Exp, scale=LAM), reads=[excl], writes=[E2])
                        op("act", lambda e: e.activation(E1.t[:, 0:S], incl.t[:, 0:S], AF.Exp, scale=LAM), reads=[incl], writes=[E1])
                        kh, bh = X["kh"], X["bh"]
                        op("dve", lambda e: e.tensor_tensor(kh.t[:, 0:S], kd.t[:, 0:S], E4.t[:, 0:S], ALU.mult), reads=[kd, E4], writes=[kh])
                        op("pool", lambda e: e.tensor_tensor(bh.t[:, 0:S], bd.t[:, 0:S], E4.t[:, 0:S], ALU.mult), reads=[bd, E4], writes=[bh])
                        yield
                        outs = ((X["o0"], kd, E3), (X["o1"], bd, E3), (X["o2"], kk, E2), (X["o3"], r_, E1))
                        for qi, (o_, a1, a2) in enumerate(outs):
                            op(("dve", "pool")[qi % 2], lambda e: e.tensor_tensor(o_.t[:, 0:S], a1.t[:, 0:S], a2.t[:, 0:S], ALU.mult),
                               reads=[a1, a2], writes=[o_])
                            st(FM[b, d, qi, fs, g0:g0 + S], o_.t[:, 0:S], o_)
                            if qi == 1:
                                yield
                        yield from transp_out(kh, tms["kh"][ks], TM[b, d, 0, g0:g0 + S, fs])
                        flush()
                        yield from transp_out(bh, tms["bh"][ks], TM[b, d, 1, g0:g0 + S, fs])
                    yield
                    flush()

                tasks = []
                segno = 0
                for b in range(NB):
                    segs = [(0, TC, True, True)]
                    segs += [(TC + s, S3, s == 0, s + S3 == T) for s in range(0, T, S3)]
                    for (g0, S, first, lastseg) in segs:
                        for hp in range(4):
                            tasks.append((b, g0, S, first, lastseg, hp, segno % NS))
                        segno += 1
                live = []
                ti_ = 0
                while ti_ < len(tasks) or live:
                    while len(live) < NS and ti_ < len(tasks):
                        (b, g0, S, first, lastseg, hp, k0) = tasks[ti_]
                        wa, g1, g2 = B_["wa"][k0], B_["g1"][k0], B_["g2"][k0]
                        if hp == 0:
                            shift(wa, inb["wa"][k0], 12, S, g0, first, lastseg, b)
                            shift(g1, inb["g1"][k0], 13, S, g0, first, lastseg, b)
                            shift(g2, inb["g2"][k0], 14, S, g0, first, lastseg, b, rows=32)
                            op("act", lambda e: e.activation(wa.t[0:64, 0:S], wa.t[0:64, 0:S], AF.Tanh), reads=[wa], writes=[wa])
                            op("act", lambda e: e.activation(g1.t[:, 0:S], g1.t[:, 0:S], AF.Sigmoid), reads=[g1], writes=[g1])
                            op("act", lambda e: e.activation(g2.t[0:32, 0:S], g2.t[0:32, 0:S], AF.Sigmoid), reads=[g2], writes=[g2])
                        live.append(hp_task(b, g0, S, first, lastseg, hp, ti_ % NS, wa, g1, g2))
                        ti_ += 1
                    for g in list(live):
                        try:
                            next(g)
                        except StopIteration:
                            live.remove(g)

            with (c.phase() if upto >= 4 else contextlib.nullcontext(False)) as _ph4:
              if upto >= 4:
                pbk = [c.ps("pb%d" % i, [128, 512]) for i in range(8)]
                cn = {"pb": 0, "tp": 0, "e": 0, "ht": 0}

                def npb():
                    cn["pb"] += 1
                    return pbk[cn["pb"] % 8]
                tp = [c.sb("tp%d" % i, [64, 8, 64], F32R) for i in range(8 if INV_BF16 else 12)]

                def ntp():
                    cn["tp"] += 1
                    return tp[cn["tp"] % len(tp)]
                tpq = [c.sb("tpq%d" % i, [64, 8, 64], BF16) for i in range(12)] if INV_BF16 else tp
                cn["tpq"] = 0

                def ntq():
                    if not INV_BF16:
                        return ntp()
                    cn["tpq"] += 1
                    return tpq[cn["tpq"] % len(tpq)]
                htmp = [c.sb("htmp%d" % i, [128, 4, 128]) for i in range(2)]
                bmask = c.sb("bmask", [128, 2])
                op("pool", lambda e: e.memset(bmask.t[:], 0.0), writes=[bmask])
                op("pool", lambda e: e.memset(bmask.t[0:64, 0:1], 1.0), writes=[bmask])
                op("pool", lambda e: e.memset(bmask.t[64:128, 1:2], 1.0), writes=[bmask])

                def V8(p):
                    return p.t[0:64, :].rearrange("p (h t) -> p h t", t=64)

                def V4(p):
                    return p.t[0:64, :].rearrange("p (h q t) -> p h q t", q=2, t=64)

                def VH(p):
                    return p.t[:, :].rearrange("p (a f) -> p a f", f=128)

                def evac(dst_ap, src_ap, reads, writes, scale=None):
                    cn["e"] += 1
                    if cn["e"] % 2 and scale is None:
                        op("dve", lambda e: e.tensor_copy(dst_ap, src_ap), reads=reads, writes=writes)
                    else:
                        op("act", lambda e: e.activation(dst_ap, src_ap, AF.Copy, scale=(1.0 if scale is None else scale)), reads=reads, writes=writes)
                strs = []
                for b in range(NB):
                    for d in range(2):
                        cl = list(range(0, TC, 64))
                        ll = list(range(TC, TT, 64))
                        i_ = len(strs)
                        strs.append(dict(
                            b=b, d=d, chunks=(cl + ll) if d == 0 else (cl[::-1] + ll[::-1]),
                            fm32=c.sb("fm32_%d" % i_, [128, 4, 4, 64]), tm32=c.sb("tm32_%d" % i_, [64, 2, 512]), vt32=c.sb("vt32_%d" % i_, [64, 512]),
                            pc=[c.sb("pc%d_%d" % (i_, k), [128, 4, 1]) for k in range(2)],
                            tm=c.sb("tm_%d" % i_, [64, 2, 512], F32R), vt=c.sb("vt_%d" % i_, [64, 512], F32R),
                            yst=c.sb("yst_%d" % i_, [64, 8, 64]), H=c.sb("H_%d" % i_, [128, 4, 128]), Hb=c.sb("Hb_%d" % i_, [128, 4, 128], F32R),
                            bd=c.sb("bd_%d" % i_, [128, 4, 4, 2, 64], F32R), AK=c.sb("AK_%d" % i_, [64, 8, 2, 64], F32R),
                            AB=c.sb("AB_%d" % i_, [64, 8, 3, 64], F32R),
                            XPb=(c.sb("XPb_%d" % i_, [64, 8, 2, 64], BF16) if INV_BF16 else None)))

                def load(S_, n):
                    g0 = S_["chunks"][n]
                    b, d = S_["b"], S_["d"]
                    dma("sp", S_["fm32"].t[:], FM[b, d].rearrange("q (hp p) t -> p q hp t", p=128)[:, :, :, g0:g0 + 64],
                        writes=[S_["fm32"]], sembuf=S_["fm32"])
                    dma("sp", S_["tm32"].t[:], TM[b, d, :, g0:g0 + 64, :].rearrange("q p f -> p q f"), writes=[S_["tm32"]], sembuf=S_["tm32"])
                    dma("sp", S_["vt32"].t[:], VT[b, g0:g0 + 64, :], writes=[S_["vt32"]], sembuf=S_["vt32"])
                    pcb = S_["pc"][n % 2]
                    dma("sp", pcb.t[:], PC[b, d].rearrange("(hp p) c -> p hp c", p=128)[:, :, g0 // 64:g0 // 64 + 1], writes=[pcb], sembuf=pcb, slow=True)

                def store_y(S_, n):
                    g0 = S_["chunks"][n]
                    dma("sp", YT[S_["b"], S_["d"]].rearrange("(h v) t -> v h t", v=64)[:, :, g0:g0 + 64], S_["yst"].t[:],
                        reads=[S_["yst"]], sembuf=S_["yst"])

                def chunk_gen(S_, n):
                    d = S_["d"]
                    bd, AK, AB, tm, vt, H, Hb, yst = S_["bd"], S_["AK"], S_["AB"], S_["tm"], S_["vt"], S_["H"], S_["Hb"], S_["yst"]
                    pcb = S_["pc"][n % 2]
                    fm32, tm32, vt32 = S_["fm32"], S_["tm32"], S_["vt32"]
                    if n > 0:
                        store_y(S_, n - 1)
                    op("pool", lambda e: e.tensor_tensor(
                        bd.t[:].rearrange("p a b q t -> p (a b) q t"),
                        fm32.t[:].rearrange("p a b t -> p (a b) t").unsqueeze(2).broadcast_to([128, 16, 2, 64]),
                        bmask.t[:].unsqueeze(1).unsqueeze(3).broadcast_to([128, 16, 2, 64]), ALU.mult), reads=[fm32, bmask], writes=[bd])
                    op("act", lambda e: e.copy(tm.t[:], tm32.t[:]), reads=[tm32], writes=[tm])
                    op("dve", lambda e: e.tensor_copy(vt.t[:], vt32.t[:]), reads=[vt32], writes=[vt])
                    yield

                    def BD(qty, h):
                        return bd.t[:, qty, h // 2, h % 2, :]
                    if n + 1 < len(S_["chunks"]):
                        load(S_, n + 1)
                    for (dst, qty) in ((AK, 0), (AB, 1)):
                        dsl = slice(0, 2) if qty == 0 else slice(1, 3)
                        pa = [npb(), npb()]
                        for h in range(8):
                            op("pe", lambda e: e.matmul(V4(pa[h // 4])[:, h % 4, :, :], BD(qty, h), bd.t[:, 2:4, h // 2, h % 2, :], start=True, stop=True),
                               reads=[bd], writes=[pa[h // 4]])
                        for hh in range(2):
                            op("dve", lambda e: e.tensor_tensor(dst.t[:, hh * 4:hh * 4 + 4, dsl, :], V4(pa[hh]),
                                                                MK[d].t[:].unsqueeze(1).broadcast_to([64, 4, 2, 64]), ALU.mult),
                               reads=[pa[hh], MK[d]], writes=[dst])
                        yield
                    pn = npb()
                    for h in range(8):
                        op("pe", lambda e: e.matmul(V8(pn)[:, h, :], BD(2, h), BD(1, h), start=True, stop=True), reads=[bd], writes=[pn])
                    Q = ntq()
                    op("dve", lambda e: e.tensor_tensor(Q.t[:], V8(pn), MNT[d].t[:].unsqueeze(1).broadcast_to([64, 8, 64]), ALU.mult),
                       reads=[pn, MNT[d]], writes=[Q])
                    XT = S_["XPb"] if INV_BF16 else AB
                    op("pool", lambda e: e.tensor_tensor(XT.t[:, :, 0, :], identf.t[0:64, 0:64].unsqueeze(1).broadcast_to([64, 8, 64]), AB.t[:, :, 1, :], ALU.subtract),
                       reads=[identf, AB], writes=[XT])
                    if INV_BF16:
                        op("act", lambda e: e.copy(XT.t[:, :, 1, :], AB.t[:, :, 1, :]), reads=[AB], writes=[XT])
                    yield
                    pP = npb()
                    pQ = npb()
                    for h in range(8):
                        op("pe", lambda e: e.matmul(V8(pP)[:, h, :], Q.t[:, h, :], XT.t[:, h, 1, :], start=True, stop=True), reads=[Q, XT], writes=[pP])
                    for h in range(8):
                        op("pe", lambda e: e.matmul(V8(pQ)[:, h, :], XT.t[:, h, 1, :], Q.t[:, h, :], start=True, stop=True), reads=[Q, XT], writes=[pQ])
                    evac(XT.t[:, :, 1, :], V8(pP), [pP], [XT])
                    Qn = ntq()
                    evac(Qn.t[:], V8(pQ), [pQ], [Qn])
                    Q = Qn
                    yield
                    for r_ in range(1, 5):
                        pa = [npb(), npb()]
                        nq = 2 if r_ < 4 else 1
                        for h in range(8):
                            op("pe", lambda e: e.matmul(V4(pa[h // 4])[:, h % 4, 0:nq, :], Q.t[:, h, :], XT.t[:, h, 0:nq, :], start=True, stop=True),
                               reads=[Q, XT], writes=[pa[h // 4]])
                        pQ = npb()
                        for h in range(8):
                            op("pe", lambda e: e.matmul(V8(pQ)[:, h, :], XT.t[:, h, 1, :], Q.t[:, h, :], start=True, stop=True), reads=[Q, XT], writes=[pQ])
                        for hh in range(2):
                            op("dve", lambda e: e.tensor_tensor(XT.t[:, hh * 4:hh * 4 + 4, 0, :], V4(pa[hh])[:, :, 0, :], XT.t[:, hh * 4:hh * 4 + 4, 0, :], ALU.add),
                               reads=[pa[hh], XT], writes=[XT])
                            if r_ < 4:
                                evac(XT.t[:, hh * 4:hh * 4 + 4, 1, :], V4(pa[hh])[:, :, 1, :], [pa[hh]], [XT])
                        Qn = ntq()
                        evac(Qn.t[:], V8(pQ), [pQ], [Qn])
                        Q = Qn
                        yield
                    pX = npb()
                    for h in range(8):
                        op("pe", lambda e: e.matmul(V8(pX)[:, h, :], Q.t[:, h, :], XT.t[:, h, 0, :], start=True, stop=True), reads=[Q, XT], writes=[pX])
                    op("dve", lambda e: e.tensor_tensor(XT.t[:, :, 0, :], V8(pX), XT.t[:, :, 0, :], ALU.add), reads=[pX, XT], writes=[XT])
                    if INV_BF16:
                        op("act", lambda e: e.copy(AB.t[:, :, 0, :], XT.t[:, :, 0, :]), reads=[XT], writes=[AB])
                    yield
                    pW = npb()
                    for h in range(8):
                        hp, qq = h // 2, h % 2
                        op("pe", lambda e: e.matmul(V8(pW)[:, h, :], BD(2, h), Hb.t[:, hp, qq * 64:qq * 64 + 64], start=True, stop=False),
                           reads=[bd, Hb], writes=[pW])
                        op("pe", lambda e: e.matmul(V8(pW)[:, h, :], AK.t[:, h, 0, :], vt.t[:, h * 64:(h + 1) * 64], start=False, stop=True),
                           reads=[AK, vt], writes=[pW])
                    W1 = ntp()
                    evac(W1.t[:], V8(pW), [pW], [W1])
                    yield
                    pU = npb()
                    for h in range(8):
                        op("pe", lambda e: e.matmul(V8(pU)[:, h, :], AB.t[:, h, 0, :], W1.t[:, h, :], start=True, stop=True), reads=[AB, W1], writes=[pU])
                    Un = ntp()
                    evac(Un.t[:], V8(pU), [pU], [Un], scale=-1.0)
                    yield
                    pY = npb()
                    for h in range(8):
                        hp, qq = h // 2, h % 2
                        op("pe", lambda e: e.matmul(V8(pY)[:, h, :], Hb.t[:, hp, qq * 64:qq * 64 + 64], BD(3, h), start=True, stop=False),
                           reads=[bd, Hb], writes=[pY])
                        op("pe", lambda e: e.matmul(V8(pY)[:, h, :], vt.t[:, h * 64:(h + 1) * 64], AK.t[:, h, 1, :], start=False, stop=False),
                           reads=[AK, vt], writes=[pY])
                        op("pe", lambda e: e.matmul(V8(pY)[:, h, :], Un.t[:, h, :], AB.t[:, h, 2, :], start=False, stop=True),
                           reads=[AB, Un], writes=[pY])
                    evac(yst.t[:], V8(pY), [pY], [yst])
                    pH = npb()
                    Unf = Un.t[:].rearrange("p h t -> p (h t)")
                    for hp in range(4):
                        fs = slice(hp * 128, (hp + 1) * 128)
                        op("pe", lambda e: e.matmul(VH(pH)[:, hp, :], tm.t[:, 0, fs], vt.t[:, fs], start=True, stop=False), reads=[tm, vt], writes=[pH])
                        op("pe", lambda e: e.matmul(VH(pH)[:, hp, :], tm.t[:, 1, fs], Unf[:, fs], start=False, stop=True), reads=[tm, Un], writes=[pH])
                    cn["ht"] += 1
                    ht = htmp[cn["ht"] % 2]
                    op("pool", lambda e: e.tensor_tensor(ht.t[:], H.t[:], pcb.t[:].broadcast_to([128, 4, 128]), ALU.mult), reads=[H, pcb], writes=[ht])
                    op("dve", lambda e: e.tensor_tensor(H.t[:], ht.t[:], VH(pH), ALU.add), reads=[ht, pH], writes=[H])
                    op("act", lambda e: e.copy(Hb.t[:], H.t[:]), reads=[H], writes=[Hb])
                    yield

                for S_ in strs:
                    op("pool", lambda e: e.memset(S_["H"].t[:], 0.0), writes=[S_["H"]])
                    op("act", lambda e: e.copy(S_["Hb"].t[:], S_["H"].t[:]), reads=[S_["H"]], writes=[S_["Hb"]])
                    load(S_, 0)
                nchunks = len(strs[0]["chunks"])
                for n in range(nchunks):
                    gens = [chunk_gen(S_, n) for S_ in strs]
                    while gens:
                        for g in list(gens):
                            try:
                                next(g)
                            except StopIteration:
                                gens.remove(g)
                for S_ in strs:
                    store_y(S_, nchunks - 1)

            ffn_st = contextlib.ExitStack()
            _prev = c.stack
            c.stack = ffn_st
            wgb = c.sb("wgb", [128, 8, DFF], BF16)
            wub = c.sb("wub", [128, 8, DFF], BF16)
            wdb = c.sb("wdb", [128, NFC, D], BF16)
            c.stack = _prev

            def wload_gen(wst4):
                steps = []
                for (dst, src, nk, ncols) in ((wgb, wg_in[l], 8, DFF), (wub, wu_in[l], 8, DFF), (wdb, wd_in[l], NFC, D)):
                    for kc in range(nk):
                        for c0 in range(0, ncols, 704):
                            steps.append((dst, src, kc, c0, min(704, ncols - c0)))
                pend = []

                def cast(i, dst, kc, c0, w, s_):
                    if i % 2:
                        op("act", lambda e: e.copy(dst.t[:, kc, c0:c0 + w], s_.t[:, 0:w]), reads=[s_], writes=[dst])
                    else:
                        op("pool", lambda e: e.tensor_copy(dst.t[:, kc, c0:c0 + w], s_.t[:, 0:w]), reads=[s_], writes=[dst])
                for i, (dst, src, kc, c0, w) in enumerate(steps):
                    s_ = wst4[i % len(wst4)]
                    dma("sp", s_.t[:, 0:w], src[kc * 128:(kc + 1) * 128, c0:c0 + w], writes=[s_], sembuf=s_)
                    pend.append((i, dst, kc, c0, w, s_))
                    if len(pend) > 2:
                        cast(*pend.pop(0))
                    yield
                while pend:
                    cast(*pend.pop(0))
                    yield

            with (c.phase() if upto >= 5 else contextlib.nullcontext(False)) as _ph5:
              if upto >= 5:
                S5 = 512
                NS = 2

                def mk5(name, dt=F32):
                    return [c.sb("%s%d" % (name, i), [128, S5], dt) for i in range(NS)]
                yf, yb, bo, gg, sq5, mt5, ms5, vr5, ob = mk5("yf"), mk5("yb"), mk5("bo"), mk5("gg"), mk5("sq5"), mk5("mt5"), mk5("ms5"), mk5("vr5"), mk5("ob", BF16)
                p5 = [c.ps("p5_%d" % i, [128, 512]) for i in range(4)]

                def p5_task(b, g0, S, hp, k, it):
                    fs = slice(hp * 128, (hp + 1) * 128)
                    dma("sp", yf[k].t[:, 0:S], YT[b, 0, fs, g0:g0 + S], writes=[yf[k]], sembuf=yf[k])
                    dma("sp", yb[k].t[:, 0:S], YT[b, 1, fs, g0:g0 + S], writes=[yb[k]], sembuf=yb[k])
                    dma("sp", bo[k].t[:, 0:S], BON[b, fs, g0:g0 + S], writes=[bo[k]], sembuf=bo[k])
                    dma("sp", gg[k].t[:, 0:S], GG[b, fs, g0:g0 + S], writes=[gg[k]], sembuf=gg[k])
                    yield
                    y = yf[k]
                    op("dve", lambda e: e.tensor_tensor(y.t[:, 0:S], y.t[:, 0:S], yb[k].t[:, 0:S], ALU.add), reads=[y, yb[k]], writes=[y])
                    yield
                    op("act", lambda e: e.activation(sq5[k].t[:, 0:S], y.t[:, 0:S], AF.Square), reads=[y], writes=[sq5[k]])
                    pA, pB = p5[(it * 2) % 4], p5[(it * 2 + 1) % 4]
                    op("pe", lambda e: e.matmul(pA.t[:, 0:S], bones.t[:], y.t[:, 0:S], start=True, stop=True), reads=[bones, y], writes=[pA])
                    yield
                    op("pe", lambda e: e.matmul(pB.t[:, 0:S], bones.t[:], sq5[k].t[:, 0:S], start=True, stop=True), reads=[bones, sq5[k]], writes=[pB])
                    m, ms, vr = mt5[k], ms5[k], vr5[k]
                    op("act", lambda e: e.activation(m.t[:, 0:S], pA.t[:, 0:S], AF.Copy, scale=1.0 / 64.0), reads=[pA], writes=[m])
                    yield
                    op("dve", lambda e: e.tensor_tensor(y.t[:, 0:S], y.t[:, 0:S], m.t[:, 0:S], ALU.subtract), reads=[y, m], writes=[y])
                    op("pool", lambda e: e.tensor_tensor(ms.t[:, 0:S], m.t[:, 0:S], m.t[:, 0:S], ALU.mult), reads=[m], writes=[ms])
                    yield
                    op("dve", lambda e: e.scalar_tensor_tensor(vr.t[:, 0:S], pB.t[:, 0:S], 1.0 / 64.0, ms.t[:, 0:S], ALU.mult, ALU.subtract),
                       reads=[pB, ms], writes=[vr])
                    yield
                    op("act", lambda e: e.activation(vr.t[:, 0:S], vr.t[:, 0:S], AF.Sqrt, bias=64e-5, scale=1.0), reads=[vr], writes=[vr])
                    yield
                    op("dve", lambda e: e.reciprocal(vr.t[:, 0:S], vr.t[:, 0:S]), reads=[vr], writes=[vr])
                    yield
                    op("pool", lambda e: e.tensor_tensor(y.t[:, 0:S], y.t[:, 0:S], vr.t[:, 0:S], ALU.mult), reads=[y, vr], writes=[y])
                    yield
                    op("act", lambda e: e.activation(y.t[:, 0:S], y.t[:, 0:S], AF.Identity, bias=pk.t[:, O_GB + hp:O_GB + hp + 1],
                                                     scale=pk.t[:, O_GG + hp:O_GG + hp + 1]), reads=[y, pk], writes=[y])
                    yield
                    op("pool", lambda e: e.tensor_tensor(y.t[:, 0:S], y.t[:, 0:S], bo[k].t[:, 0:S], ALU.add), reads=[y, bo[k]], writes=[y])
                    yield
                    op("dve", lambda e: e.tensor_tensor(ob[k].t[:, 0:S], y.t[:, 0:S], gg[k].t[:, 0:S], ALU.mult), reads=[y, gg[k]], writes=[ob[k]])
                    yield
                    dma("sp", MIXT[b, 512 + hp * 128:512 + (hp + 1) * 128, g0:g0 + S], ob[k].t[:, 0:S], reads=[ob[k]], sembuf=ob[k])

                wst4 = [c.sb("wst4_%d" % i, [128, 704]) for i in range(4)]
                wl = wload_gen(wst4)
                tasks5 = []
                for b in range(NB):
                    for (g0, S) in [(0, TC)] + [(TC + s, S5) for s in range(0, T, S5)]:
                        for hp in range(4):
                            tasks5.append((b, g0, S, hp))
                live = []
                ti_ = 0
                while ti_ < len(tasks5) or live:
                    while len(live) < NS and ti_ < len(tasks5):
                        (b, g0, S, hp) = tasks5[ti_]
                        live.append(p5_task(b, g0, S, hp, ti_ % NS, ti_))
                        ti_ += 1
                    for g in list(live):
                        try:
                            next(g)
                        except StopIteration:
                            live.remove(g)
                    if wl is not None:
                        try:
                            next(wl)
                        except StopIteration:
                            wl = None
                if wl is not None:
                    for _ in wl:
                        pass

            with (c.phase() if upto >= 6 else contextlib.nullcontext(False)) as _ph6:
              if upto >= 6:
                wob = c.sb("wob", [128, 8, D], BF16)
                wst = [c.sb("wst%d" % i, [128, 512]) for i in range(2)]
                load_w_bf16(wob, wout_in[l], 8, D, wst, 512)
                po = [c.ps("po%d" % i, [128, 512]) for i in range(4)]
                GT = make_gt(0, po)
                mxs = [c.sb("mx%d" % i, [128, 8, 256], BF16) for i in range(2)]
                xts = [c.sb("xa%d" % i, [128, D]) for i in range(2)]
                tmp = [c.sb("tmpa%d" % i, [128, D]) for i in range(2)]
                nblk = 0
                nt_ = 0
                for b in range(NB):
                    seqs = [] if last else [(ctx_in if l == 0 else CXs, CXs, TC, 0, NB, False)]
                    seqs.append((x_in if l == 0 else Xs, Xs, T, TC, b, True))
                    for (src, dst, Ts, off, r, latent) in seqs:
                        for s0 in range(0, Ts, 256):
                            S = min(256, Ts - s0)
                            mx = mxs[nblk % 2]
                            nblk += 1
                            dma(q2(), mx.t[:, :, 0:S], MIXT[b].rearrange("(kc p) t -> p kc t", p=128)[:, :, off + s0:off + s0 + S],
                                writes=[mx], sembuf=mx)
                            for ti in range(S // 128):
                                xt = xts[nt_ % 2]
                                tm_ = tmp[nt_ % 2]
                                nt_ += 1
                                rows = x_rows(src[b], l, s0 // 128 + ti, latent)
                                for (p0, npart, ap) in rows:
                                    dma(q2(), xt.t[p0:p0 + npart, :], ap, writes=[xt], sembuf=xt)
                                for n2 in range(2):
                                    p = po[(nt_ * 2 + n2) % 4]
                                    for kc in range(8):
                                        op("pe", lambda e: e.matmul(p.t[:], mx.t[:, kc, ti * 128:(ti + 1) * 128], wob.t[:, kc, n2 * 512:(n2 + 1) * 512],
                                                                    start=(kc == 0), stop=(kc == 7)), reads=[mx, wob], writes=[p])
                                    op("dve", lambda e: e.tensor_tensor(tm_.t[:, n2 * 512:(n2 + 1) * 512], p.t[:], GT[r].t[:, n2 * 512:(n2 + 1) * 512], ALU.mult),
                                       reads=[p, GT[r]], writes=[tm_])
                                op("pool", lambda e: e.tensor_tensor(xt.t[:], xt.t[:], tm_.t[:], ALU.add), reads=[xt, tm_], writes=[xt])
                                for (p0, npart, ap) in x_rows(dst[b], l, s0 // 128 + ti, latent):
                                    dma(q2(), ap, xt.t[p0:p0 + npart, :], reads=[xt], sembuf=xt)

            with (c.phase() if upto >= 7 else contextlib.nullcontext(False)) as _ph7:
              if upto >= 7:
                pd = [c.ps("pd%d" % i, [128, 512]) for i in range(3)]
                GT = make_gt(1, pd)
                NBK = 256
                fgb = None
                if last:
                    fgb = c.sb("fgb", [128, D])
                    dma("sp", fgb.t[:], fg_in, writes=[fgb], sembuf=fgb)
                xts = [c.sb("xb%d" % i, [128, D]) for i in range(4)]
                xns = [c.sb("xnb%d" % i, [128, D], BF16) for i in range(2)]
                sss = [c.sb("ssb%d" % i, [128, 1]) for i in range(2)]
                rss = [c.sb("rsb%d" % i, [128, 1]) for i in range(2)]
                hTs = [c.sb("h2T%d" % i, [128, 8, NBK], BF16) for i in range(2)]
                actT = [c.sb("actT%d" % i, [128, NFC, NBK], BF16) for i in range(1)]
                sgs = [c.sb("sg%d" % i, [128, NBK]) for i in range(1)]
                ftmps = [c.sb("ftmp%d" % i, [128, 512]) for i in range(1)]
                pTs = [c.ps("pTb%d" % i, [128, D], BF16) for i in range(1)]
                pg = [c.ps("pg%d" % i, [128, 512]) for i in range(4)]
                nf = 0
                blks = []
                for b in range(NB):
                    seqs = [] if last else [(CXs, TC, NB, False)]
                    seqs.append((Xs, T, b, True))
                    for (src, Ts, r, latent) in seqs:
                        for s0 in range(0, Ts, NBK):
                            blks.append((b, src, r, latent, s0, min(NBK, Ts - s0)))

                def prep6(bi):
                    (b, src, r, latent, s0, S) = blks[bi]
                    hT = hTs[bi % 2]
                    xl = []
                    for ti in range(S // 128):
                        xt = xts[(bi * 2 + ti) % 4]
                        k = ti % 2
                        xl.append(xt)
                        for (p0, npart, ap) in x_rows(src[b], l, s0 // 128 + ti, latent):
                            dma("sp", xt.t[p0:p0 + npart, :], ap, writes=[xt], sembuf=xt)
                        norm_transpose(xt, r, G2, SH2, hT, ti * 128, sss[k], rss[k], xns[k], pTs[0])
                    return xl
                xl_next = prep6(0)
                if True:
                    if True:
                        for bi in range(len(blks)):
                            (b, src, r, latent, s0, S) = blks[bi]
                            hT = hTs[bi % 2]
                            aT = actT[0]
                            xl = xl_next
                            for fc in range(NFC):
                                nf += 1
                                p_g, p_u = pg[(nf * 2) % 4], pg[(nf * 2 + 1) % 4]
                                for kc in range(8):
                                    op("pe", lambda e: e.matmul(p_g.t[:, 0:S], wgb.t[:, kc, fc * 128:(fc + 1) * 128], hT.t[:, kc, 0:S],
                                                                start=(kc == 0), stop=(kc == 7)), reads=[wgb, hT], writes=[p_g])
                                for kc in range(8):
                                    op("pe", lambda e: e.matmul(p_u.t[:, 0:S], wub.t[:, kc, fc * 128:(fc + 1) * 128], hT.t[:, kc, 0:S],
                                                                start=(kc == 0), stop=(kc == 7)), reads=[wub, hT], writes=[p_u])
                                sg = sgs[0]
                                op("act", lambda e: e.activation(sg.t[:, 0:S], p_g.t[:, 0:S], AF.Silu), reads=[p_g], writes=[sg])
                                op("dve", lambda e: e.tensor_tensor(aT.t[:, fc, 0:S], sg.t[:, 0:S], p_u.t[:, 0:S], ALU.mult), reads=[sg, p_u], writes=[aT])
                                if fc == 12 and bi + 1 < len(blks):
                                    xl_next = prep6(bi + 1)
                            for ti in range(S // 128):
                                xt = xl[ti]
                                for n2 in range(2):
                                    nf += 1
                                    p = pd[nf % 3]
                                    for fc in range(NFC):
                                        op("pe", lambda e: e.matmul(p.t[:], aT.t[:, fc, ti * 128:(ti + 1) * 128], wdb.t[:, fc, n2 * 512:(n2 + 1) * 512],
                                                                    start=(fc == 0), stop=(fc == NFC - 1)), reads=[aT, wdb], writes=[p])
                                    sl = slice(n2 * 512, (n2 + 1) * 512)
                                    fx = ftmps[0]
                                    op("dve", lambda e: e.tensor_tensor(fx.t[:], p.t[:], GT[r].t[:, sl], ALU.mult), reads=[p, GT[r]], writes=[fx])
                                    op("pool", lambda e: e.tensor_tensor(xt.t[:, sl], xt.t[:, sl], fx.t[:], ALU.add), reads=[xt, fx], writes=[xt])
                                if last:
                                    k = ti % 2
                                    op("pool", lambda e: e.memset(sss[k].t[:], 0.0), writes=[sss[k]])
                                    op("act", lambda e: e.activation(xns[k].t[:], xt.t[:], AF.Square, scale=1.0 / 32.0, accum_out=sss[k].t[:, 0:1]),
                                       reads=[xt, sss[k]], writes=[xns[k], sss[k]])
                                    rstd_of(sss[k], rss[k])
                                    op("dve", lambda e: e.scalar_tensor_tensor(xt.t[:], xt.t[:], rss[k].t[:, 0:1], fgb.t[:], ALU.mult, ALU.mult),
                                       reads=[xt, rss[k], fgb], writes=[xt])
                                    for (p0, npart, ap) in x_rows(out[b], l, s0 // 128 + ti, True):
                                        dma("sp", ap, xt.t[p0:p0 + npart, :], reads=[xt], sembuf=xt)
                                else:
                                    for (p0, npart, ap) in x_rows(src[b], l, s0 // 128 + ti, latent):
                                        dma("sp", ap, xt.t[p0:p0 + npart, :], reads=[xt], sembuf=xt)
            ffn_st.close()
        c.barrier()
        print("instructions:", c.ninst, "dma sems:", len(c.dpool))
    return nc


def _pack(inp, L):
    pk = np.zeros((L, 128, NPK), np.float32)

    def fm(v, n):
        return np.asarray(v, np.float32).reshape(L, n, 128).transpose(0, 2, 1)
    pk[:, :, O_ADAB:O_ADAB + 48] = fm(inp["ada_b"], 48)
    pk[:, :, O_N1G:O_N1G + 8] = fm(inp["norm1_g"], 8)
    pk[:, :, O_N2G:O_N2G + 8] = fm(inp["norm2_g"], 8)
    pk[:, :, O_CW:O_CW + 124] = np.asarray(inp["conv_w"], np.float32).reshape(L, CK, 4, 128).transpose(0, 3, 2, 1).reshape(L, 128, 124)
    pk[:, :, O_CB:O_CB + 4] = fm(inp["conv_b"], 4)
    pk[:, :, O_CG:O_CG + 4] = fm(inp["cnorm_g"], 4)
    pk[:, :, O_CBB:O_CBB + 4] = fm(inp["cnorm_b"], 4)
    sw = np.zeros((L, 3, 1920), np.float32)
    sw[:, :, :1824] = np.asarray(inp["shift_w"], np.float32)
    pk[:, :, O_SW:O_SW + 45] = sw.reshape(L, 3, 15, 128).transpose(0, 3, 2, 1).reshape(L, 128, 45)
    pk[:, :, O_DB0:O_DB0 + 8] = np.asarray(inp["decay_b0"], np.float32).reshape(L, 2, 4, 128).transpose(0, 3, 1, 2).reshape(L, 128, 8)
    pk[:, :, O_IB0:O_IB0 + 8] = np.asarray(inp["iclr_b0"], np.float32).reshape(L, 2, 4, 128).transpose(0, 3, 1, 2).reshape(L, 128, 8)
    pk[:, :, O_KK:O_KK + 4] = fm(inp["k_k"], 4)
    pk[:, :, O_KA:O_KA + 4] = fm(inp["k_a"], 4)
    pk[:, :, O_RK:O_RK + 4] = fm(inp["r_k"], 4)
    pk[:, :, O_GG:O_GG + 4] = fm(inp["gn_g"], 4)
    pk[:, :, O_GB:O_GB + 4] = fm(inp["gn_b"], 4)
    return pk


def make_in_maps(inp, NB, ncores):
    L = inp["ada_w"].shape[0]
    f = lambda a: np.ascontiguousarray(np.asarray(a, np.float32))
    pk = _pack(inp, L)
    shared = {"pk": pk, "ada_b": f(inp["ada_b"]), "ada_w": f(inp["ada_w"]), "w_in": f(inp["w_in"]),
              "decay_up": f(inp["decay_up"]), "iclr_up": f(inp["iclr_up"]), "g_up": f(inp["g_up"]),
              "w_out": f(inp["w_out"]), "ffn_wg": f(inp["ffn_wg"]), "ffn_wu": f(inp["ffn_wu"]), "ffn_wd": f(inp["ffn_wd"]),
              "fg": np.ascontiguousarray(np.broadcast_to(f(inp["final_g"])[None, :], (128, D)))}
    x = f(inp["x"]); ctx = f(inp["ctx"]); cc = f(inp["c"]); c_ctx = f(inp["c_ctx"])
    maps = []
    for i in range(ncores):
        rows = np.concatenate([cc[i * NB:(i + 1) * NB], c_ctx[None, :]], axis=0)
        cT = np.ascontiguousarray(rows.reshape(NB + 1, 8, 128).transpose(2, 1, 0))
        m = dict(shared)
        m.update({"x": np.ascontiguousarray(x[i * NB:(i + 1) * NB]), "ctx": np.ascontiguousarray(ctx[i * NB:(i + 1) * NB]), "cT": cT})
        maps.append(m)
    return maps


def kernel(**inputs):
    NB, NCORES = 2, 8
    B, T, _ = inputs["x"].shape
    TC = inputs["ctx"].shape[1]
    L = inputs["ada_w"].shape[0]
    nc = build(NB, T, TC, L)
    maps = make_in_maps(inputs, NB, NCORES)
    res = run_bass_kernel_spmd(nc, maps, core_ids=list(range(NCORES)))
    return np.concatenate([np.asarray(r["out"], np.float32) for r in res.results], axis=0)
```

```python
import contextlib
import os
import numpy as np
import concourse.bass as bass
import concourse.mybir as mybir
from concourse.bass_utils import run_bass_kernel_spmd

F32 = mybir.dt.float32
BF16 = mybir.dt.bfloat16
F32R = mybir.dt.float32r
AF = mybir.ActivationFunctionType
ALU = mybir.AluOpType

D = 1024
INC = 2848
DFF = 2816
NFC = 22
CK = 31
LAM = -0.606531
INV_BF16 = False
NPK = 281
O_ADAB, O_N1G, O_N2G, O_CW, O_CB, O_CG, O_CBB, O_SW, O_DB0, O_IB0, O_KK, O_KA, O_RK, O_GG, O_GB = (
    0, 48, 56, 64, 188, 192, 196, 200, 245, 253, 261, 265, 269, 273, 277)


class Buf:
    __slots__ = ("name", "t", "lw", "rd", "semi")

    def __init__(self, name, t=None):
        self.name = name
        self.t = t
        self.lw = {}
        self.rd = {}
        self.semi = None


class Ctx:
    def __init__(self, nc, gstack):
        self.nc = nc
        self.gstack = gstack
        self.stack = gstack
        self.engs = {"pe": nc.tensor, "dve": nc.vector, "act": nc.scalar, "pool": nc.gpsimd, "sp": nc.sync}
        self.esem = {}
        self.ecnt = {}
        self.known = {k: {} for k in self.engs}
        for k in self.engs:
            self.esem[k] = gstack.enter_context(nc.semaphore("e_" + k))
            self.ecnt[k] = 0
        self.dpool = []
        self.dnext = 0
        self.ninst = 0
        self.uid = 0

    def sb(self, name, shape, dt=F32):
        self.uid += 1
        return Buf(name, self.stack.enter_context(self.nc.sbuf_tensor("%s_%d" % (name, self.uid), list(shape), dt)))

    def ps(self, name, shape, dt=F32):
        self.uid += 1
        return Buf(name, self.stack.enter_context(self.nc.psum_tensor("%s_%d" % (name, self.uid), list(shape), dt)))

    def _need(self, eng, toks):
        kn = self.known[eng]
        best = {}
        for sem, val in toks:
            k = id(sem)
            if kn.get(k, 0) >= val:
                continue
            if k not in best or best[k][1] < val:
                best[k] = (sem, val)
        for sem, val in best.values():
            self.engs[eng].wait_ge(sem, val)
            kn[id(sem)] = val

    def _deps(self, eng, reads, writes):
        toks = []
        own = self.esem[eng]
        for b in reads:
            for t in b.lw.values():
                if t[0] is own and eng == "pe":
                    continue
                toks.append(t)
        for b in writes:
            for t in b.lw.values():
                if t[0] is not own:
                    toks.append(t)
            for t in b.rd.values():
                if t[0] is not own:
                    toks.append(t)
        self._need(eng, toks)

    def op(self, eng, fn, reads=(), writes=()):
        self._deps(eng, reads, writes)
        ins = fn(self.engs[eng])
        self.ecnt[eng] += 1
        tok = (self.esem[eng], self.ecnt[eng])
        ins.then_inc(self.esem[eng], 1)
        k = id(tok[0])
        for b in writes:
            b.lw[k] = tok
        for b in reads:
            b.rd[k] = tok
        self.ninst += 1
        return ins

    def dma(self, q, out_ap, in_ap, reads=(), writes=(), sembuf=None, slow=False):
        self._deps(q, reads, writes)
        sbf = sembuf
        if sbf.semi is None:
            if self.dnext >= len(self.dpool):
                self.dpool.append([self.gstack.enter_context(self.nc.semaphore("d_%d" % len(self.dpool))), 0])
            sbf.semi = self.dnext
            self.dnext += 1
        ent = self.dpool[sbf.semi]
        ins = self.engs[q].dma_start(out=out_ap, in_=in_ap, allow_slow_non_contiguous=True) if slow else self.engs[q].dma_start(out=out_ap, in_=in_ap)
        ent[1] += 16
        ins.then_inc(ent[0], 16)
        tok = (ent[0], ent[1])
        k = id(tok[0])
        for b in writes:
            b.lw[k] = tok
        for b in reads:
            b.rd[k] = tok
        self.ninst += 1
        return ins

    def presem(self, sbf):
        self.dpool.append([self.gstack.enter_context(self.nc.semaphore("d_%d" % len(self.dpool))), 0])
        sbf.semi = len(self.dpool) - 1
        self.dnext = len(self.dpool)

    def barrier(self):
        toks = [(self.esem[k], self.ecnt[k]) for k in self.engs if self.ecnt[k] > 0]
        toks += [(e[0], e[1]) for e in self.dpool if e[1] > 0]
        for k in self.engs:
            self._need(k, toks)

    @contextlib.contextmanager
    def phase(self):
        old = self.stack
        with contextlib.ExitStack() as st:
            self.stack = st
            self.dnext_save = self.dnext
            yield
            self.barrier()
        self.stack = old
        self.dnext = self.dnext_save


def build(NB, T, TC, L, dbg=False, upto=99):
    R = NB + 1
    TT = TC + T
    ROWS = T // 64
    NCHT = TT // 64
    nc = bass.Bass("TRN2", target_bir_lowering=False)
    KI = "ExternalInput"
    SK = "ExternalOutput" if dbg else "Internal"

    def dr(name, shape, dt=F32, kind=KI):
        return nc.dram_tensor(name, list(shape), dt, kind=kind).ap()

    x_in = dr("x", [NB, T, D])
    ctx_in = dr("ctx", [NB, TC, D])
    cT_in = dr("cT", [128, 8, R])
    pk_in = dr("pk", [L, 128, NPK])
    adab_in = dr("ada_b", [L, 6144])
    adaw_in = dr("ada_w", [L, D, 6144])
    win_in = dr("w_in", [L, D, INC])
    dup_in = dr("decay_up", [L, 2, 64, 512])
    iup_in = dr("iclr_up", [L, 2, 64, 512])
    gup_in = dr("g_up", [L, 160, 512])
    wout_in = dr("w_out", [L, D, D])
    wg_in = dr("ffn_wg", [L, D, DFF])
    wu_in = dr("ffn_wu", [L, D, DFF])
    wd_in = dr("ffn_wd", [L, DFF, D])
    fg_in = dr("fg", [128, D])
    out = dr("out", [NB, T, D], kind="ExternalOutput")
    Xs = dr("Xs", [NB, T, D], kind=SK)
    CXs = dr("CXs", [NB, TC, D], kind=SK)
    UT = dr("UT", [NB, INC, TT], kind=SK)
    MIXT = dr("MIXT", [NB, D, TT], BF16, kind=SK)
    FM = dr("FM", [NB, 2, 4, 512, TT], kind=SK)
    TM = dr("TM", [NB, 2, 2, TT, 512], kind=SK)
    VT = dr("VT", [NB, TT, 512], kind=SK)
    PC = dr("PC", [NB, 2, 512, NCHT], kind=SK)
    BON = dr("BON", [NB, 512, TT], kind=SK)
    GG = dr("GG", [NB, 512, TT], kind=SK)
    YT = dr("YT", [NB, 2, 512, TT], kind=SK)

    gst = contextlib.ExitStack()
    with gst:
        c = Ctx(nc, gst)
        op, dma = c.op, c.dma
        rr = [0]

        def q2():
            rr[0] += 1
            return ("sp", "pool")[rr[0] % 2]

        identf = c.sb("identf", [128, 128])
        identb = c.sb("identb", [128, 128], BF16)
        bones = c.sb("bones", [128, 128])
        MK = [c.sb("mk%d" % d, [64, 2, 64]) for d in range(2)]
        MNT = [c.sb("mnt%d" % d, [64, 64]) for d in range(2)]
        rmask = c.sb("rmask", [128, 512])
        ones1 = c.sb("ones1", [1, 128])
        sel = c.sb("sel", [R, R, 128])
        silc = c.sb("silc", [128, 8, R])
        G1 = c.sb("G1", [128, 8, R]); SH1 = c.sb("SH1", [128, 8, R])
        G2 = c.sb("G2", [128, 8, R]); SH2 = c.sb("SH2", [128, 8, R])
        gtrow = c.sb("gtrow", [R, 2, D])
        pk = c.sb("pk", [128, NPK])
        omka = c.sb("omka", [128, 4])

        def asel(buf, ap, pattern, base, cm):
            op("pool", lambda e: e.affine_select(ap, ap, pattern=pattern, compare_op=ALU.is_ge, fill=0.0,
                                                 base=base, channel_multiplier=cm), reads=[buf], writes=[buf])

        tmpi = c.sb("tmpi", [128, 128])
        op("pool", lambda e: e.memset(identf.t[:], 1.0), writes=[identf])
        asel(identf, identf.t[:], [[1, 128]], 0, -1)
        asel(identf, identf.t[:], [[-1, 128]], 0, 1)
        op("dve", lambda e: e.tensor_copy(identb.t[:], identf.t[:]), reads=[identf], writes=[identb])
        op("pool", lambda e: e.memset(bones.t[:], 0.0), writes=[bones])
        op("pool", lambda e: e.memset(bones.t[0:64, 0:64], 1.0), writes=[bones])
        op("pool", lambda e: e.memset(bones.t[64:128, 64:128], 1.0), writes=[bones])
        for d in range(2):
            op("pool", lambda e: e.memset(MK[d].t[:], 1.0), writes=[MK[d]])
            op("pool", lambda e: e.memset(MNT[d].t[:], 1.0), writes=[MNT[d]])
        asel(MK[0], MK[0].t[:, 0, :], [[1, 64]], -1, -1)
        asel(MK[0], MK[0].t[:, 1, :], [[1, 64]], 0, -1)
        asel(MK[1], MK[1].t[:, 0, :], [[-1, 64]], -1, 1)
        asel(MK[1], MK[1].t[:, 1, :], [[-1, 64]], 0, 1)
        asel(MNT[0], MNT[0].t[:], [[-1, 64]], -1, 1)
        asel(MNT[1], MNT[1].t[:], [[1, 64]], -1, -1)
        op("pool", lambda e: e.memset(rmask.t[:], 1.0), writes=[rmask])
        op("pool", lambda e: e.memset(rmask.t[:].rearrange("p (c t) -> p c t", t=64)[:, :, 0:1], 0.0), writes=[rmask])
        op("pool", lambda e: e.memset(ones1.t[:], 1.0), writes=[ones1])
        op("dve", lambda e: e.tensor_copy(sel.t[:], identf.t[0:R, 0:R].unsqueeze(2).broadcast_to([R, R, 128])),
           reads=[identf], writes=[sel])
        c.presem(pk)
        dma("sp", silc.t[:], cT_in, writes=[silc], sembuf=silc)
        op("act", lambda e: e.activation(silc.t[:], silc.t[:], AF.Silu), reads=[silc], writes=[silc])

        def x_rows(src_b, l, i, latent):
            if (not latent) or l % 2 == 0:
                return [(0, 128, src_b[i * 128:(i + 1) * 128, :])]
            ncol = 128 // ROWS
            v = src_b.rearrange("(r c) d -> c r d", c=64)
            return [(cl * ROWS, ROWS, v[i * ncol + cl]) for cl in range(ncol)]

        def rstd_of(ss, rs):
            op("act", lambda e: e.activation(rs.t[:], ss.t[:], AF.Sqrt, bias=1e-6, scale=1.0), reads=[ss], writes=[rs])
            op("dve", lambda e: e.reciprocal(rs.t[:], rs.t[:]), reads=[rs], writes=[rs])

        for l in range(L):
            last = l == L - 1
            with (c.phase() if upto >= 0 else contextlib.nullcontext(False)) as _ph0:
              if upto >= 0:
                dma("sp", pk.t[:], pk_in[l], writes=[pk], sembuf=pk)
                adab = c.sb("adab", [1, 6144])
                dma("pool", adab.t[:], adab_in[l:l + 1, :], writes=[adab], sembuf=adab)
                aw = [c.sb("aw%d" % i, [128, 8, 1024]) for i in range(2)]
                modT = c.ps("modT", [128, 48, R])
                mrow = c.ps("mrow", [R, 2, 512])
                modsb = c.sb("modsb", [128, 48, R])
                awv = adaw_in[l].rearrange("(kc p) n -> p kc n", p=128)
                for jb in range(6):
                    a = aw[jb % 2]
                    for kc in range(8):
                        dma(q2(), a.t[:, kc, :], awv[:, kc, jb * 1024:(jb + 1) * 1024], writes=[a], sembuf=a)
                    for j in range(8):
                        for kc in range(8):
                            op("pe", lambda e: e.matmul(modT.t[:, jb * 8 + j, :], a.t[:, kc, j * 128:(j + 1) * 128],
                                                        silc.t[:, kc, :], start=(kc == 0), stop=(kc == 7)),
                               reads=[a, silc], writes=[modT])
                    if jb in (2, 5):
                        wh = 0 if jb == 2 else 1
                        for nt in range(2):
                            for kc in range(8):
                                op("pe", lambda e: e.matmul(mrow.t[:, nt, :], silc.t[:, kc, :],
                                                            a.t[:, kc, nt * 512:(nt + 1) * 512], start=(kc == 0), stop=False),
                                   reads=[a, silc], writes=[mrow])
                            op("pe", lambda e: e.matmul(mrow.t[:, nt, :], ones1.t[0:1, 0:R],
                                                        adab.t[0:1, jb * 1024 + nt * 512: jb * 1024 + (nt + 1) * 512],
                                                        start=False, stop=True), reads=[adab, ones1], writes=[mrow])
                        op("dve", lambda e: e.tensor_copy(gtrow.t[:, wh, :], mrow.t[:].rearrange("p a b -> p (a b)")),
                           reads=[mrow], writes=[gtrow])
                op("dve", lambda e: e.tensor_tensor(modsb.t[:], modT.t[:],
                                                    pk.t[:, O_ADAB:O_ADAB + 48].unsqueeze(2).broadcast_to([128, 48, R]), ALU.add),
                   reads=[modT, pk], writes=[modsb])
                for (Gx, SHx, o_sh, o_sc, o_g) in ((G1, SH1, 0, 8, O_N1G), (G2, SH2, 24, 32, O_N2G)):
                    op("dve", lambda e: e.tensor_copy(SHx.t[:], modsb.t[:, o_sh:o_sh + 8, :]), reads=[modsb], writes=[SHx])
                    op("dve", lambda e: e.tensor_scalar(Gx.t[:], modsb.t[:, o_sc:o_sc + 8, :], 1.0, None, ALU.add),
                       reads=[modsb], writes=[Gx])
                    op("dve", lambda e: e.tensor_tensor(Gx.t[:], Gx.t[:],
                                                        pk.t[:, o_g:o_g + 8].unsqueeze(2).broadcast_to([128, 8, R]), ALU.mult),
                       reads=[Gx, pk], writes=[Gx])
                op("dve", lambda e: e.tensor_scalar(omka.t[:], pk.t[:, O_KA:O_KA + 4], -1.0, 1.0, ALU.mult, ALU.add),
                   reads=[pk], writes=[omka])

            def make_gt(which, bcp):
                res = []
                for r in range(R):
                    g = c.sb("gt%d" % r, [128, D])
                    for nt in range(2):
                        p = bcp[nt]
                        op("pe", lambda e: e.matmul(p.t[:], sel.t[0:R, r, :], gtrow.t[0:R, which, nt * 512:(nt + 1) * 512],
                                                    start=True, stop=True), reads=[sel, gtrow], writes=[p])
                        op("act", lambda e: e.copy(g.t[:, nt * 512:(nt + 1) * 512], p.t[:]), reads=[p], writes=[g])
                    res.append(g)
                return res

            def norm_transpose(xt, r, Gx, SHx, hT, col0, ss, rs, xn, pT):
                op("pool", lambda e: e.memset(ss.t[:], 0.0), writes=[ss])
                op("act", lambda e: e.activation(xn.t[:], xt.t[:], AF.Square, scale=1.0 / 32.0, accum_out=ss.t[:, 0:1]),
                   reads=[xt, ss], writes=[xn, ss])
                rstd_of(ss, rs)
                op("act", lambda e: e.activation(xn.t[:], xt.t[:], AF.Copy, scale=rs.t[:, 0:1]), reads=[xt, rs], writes=[xn])
                for kc in range(8):
                    op("pe", lambda e: e.transpose(pT.t[:, kc * 128:(kc + 1) * 128], xn.t[:, kc * 128:(kc + 1) * 128], identb.t[:]),
                       reads=[xn, identb], writes=[pT])
                for kc in range(8):
                    if kc % 2 == 0:
                        op("act", lambda e: e.activation(hT.t[:, kc, col0:col0 + 128], pT.t[:, kc * 128:(kc + 1) * 128], AF.Identity,
                                                         bias=SHx.t[:, kc, r:r + 1], scale=Gx.t[:, kc, r:r + 1]),
                           reads=[pT, SHx, Gx], writes=[hT])
                    else:
                        op("dve", lambda e: e.tensor_scalar(hT.t[:, kc, col0:col0 + 128], pT.t[:, kc * 128:(kc + 1) * 128],
                                                            Gx.t[:, kc, r:r + 1], SHx.t[:, kc, r:r + 1], ALU.mult, ALU.add),
                           reads=[pT, SHx, Gx], writes=[hT])

            def load_w_bf16(dst, src_ap, nk, ncols, wst, cwid=1424):
                cnt = 0
                for kc in range(nk):
                    for c0 in range(0, ncols, cwid):
                        w = min(cwid, ncols - c0)
                        s = wst[cnt % len(wst)]
                        dma(q2(), s.t[:, 0:w], src_ap[kc * 128:(kc + 1) * 128, c0:c0 + w], writes=[s], sembuf=s)
                        e_ = ("pool", "dve", "act")[cnt % 3]
                        if e_ == "act":
                            op("act", lambda e: e.copy(dst.t[:, kc, c0:c0 + w], s.t[:, 0:w]), reads=[s], writes=[dst])
                        else:
                            op(e_, lambda e: e.tensor_copy(dst.t[:, kc, c0:c0 + w], s.t[:, 0:w]), reads=[s], writes=[dst])
                        cnt += 1

            with (c.phase() if upto >= 1 else contextlib.nullcontext(False)) as _ph1:
              if upto >= 1:
                wb = c.sb("winb", [128, 8, INC], BF16)
                wst = [c.sb("wst%d" % i, [128, 1424]) for i in range(2)]
                load_w_bf16(wb, win_in[l], 8, INC, wst)
                SEG = min(T, 2048)
                hTs = [c.sb("hT%d" % i, [128, 8, SEG], BF16) for i in range(2)]
                xts = [c.sb("xt%d" % i, [128, D]) for i in range(2)]
                xns = [c.sb("xn%d" % i, [128, D], BF16) for i in range(2)]
                sss = [c.sb("ss%d" % i, [128, 1]) for i in range(2)]
                rss = [c.sb("rs%d" % i, [128, 1]) for i in range(2)]
                pTs = [c.ps("pT%d" % i, [128, D], BF16) for i in range(2)]
                pms = [c.ps("pm%d" % i, [128, 512]) for i in range(4)]
                stg = [c.sb("stg%d" % i, [128, SEG]) for i in range(2)]
                segl = []
                for b in range(NB):
                    xsrc = (x_in if l == 0 else Xs)[b]
                    csrc = (ctx_in if l == 0 else CXs)[b]
                    for (src, Ts, off, r, latent) in ((csrc, TC, 0, NB, False), (xsrc, T, TC, b, True)):
                        for s0 in range(0, Ts, SEG):
                            segl.append((b, src, off, r, latent, s0, min(SEG, Ts - s0)))
                cnt1 = {"tile": 0, "mm": 0}

                def prep_gen(si):
                    (b, src, off, r, latent, s0, S) = segl[si]
                    hT = hTs[si % 2]
                    for i in range(S // 128):
                        k = cnt1["tile"] % 2
                        cnt1["tile"] += 1
                        for (p0, npart, ap) in x_rows(src, l, s0 // 128 + i, latent):
                            dma("sp", xts[k].t[p0:p0 + npart, :], ap, writes=[xts[k]], sembuf=xts[k])
                        norm_transpose(xts[k], r, G1, SH1, hT, i * 128, sss[k], rss[k], xns[k], pTs[k])
                        yield

                def mm_gen(si):
                    (b, src, off, r, latent, s0, S) = segl[si]
                    hT = hTs[si % 2]
                    for j in range(23):
                        cw = 128 if j < 22 else 32
                        sg = stg[j % 2]
                        for nt in range(S // 512 if S >= 512 else 1):
                            n = min(512, S)
                            pm = pms[cnt1["mm"] % 4]
                            cnt1["mm"] += 1
                            for kc in range(8):
                                op("pe", lambda e: e.matmul(pm.t[0:cw, 0:n], wb.t[:, kc, j * 128:j * 128 + cw],
                                                            hT.t[:, kc, nt * 512:nt * 512 + n], start=(kc == 0), stop=(kc == 7)),
                                   reads=[wb, hT], writes=[pm])
                            if cnt1["mm"] % 2:
                                op("act", lambda e: e.copy(sg.t[0:cw, nt * 512:nt * 512 + n], pm.t[0:cw, 0:n]), reads=[pm], writes=[sg])
                            else:
                                op("dve", lambda e: e.tensor_copy(sg.t[0:cw, nt * 512:nt * 512 + n], pm.t[0:cw, 0:n]), reads=[pm], writes=[sg])
                            yield
                        dma("sp", UT[b, j * 128:j * 128 + cw, off + s0:off + s0 + S], sg.t[0:cw, 0:S], reads=[sg], sembuf=sg)

                for _ in prep_gen(0):
                    pass
                for si in range(len(segl)):
                    mg = mm_gen(si)
                    pg_ = prep_gen(si + 1) if si + 1 < len(segl) else None
                    nst = 0
                    for _ in mg:
                        nst += 1
                        if pg_ is not None and nst % 4 == 0:
                            try:
                                next(pg_)
                            except StopIteration:
                                pg_ = None
                    if pg_ is not None:
                        for _ in pg_:
                            pass

            with (c.phase() if upto >= 2 else contextlib.nullcontext(False)) as _ph2:
              if upto >= 2:
                zps = [c.sb("zp%d" % i, [128, TT + 60], F32R) for i in range(2)]
                zer = c.sb("zer", [128, 32])
                op("pool", lambda e: e.memset(zer.t[:], 0.0), writes=[zer])
                for zp in zps:
                    for (z0, zn) in ((0, 15), (15 + TC, 30), (TC + 45 + T, 15)):
                        op("pool", lambda e: e.tensor_copy(zp.t[:, z0:z0 + zn], zer.t[:, 0:zn]), reads=[zer], writes=[zp])
                dg = c.sb("dg", [128, 4 * CK, 128], F32R)
                for i in range(4 * CK):
                    e_ = ("dve", "pool")[i % 2]
                    op(e_, lambda e: e.tensor_scalar(dg.t[:, i, :], identf.t[:], pk.t[:, O_CW + i:O_CW + i + 1], None, ALU.mult),
                       reads=[identf, pk], writes=[dg])
                gates = [c.sb("gate%d" % i, [128, TT]) for i in range(1)]
                vals = [c.sb("val%d" % i, [128, TT]) for i in range(1)]
                mixst = c.sb("mixst", [128, TT], BF16)
                NT2 = 3

                def mk2(name):
                    return [c.sb("%s%d" % (name, i), [128, 512]) for i in range(NT2)]
                accs, sqts, mts, dtts, msqs, vars_ = mk2("acc"), mk2("sqt"), mk2("mt"), mk2("dtt"), mk2("msq"), mk2("var")
                pcv = [c.ps("pcv%d" % i, [128, 512]) for i in range(3)]
                pAB = [c.ps("pAB%d" % i, [128, 512]) for i in range(4)]
                tiles = [(0 + s, 0 + s, min(512, TC - s)) for s in range(0, TC, 512)]
                tiles += [(TC + 30 + s, TC + s, 512) for s in range(0, T, 512)]
                itn = 0
                ibc = 0
                its2 = [(b, cc) for b in range(NB) for cc in range(4)]

                def prep2(i2):
                    (b, cc) = its2[i2]
                    zp = zps[i2 % 2]
                    gate, val = gates[0], vals[0]
                    dma("sp", gate.t[:], UT[b, 512 + cc * 128:512 + (cc + 1) * 128, :], writes=[gate], sembuf=gate)
                    dma("sp", val.t[:], UT[b, cc * 128:(cc + 1) * 128, :], writes=[val], sembuf=val)
                    op("act", lambda e: e.activation(gate.t[:], gate.t[:], AF.Sigmoid), reads=[gate], writes=[gate])
                    op("pool", lambda e: e.tensor_tensor(zp.t[:, 15:15 + TC], val.t[:, 0:TC], gate.t[:, 0:TC], ALU.mult),
                       reads=[val, gate], writes=[zp])
                    hT_ = T // 2
                    op("dve", lambda e: e.tensor_tensor(zp.t[:, TC + 45:TC + 45 + hT_], val.t[:, TC:TC + hT_], gate.t[:, TC:TC + hT_], ALU.mult),
                       reads=[val, gate], writes=[zp])
                    op("pool", lambda e: e.tensor_tensor(zp.t[:, TC + 45 + hT_:TC + 45 + T], val.t[:, TC + hT_:TT], gate.t[:, TC + hT_:TT], ALU.mult),
                       reads=[val, gate], writes=[zp])
                prep2(0)
                if True:
                    for i2 in range(len(its2)):
                        (b, cc) = its2[i2]
                        zp = zps[i2 % 2]
                        tcount = 0
                        for (a0, o0, n) in tiles:
                            tcount += 1
                            if tcount == min(3, len(tiles)) and i2 + 1 < len(its2):
                                prep2(i2 + 1)
                            itn += 1
                            k = itn % NT2
                            acc, sqt, mt, dtt, msq, var = accs[k], sqts[k], mts[k], dtts[k], msqs[k], vars_[k]
                            pc_ = pcv[itn % 3]
                            for j in range(CK):
                                op("pe", lambda e: e.matmul(pc_.t[:, 0:n], dg.t[:, cc * CK + j, :], zp.t[:, a0 + j:a0 + j + n],
                                                            start=(j == 0), stop=(j == CK - 1)), reads=[dg, zp], writes=[pc_])
                            op("act", lambda e: e.activation(acc.t[:, 0:n], pc_.t[:, 0:n], AF.Identity, bias=pk.t[:, O_CB + cc:O_CB + cc + 1], scale=1.0),
                               reads=[pc_, pk], writes=[acc])
                            op("act", lambda e: e.activation(sqt.t[:, 0:n], acc.t[:, 0:n], AF.Square), reads=[acc], writes=[sqt])
                            pA, pB = pAB[(itn * 2) % 4], pAB[(itn * 2 + 1) % 4]
                            op("pe", lambda e: e.matmul(pA.t[:, 0:n], bones.t[:], acc.t[:, 0:n], start=True, stop=True),
                               reads=[bones, acc], writes=[pA])
                            op("pe", lambda e: e.matmul(pB.t[:, 0:n], bones.t[:], sqt.t[:, 0:n], start=True, stop=True),
                               reads=[bones, sqt], writes=[pB])
                            op("act", lambda e: e.activation(mt.t[:, 0:n], pA.t[:, 0:n], AF.Copy, scale=1.0 / 64.0), reads=[pA], writes=[mt])
                            op("dve", lambda e: e.tensor_tensor(dtt.t[:, 0:n], acc.t[:, 0:n], mt.t[:, 0:n], ALU.subtract),
                               reads=[acc, mt], writes=[dtt])
                            op("pool", lambda e: e.tensor_tensor(msq.t[:, 0:n], mt.t[:, 0:n], mt.t[:, 0:n], ALU.mult), reads=[mt], writes=[msq])
                            op("dve", lambda e: e.scalar_tensor_tensor(var.t[:, 0:n], pB.t[:, 0:n], 1.0 / 64.0, msq.t[:, 0:n],
                                                                       ALU.mult, ALU.subtract), reads=[pB, msq], writes=[var])
                            op("act", lambda e: e.activation(var.t[:, 0:n], var.t[:, 0:n], AF.Sqrt, bias=1e-5, scale=1.0), reads=[var], writes=[var])
                            op("dve", lambda e: e.reciprocal(var.t[:, 0:n], var.t[:, 0:n]), reads=[var], writes=[var])
                            op("pool", lambda e: e.tensor_tensor(dtt.t[:, 0:n], dtt.t[:, 0:n], var.t[:, 0:n], ALU.mult), reads=[dtt, var], writes=[dtt])
                            op("act", lambda e: e.activation(mixst.t[:, o0:o0 + n], dtt.t[:, 0:n], AF.Silu,
                                                             bias=pk.t[:, O_CBB + cc:O_CBB + cc + 1], scale=pk.t[:, O_CG + cc:O_CG + cc + 1]),
                               reads=[dtt, pk], writes=[mixst])
                        dma("sp", MIXT[b, cc * 128:(cc + 1) * 128, :], mixst.t[:], reads=[mixst], sembuf=mixst)

            with (c.phase() if upto >= 3 else contextlib.nullcontext(False)) as _ph3:
              if upto >= 3:
                dup = c.sb("dup", [64, 2, 512]); iup = c.sb("iup", [128, 2, 512])
                gup1 = c.sb("gup1", [128, 512]); gup2 = c.sb("gup2", [32, 512])
                for d in range(2):
                    dma("sp", dup.t[:, d, :], dup_in[l, d], writes=[dup], sembuf=dup)
                    dma("sp", iup.t[64:128, d, :], iup_in[l, d], writes=[iup], sembuf=iup)
                dma("pool", gup1.t[:], gup_in[l, 0:128, :], writes=[gup1], sembuf=gup1)
                dma("pool", gup2.t[:], gup_in[l, 128:160, :], writes=[gup2], sembuf=gup2)
                S3 = 512
                NS = 2
                pp = [c.ps("pp%d" % i, [128, 512]) for i in range(8)]
                pi = [0]

                def nps():
                    pi[0] += 1
                    return pp[pi[0] % 8]

                def mk(name, shape=None, dt=F32):
                    return [c.sb("%s%d" % (name, i), shape or [128, S3], dt) for i in range(NS)]
                inb = {k_: mk("in" + k_, [128, S3 + 2]) for k_ in ("r", "k", "v", "wa", "g1", "g2")}
                B_ = {k_: mk(k_) for k_ in ("r", "k", "v", "wa", "g1", "g2", "gst", "rk", "bon", "kkr", "sq", "rn", "kk",
                                             "sig", "a", "t1", "kd", "bd", "cs", "ex", "rem", "cb", "E1", "E2", "E3", "E4",
                                             "kh", "bh")}
                for k_ in ("o0", "o1", "o2", "o3"):
                    B_[k_] = mk(k_, None, F32)
                tms = {k_: mk("tm" + k_, [128, 4, 128]) for k_ in ("v", "kh", "bh")}
                pcs = mk("pcs", [128, 8])
                it = 0

                def shift(dst, ib, ch, S, g0, first, lastseg, b, rows=128):
                    f0 = 1024 + ch * 128
                    lo = g0 - (0 if first else 1)
                    hi = g0 + S + (0 if lastseg else 1)
                    if first:
                        op("pool", lambda e: e.memset(ib.t[0:rows, 0:1], 0.0), writes=[ib])
                    if lastseg:
                        op("pool", lambda e: e.memset(ib.t[0:rows, S + 1:S + 2], 0.0), writes=[ib])
                    dma("sp", ib.t[0:rows, (1 if first else 0):(1 if first else 0) + hi - lo], UT[b, f0:f0 + rows, lo:hi], writes=[ib], sembuf=ib)
                    sw = O_SW + ch * 3
                    op("act", lambda e: e.activation(dst.t[0:rows, 0:S], ib.t[0:rows, 0:S], AF.Copy, scale=pk.t[0:rows, sw:sw + 1]),
                       reads=[ib, pk], writes=[dst])
                    for j in (1, 2):
                        op("dve", lambda e: e.scalar_tensor_tensor(dst.t[0:rows, 0:S], ib.t[0:rows, j:j + S], pk.t[0:rows, sw + j:sw + j + 1],
                                                                   dst.t[0:rows, 0:S], ALU.mult, ALU.add), reads=[ib, pk, dst], writes=[dst])

                def hp_task(b, g0, S, first, lastseg, hp, ks, wa, g1, g2):
                    NCS = S // 64
                    pend = []

                    def st(dst_ap, src_ap, buf):
                        pend.append((dst_ap, src_ap, buf))

                    def flush():
                        for (dst_ap, src_ap, buf) in pend:
                            dma("sp", dst_ap, src_ap, reads=[buf], sembuf=buf)
                        del pend[:]
                    X = {k_: v_[ks] for k_, v_ in B_.items()}
                    r_, k_b, v_ = X["r"], X["k"], X["v"]
                    shift(r_, inb["r"][ks], hp, S, g0, first, lastseg, b)
                    yield
                    shift(k_b, inb["k"][ks], 4 + hp, S, g0, first, lastseg, b)
                    yield
                    shift(v_, inb["v"][ks], 8 + hp, S, g0, first, lastseg, b)
                    fs = slice(hp * 128, (hp + 1) * 128)
                    p = nps()
                    op("pe", lambda e: e.matmul(p.t[:, 0:S], gup1.t[:, fs], g1.t[:, 0:S], start=True, stop=False), reads=[gup1, g1], writes=[p])
                    op("pe", lambda e: e.matmul(p.t[:, 0:S], gup2.t[0:32, fs], g2.t[0:32, 0:S], start=False, stop=True), reads=[gup2, g2], writes=[p])
                    yield
                    gs = X["gst"]
                    op("act", lambda e: e.copy(gs.t[:, 0:S], p.t[:, 0:S]), reads=[p], writes=[gs])
                    st(GG[b, fs, g0:g0 + S], gs.t[:, 0:S], gs)
                    rk = X["rk"]
                    op("dve", lambda e: e.scalar_tensor_tensor(rk.t[:, 0:S], r_.t[:, 0:S], pk.t[:, O_RK + hp:O_RK + hp + 1], k_b.t[:, 0:S],
                                                               ALU.mult, ALU.mult), reads=[r_, k_b, pk], writes=[rk])
                    kkr, sq, rn, kk = X["kkr"], X["sq"], X["rn"], X["kk"]
                    op("pool", lambda e: e.tensor_scalar(kkr.t[:, 0:S], k_b.t[:, 0:S], pk.t[:, O_KK + hp:O_KK + hp + 1], None, ALU.mult),
                       reads=[k_b, pk], writes=[kkr])
                    yield
                    flush()
                    p1 = nps()
                    op("pe", lambda e: e.matmul(p1.t[:, 0:S], bones.t[:], rk.t[:, 0:S], start=True, stop=True), reads=[bones, rk], writes=[p1])
                    op("act", lambda e: e.activation(sq.t[:, 0:S], kkr.t[:, 0:S], AF.Square), reads=[kkr], writes=[sq])
                    yield
                    bon = X["bon"]
                    op("dve", lambda e: e.tensor_tensor(bon.t[:, 0:S], p1.t[:, 0:S], v_.t[:, 0:S], ALU.mult), reads=[p1, v_], writes=[bon])
                    st(BON[b, fs, g0:g0 + S], bon.t[:, 0:S], bon)
                    p2 = nps()
                    op("pe", lambda e: e.matmul(p2.t[:, 0:S], bones.t[:], sq.t[:, 0:S], start=True, stop=True), reads=[bones, sq], writes=[p2])
                    yield
                    flush()
                    op("act", lambda e: e.activation(rn.t[:, 0:S], p2.t[:, 0:S], AF.Sqrt), reads=[p2], writes=[rn])
                    yield
                    op("dve", lambda e: e.tensor_scalar(rn.t[:, 0:S], rn.t[:, 0:S], 1e-12, None, ALU.max), reads=[rn], writes=[rn])
                    yield
                    op("dve", lambda e: e.reciprocal(rn.t[:, 0:S], rn.t[:, 0:S]), reads=[rn], writes=[rn])
                    yield
                    op("pool", lambda e: e.tensor_tensor(kk.t[:, 0:S], kkr.t[:, 0:S], rn.t[:, 0:S], ALU.mult), reads=[kkr, rn], writes=[kk])

                    def transp_out(src, tmb, dst_ap):
                        p = nps()
                        nb_ = S // 128
                        for tb in range(nb_):
                            op("pe", lambda e: e.transpose(p.t[:, tb * 128:(tb + 1) * 128], src.t[:, tb * 128:(tb + 1) * 128], identf.t[:]),
                               reads=[src, identf], writes=[p])
                        yield
                        op("act", lambda e: e.copy(tmb.t[:, 0:nb_, :], p.t[:, 0:nb_ * 128].rearrange("p (a f) -> p a f", f=128)),
                           reads=[p], writes=[tmb])
                        st(dst_ap.rearrange("(tb p) f -> p tb f", p=128), tmb.t[:, 0:nb_, :], tmb)
                    yield from transp_out(v_, tms["v"][ks], VT[b, g0:g0 + S, fs])
                    for d in range(2):
                        sig, a_, t1, kd, bd = X["sig"], X["a"], X["t1"], X["kd"], X["bd"]
                        pa_ = nps()
                        op("pe", lambda e: e.matmul(pa_.t[:, 0:S], dup.t[0:64, d, fs], wa.t[0:64, 0:S], start=True, stop=True), reads=[dup, wa], writes=[pa_])
                        pb_ = nps()
                        op("pe", lambda e: e.matmul(pb_.t[:, 0:S], iup.t[64:128, d, fs], wa.t[64:128, 0:S], start=True, stop=True), reads=[iup, wa], writes=[pb_])
                        yield
                        flush()
                        op("act", lambda e: e.activation(sig.t[:, 0:S], pa_.t[:, 0:S], AF.Sigmoid, bias=pk.t[:, O_DB0 + d * 4 + hp:O_DB0 + d * 4 + hp + 1]),
                           reads=[pa_, pk], writes=[sig])
                        op("act", lambda e: e.activation(a_.t[:, 0:S], pb_.t[:, 0:S], AF.Sigmoid, bias=pk.t[:, O_IB0 + d * 4 + hp:O_IB0 + d * 4 + hp + 1]),
                           reads=[pb_, pk], writes=[a_])
                        yield
                        cs, ex, rem, cb = X["cs"], X["ex"], X["rem"], X["cb"]
                        op("dve", lambda e: e.tensor_tensor_scan(cs.t[:, 0:S], rmask.t[:, 0:S], sig.t[:, 0:S], 0.0, ALU.mult, ALU.add),
                           reads=[rmask, sig], writes=[cs])
                        op("pool", lambda e: e.tensor_tensor(bd.t[:, 0:S], kk.t[:, 0:S], a_.t[:, 0:S], ALU.mult), reads=[kk, a_], writes=[bd])
                        yield
                        op("act", lambda e: e.activation(t1.t[:, 0:S], a_.t[:, 0:S], AF.Identity, bias=omka.t[:, hp:hp + 1],
                                                         scale=pk.t[:, O_KA + hp:O_KA + hp + 1]), reads=[a_, pk, omka], writes=[t1])
                        op("pool", lambda e: e.tensor_tensor(ex.t[:, 0:S], cs.t[:, 0:S], sig.t[:, 0:S], ALU.subtract), reads=[cs, sig], writes=[ex])
                        csv = cs.t[:, 0:S].rearrange("p (c t) -> p c t", t=64)
                        pc_ = pcs[ks]
                        op("act", lambda e: e.activation(pc_.t[:, 0:NCS], csv[:, :, 63], AF.Exp, scale=LAM), reads=[cs], writes=[pc_])
                        st(PC[b, d, fs, g0 // 64:g0 // 64 + NCS], pc_.t[:, 0:NCS], pc_)
                        yield
                        flush()
                        op("dve", lambda e: e.tensor_tensor(rem.t[:, 0:S].rearrange("p (c t) -> p c t", t=64),
                                                            csv[:, :, 63:64].broadcast_to([128, NCS, 64]), csv, ALU.subtract),
                           reads=[cs], writes=[rem])
                        op("pool", lambda e: e.tensor_tensor(kd.t[:, 0:S], t1.t[:, 0:S], k_b.t[:, 0:S], ALU.mult), reads=[t1, k_b], writes=[kd])
                        yield
                        if d == 0:
                            incl, excl, aft = cs, ex, rem
                        else:
                            op("pool", lambda e: e.tensor_tensor(cb.t[:, 0:S], rem.t[:, 0:S], sig.t[:, 0:S], ALU.add), reads=[rem, sig], writes=[cb])
                            incl, excl, aft = cb, rem, ex
                            yield
                        E1, E2, E3, E4 = X["E1"], X["E2"], X["E3"], X["E4"]
                        op("act", lambda e: e.activation(E3.t[:, 0:S], incl.t[:, 0:S], AF.Exp, scale=-LAM), reads=[incl], writes=[E3])
                        op("act", lambda e: e.activation(E4.t[:, 0:S], aft.t[:, 0:S], AF.Exp, scale=LAM), reads=[aft], writes=[E4])
                        yield
                        op("act", lambda e: e.activation(E2.t[:, 0:S], excl.t[:, 0:S], AF.Exp, scale=LAM), reads=[excl], writes=[E2])
                        op("act", lambda e: e.activation(E1.t[:, 0:S], incl.t[:, 0:S], AF.Exp, scale=LAM), reads=[incl], writes=[E1])
                        kh, bh = X["kh"], X["bh"]
                        op("dve", lambda e: e.tensor_tensor(kh.t[:, 0:S], kd.t[:, 0:S], E4.t[:, 0:S], ALU.mult), reads=[kd, E4], writes=[kh])
                        op("pool", lambda e: e.tensor_tensor(bh.t[:, 0:S], bd.t[:, 0:S], E4.t[:, 0:S], ALU.mult), reads=[bd, E4], writes=[bh])
                        yield
                        outs = ((X["o0"], kd, E3), (X["o1"], bd, E3), (X["o2"], kk, E2), (X["o3"], r_, E1))
                        for qi, (o_, a1, a2) in enumerate(outs):
                            op(("dve", "pool")[qi % 2], lambda e: e.tensor_tensor(o_.t[:, 0:S], a1.t[:, 0:S], a2.t[:, 0:S], ALU.mult),
                               reads=[a1, a2], writes=[o_])
                            st(FM[b, d, qi, fs, g0:g0 + S], o_.t[:, 0:S], o_)
                            if qi == 1:
                                yield
                        yield from transp_out(kh, tms["kh"][ks], TM[b, d, 0, g0:g0 + S, fs])
                        flush()
                        yield from transp_out(bh, tms["bh"][ks], TM[b, d, 1, g0:g0 + S, fs])
                    yield
                    flush()

                tasks = []
                segno = 0
                for b in range(NB):
                    segs = [(0, TC, True, True)]
                    segs += [(TC + s, S3, s == 0, s + S3 == T) for s in range(0, T, S3)]
                    for (g0, S, first, lastseg) in segs:
                        for hp in range(4):
                            tasks.append((b, g0, S, first, lastseg, hp, segno % NS))
                        segno += 1
                live = []
                ti_ = 0
                while ti_ < len(tasks) or live:
                    while len(live) < NS and ti_ < len(tasks):
                        (b, g0, S, first, lastseg, hp, k0) = tasks[ti_]
                        wa, g1, g2 = B_["wa"][k0], B_["g1"][k0], B_["g2"][k0]
                        if hp == 0:
                            shift(wa, inb["wa"][k0], 12, S, g0, first, lastseg, b)
                            shift(g1, inb["g1"][k0], 13, S, g0, first, lastseg, b)
                            shift(g2, inb["g2"][k0], 14, S, g0, first, lastseg, b, rows=32)
                            op("act", lambda e: e.activation(wa.t[0:64, 0:S], wa.t[0:64, 0:S], AF.Tanh), reads=[wa], writes=[wa])
                            op("act", lambda e: e.activation(g1.t[:, 0:S], g1.t[:, 0:S], AF.Sigmoid), reads=[g1], writes=[g1])
                            op("act", lambda e: e.activation(g2.t[0:32, 0:S], g2.t[0:32, 0:S], AF.Sigmoid), reads=[g2], writes=[g2])
                        live.append(hp_task(b, g0, S, first, lastseg, hp, ti_ % NS, wa, g1, g2))
                        ti_ += 1
                    for g in list(live):
                        try:
                            next(g)
                        except StopIteration:
                            live.remove(g)

            with (c.phase() if upto >= 4 else contextlib.nullcontext(False)) as _ph4:
              if upto >= 4:
                pbk = [c.ps("pb%d" % i, [128, 512]) for i in range(8)]
                cn = {"pb": 0, "tp": 0, "e": 0, "ht": 0}

                def npb():
                    cn["pb"] += 1
                    return pbk[cn["pb"] % 8]
                tp = [c.sb("tp%d" % i, [64, 8, 64], F32R) for i in range(8 if INV_BF16 else 12)]

                def ntp():
                    cn["tp"] += 1
                    return tp[cn["tp"] % len(tp)]
                tpq = [c.sb("tpq%d" % i, [64, 8, 64], BF16) for i in range(12)] if INV_BF16 else tp
                cn["tpq"] = 0

                def ntq():
                    if not INV_BF16:
                        return ntp()
                    cn["tpq"] += 1
                    return tpq[cn["tpq"] % len(tpq)]
                htmp = [c.sb("htmp%d" % i, [128, 4, 128]) for i in range(2)]
                bmask = c.sb("bmask", [128, 2])
                op("pool", lambda e: e.memset(bmask.t[:], 0.0), writes=[bmask])
                op("pool", lambda e: e.memset(bmask.t[0:64, 0:1], 1.0), writes=[bmask])
                op("pool", lambda e: e.memset(bmask.t[64:128, 1:2], 1.0), writes=[bmask])

                def V8(p):
                    return p.t[0:64, :].rearrange("p (h t) -> p h t", t=64)

                def V4(p):
                    return p.t[0:64, :].rearrange("p (h q t) -> p h q t", q=2, t=64)

                def VH(p):
                    return p.t[:, :].rearrange("p (a f) -> p a f", f=128)

                def evac(dst_ap, src_ap, reads, writes, scale=None):
                    cn["e"] += 1
                    if cn["e"] % 2 and scale is None:
                        op("dve", lambda e: e.tensor_copy(dst_ap, src_ap), reads=reads, writes=writes)
                    else:
                        op("act", lambda e: e.activation(dst_ap, src_ap, AF.Copy, scale=(1.0 if scale is None else scale)), reads=reads, writes=writes)
                strs = []
                for b in range(NB):
                    for d in range(2):
                        cl = list(range(0, TC, 64))
                        ll = list(range(TC, TT, 64))
                        i_ = len(strs)
                        strs.append(dict(
                            b=b, d=d, chunks=(cl + ll) if d == 0 else (cl[::-1] + ll[::-1]),
                            fm32=c.sb("fm32_%d" % i_, [128, 4, 4, 64]), tm32=c.sb("tm32_%d" % i_, [64, 2, 512]), vt32=c.sb("vt32_%d" % i_, [64, 512]),
                            pc=[c.sb("pc%d_%d" % (i_, k), [128, 4, 1]) for k in range(2)],
                            tm=c.sb("tm_%d" % i_, [64, 2, 512], F32R), vt=c.sb("vt_%d" % i_, [64, 512], F32R),
                            yst=c.sb("yst_%d" % i_, [64, 8, 64]), H=c.sb("H_%d" % i_, [128, 4, 128]), Hb=c.sb("Hb_%d" % i_, [128, 4, 128], F32R),
                            bd=c.sb("bd_%d" % i_, [128, 4, 4, 2, 64], F32R), AK=c.sb("AK_%d" % i_, [64, 8, 2, 64], F32R),
                            AB=c.sb("AB_%d" % i_, [64, 8, 3, 64], F32R),
                            XPb=(c.sb("XPb_%d" % i_, [64, 8, 2, 64], BF16) if INV_BF16 else None)))

                def load(S_, n):
                    g0 = S_["chunks"][n]
                    b, d = S_["b"], S_["d"]
                    dma("sp", S_["fm32"].t[:], FM[b, d].rearrange("q (hp p) t -> p q hp t", p=128)[:, :, :, g0:g0 + 64],
                        writes=[S_["fm32"]], sembuf=S_["fm32"])
                    dma("sp", S_["tm32"].t[:], TM[b, d, :, g0:g0 + 64, :].rearrange("q p f -> p q f"), writes=[S_["tm32"]], sembuf=S_["tm32"])
                    dma("sp", S_["vt32"].t[:], VT[b, g0:g0 + 64, :], writes=[S_["vt32"]], sembuf=S_["vt32"])
                    pcb = S_["pc"][n % 2]
                    dma("sp", pcb.t[:], PC[b, d].rearrange("(hp p) c -> p hp c", p=128)[:, :, g0 // 64:g0 // 64 + 1], writes=[pcb], sembuf=pcb, slow=True)

                def store_y(S_, n):
                    g0 = S_["chunks"][n]
                    dma("sp", YT[S_["b"], S_["d"]].rearrange("(h v) t -> v h t", v=64)[:, :, g0:g0 + 64], S_["yst"].t[:],
                        reads=[S_["yst"]], sembuf=S_["yst"])

                def chunk_gen(S_, n):
                    d = S_["d"]
                    bd, AK, AB, tm, vt, H, Hb, yst = S_["bd"], S_["AK"], S_["AB"], S_["tm"], S_["vt"], S_["H"], S_["Hb"], S_["yst"]
                    pcb = S_["pc"][n % 2]
                    fm32, tm32, vt32 = S_["fm32"], S_["tm32"], S_["vt32"]
                    if n > 0:
                        store_y(S_, n - 1)
                    op("pool", lambda e: e.tensor_tensor(
                        bd.t[:].rearrange("p a b q t -> p (a b) q t"),
                        fm32.t[:].rearrange("p a b t -> p (a b) t").unsqueeze(2).broadcast_to([128, 16, 2, 64]),
                        bmask.t[:].unsqueeze(1).unsqueeze(3).broadcast_to([128, 16, 2, 64]), ALU.mult), reads=[fm32, bmask], writes=[bd])
                    op("act", lambda e: e.copy(tm.t[:], tm32.t[:]), reads=[tm32], writes=[tm])
                    op("dve", lambda e: e.tensor_copy(vt.t[:], vt32.t[:]), reads=[vt32], writes=[vt])
                    yield

                    def BD(qty, h):
                        return bd.t[:, qty, h // 2, h % 2, :]
                    if n + 1 < len(S_["chunks"]):
                        load(S_, n + 1)
                    for (dst, qty) in ((AK, 0), (AB, 1)):
                        dsl = slice(0, 2) if qty == 0 else slice(1, 3)
                        pa = [npb(), npb()]
                        for h in range(8):
                            op("pe", lambda e: e.matmul(V4(pa[h // 4])[:, h % 4, :, :], BD(qty, h), bd.t[:, 2:4, h // 2, h % 2, :], start=True, stop=True),
                               reads=[bd], writes=[pa[h // 4]])
                        for hh in range(2):
                            op("dve", lambda e: e.tensor_tensor(dst.t[:, hh * 4:hh * 4 + 4, dsl, :], V4(pa[hh]),
                                                                MK[d].t[:].unsqueeze(1).broadcast_to([64, 4, 2, 64]), ALU.mult),
                               reads=[pa[hh], MK[d]], writes=[dst])
                        yield
                    pn = npb()
                    for h in range(8):
                        op("pe", lambda e: e.matmul(V8(pn)[:, h, :], BD(2, h), BD(1, h), start=True, stop=True), reads=[bd], writes=[pn])
                    Q = ntq()
                    op("dve", lambda e: e.tensor_tensor(Q.t[:], V8(pn), MNT[d].t[:].unsqueeze(1).broadcast_to([64, 8, 64]), ALU.mult),
                       reads=[pn, MNT[d]], writes=[Q])
                    XT = S_["XPb"] if INV_BF16 else AB
                    op("pool", lambda e: e.tensor_tensor(XT.t[:, :, 0, :], identf.t[0:64, 0:64].unsqueeze(1).broadcast_to([64, 8, 64]), AB.t[:, :, 1, :], ALU.subtract),
                       reads=[identf, AB], writes=[XT])
                    if INV_BF16:
                        op("act", lambda e: e.copy(XT.t[:, :, 1, :], AB.t[:, :, 1, :]), reads=[AB], writes=[XT])
                    yield
                    pP = npb()
                    pQ = npb()
                    for h in range(8):
                        op("pe", lambda e: e.matmul(V8(pP)[:, h, :], Q.t[:, h, :], XT.t[:, h, 1, :], start=True, stop=True), reads=[Q, XT], writes=[pP])
                    for h in range(8):
                        op("pe", lambda e: e.matmul(V8(pQ)[:, h, :], XT.t[:, h, 1, :], Q.t[:, h, :], start=True, stop=True), reads=[Q, XT], writes=[pQ])
                    evac(XT.t[:, :, 1, :], V8(pP), [pP], [XT])
                    Qn = ntq()
                    evac(Qn.t[:], V8(pQ), [pQ], [Qn])
                    Q = Qn
                    yield
                    for r_ in range(1, 5):
                        pa = [npb(), npb()]
                        nq = 2 if r_ < 4 else 1
                        for h in range(8):
                            op("pe", lambda e: e.matmul(V4(pa[h // 4])[:, h % 4, 0:nq, :], Q.t[:, h, :], XT.t[:, h, 0:nq, :], start=True, stop=True),
                               reads=[Q, XT], writes=[pa[h // 4]])
                        pQ = npb()
                        for h in range(8):
                            op("pe", lambda e: e.matmul(V8(pQ)[:, h, :], XT.t[:, h, 1, :], Q.t[:, h, :], start=True, stop=True), reads=[Q, XT], writes=[pQ])
                        for hh in range(2):
                            op("dve", lambda e: e.tensor_tensor(XT.t[:, hh * 4:hh * 4 + 4, 0, :], V4(pa[hh])[:, :, 0, :], XT.t[:, hh * 4:hh * 4 + 4, 0, :], ALU.add),
                               reads=[pa[hh], XT], writes=[XT])
                            if r_ < 4:
                                evac(XT.t[:, hh * 4:hh * 4 + 4, 1, :], V4(pa[hh])[:, :, 1, :], [pa[hh]], [XT])
                        Qn = ntq()
                        evac(Qn.t[:], V8(pQ), [pQ], [Qn])
                        Q = Qn
                        yield
                    pX = npb()
                    for h in range(8):
                        op("pe", lambda e: e.matmul(V8(pX)[:, h, :], Q.t[:, h, :], XT.t[:, h, 0, :], start=True, stop=True), reads=[Q, XT], writes=[pX])
                    op("dve", lambda e: e.tensor_tensor(XT.t[:, :, 0, :], V8(pX), XT.t[:, :, 0, :], ALU.add), reads=[pX, XT], writes=[XT])
                    if INV_BF16:
                        op("act", lambda e: e.copy(AB.t[:, :, 0, :], XT.t[:, :, 0, :]), reads=[XT], writes=[AB])
                    yield
                    pW = npb()
                    for h in range(8):
                        hp, qq = h // 2, h % 2
                        op("pe", lambda e: e.matmul(V8(pW)[:, h, :], BD(2, h), Hb.t[:, hp, qq * 64:qq * 64 + 64], start=True, stop=False),
                           reads=[bd, Hb], writes=[pW])
                        op("pe", lambda e: e.matmul(V8(pW)[:, h, :], AK.t[:, h, 0, :], vt.t[:, h * 64:(h + 1) * 64], start=False, stop=True),
                           reads=[AK, vt], writes=[pW])
                    W1 = ntp()
                    evac(W1.t[:], V8(pW), [pW], [W1])
                    yield
                    pU = npb()
                    for h in range(8):
                        op("pe", lambda e: e.matmul(V8(pU)[:, h, :], AB.t[:, h, 0, :], W1.t[:, h, :], start=True, stop=True), reads=[AB, W1], writes=[pU])
                    Un = ntp()
                    evac(Un.t[:], V8(pU), [pU], [Un], scale=-1.0)
                    yield
                    pY = npb()
                    for h in range(8):
                        hp, qq = h // 2, h % 2
                        op("pe", lambda e: e.matmul(V8(pY)[:, h, :], Hb.t[:, hp, qq * 64:qq * 64 + 64], BD(3, h), start=True, stop=False),
                           reads=[bd, Hb], writes=[pY])
                        op("pe", lambda e: e.matmul(V8(pY)[:, h, :], vt.t[:, h * 64:(h + 1) * 64], AK.t[:, h, 1, :], start=False, stop=False),
                           reads=[AK, vt], writes=[pY])
                        op("pe", lambda e: e.matmul(V8(pY)[:, h, :], Un.t[:, h, :], AB.t[:, h, 2, :], start=False, stop=True),
                           reads=[AB, Un], writes=[pY])
                    evac(yst.t[:], V8(pY), [pY], [yst])
                    pH = npb()
                    Unf = Un.t[:].rearrange("p h t -> p (h t)")
                    for hp in range(4):
                        fs = slice(hp * 128, (hp + 1) * 128)
                        op("pe", lambda e: e.matmul(VH(pH)[:, hp, :], tm.t[:, 0, fs], vt.t[:, fs], start=True, stop=False), reads=[tm, vt], writes=[pH])
                        op("pe", lambda e: e.matmul(VH(pH)[:, hp, :], tm.t[:, 1, fs], Unf[:, fs], start=False, stop=True), reads=[tm, Un], writes=[pH])
                    cn["ht"] += 1
                    ht = htmp[cn["ht"] % 2]
                    op("pool", lambda e: e.tensor_tensor(ht.t[:], H.t[:], pcb.t[:].broadcast_to([128, 4, 128]), ALU.mult), reads=[H, pcb], writes=[ht])
                    op("dve", lambda e: e.tensor_tensor(H.t[:], ht.t[:], VH(pH), ALU.add), reads=[ht, pH], writes=[H])
                    op("act", lambda e: e.copy(Hb.t[:], H.t[:]), reads=[H], writes=[Hb])
                    yield

                for S_ in strs:
                    op("pool", lambda e: e.memset(S_["H"].t[:], 0.0), writes=[S_["H"]])
                    op("act", lambda e: e.copy(S_["Hb"].t[:], S_["H"].t[:]), reads=[S_["H"]], writes=[S_["Hb"]])
                    load(S_, 0)
                nchunks = len(strs[0]["chunks"])
                for n in range(nchunks):
                    gens = [chunk_gen(S_, n) for S_ in strs]
                    while gens:
                        for g in list(gens):
                            try:
                                next(g)
                            except StopIteration:
                                gens.remove(g)
                for S_ in strs:
                    store_y(S_, nchunks - 1)

            ffn_st = contextlib.ExitStack()
            _prev = c.stack
            c.stack = ffn_st
            wgb = c.sb("wgb", [128, 8, DFF], BF16)
            wub = c.sb("wub", [128, 8, DFF], BF16)
            wdb = c.sb("wdb", [128, NFC, D], BF16)
            c.stack = _prev

            def wload_gen(wst4):
                steps = []
                for (dst, src, nk, ncols) in ((wgb, wg_in[l], 8, DFF), (wub, wu_in[l], 8, DFF), (wdb, wd_in[l], NFC, D)):
                    for kc in range(nk):
                        for c0 in range(0, ncols, 640):
                            steps.append((dst, src, kc, c0, min(640, ncols - c0)))
                pend = []

                def cast(i, dst, kc, c0, w, s_):
                    if i % 2:
                        op("act", lambda e: e.copy(dst.t[:, kc, c0:c0 + w], s_.t[:, 0:w]), reads=[s_], writes=[dst])
                    else:
                        op("pool", lambda e: e.tensor_copy(dst.t[:, kc, c0:c0 + w], s_.t[:, 0:w]), reads=[s_], writes=[dst])
                for i, (dst, src, kc, c0, w) in enumerate(steps):
                    s_ = wst4[i % len(wst4)]
                    dma("sp", s_.t[:, 0:w], src[kc * 128:(kc + 1) * 128, c0:c0 + w], writes=[s_], sembuf=s_)
                    pend.append((i, dst, kc, c0, w, s_))
                    if len(pend) > 2:
                        cast(*pend.pop(0))
                    yield
                while pend:
                    cast(*pend.pop(0))
                    yield

            with (c.phase() if upto >= 5 else contextlib.nullcontext(False)) as _ph5:
              if upto >= 5:
                S5 = 512
                NS = 3

                def mk5(name, dt=F32):
                    return [c.sb("%s%d" % (name, i), [128, S5], dt) for i in range(NS)]
                yf, yb, bo, gg, sq5, mt5, ms5, vr5, ob = mk5("yf"), mk5("yb"), mk5("bo"), mk5("gg"), mk5("sq5"), mk5("mt5"), mk5("ms5"), mk5("vr5"), mk5("ob", BF16)
                p5 = [c.ps("p5_%d" % i, [128, 512]) for i in range(6)]

                def p5_task(b, g0, S, hp, k, it):
                    fs = slice(hp * 128, (hp + 1) * 128)
                    dma("sp", yf[k].t[:, 0:S], YT[b, 0, fs, g0:g0 + S], writes=[yf[k]], sembuf=yf[k])
                    dma("sp", yb[k].t[:, 0:S], YT[b, 1, fs, g0:g0 + S], writes=[yb[k]], sembuf=yb[k])
                    dma("sp", bo[k].t[:, 0:S], BON[b, fs, g0:g0 + S], writes=[bo[k]], sembuf=bo[k])
                    dma("sp", gg[k].t[:, 0:S], GG[b, fs, g0:g0 + S], writes=[gg[k]], sembuf=gg[k])
                    yield
                    y = yf[k]
                    op("dve", lambda e: e.tensor_tensor(y.t[:, 0:S], y.t[:, 0:S], yb[k].t[:, 0:S], ALU.add), reads=[y, yb[k]], writes=[y])
                    yield
                    op("act", lambda e: e.activation(sq5[k].t[:, 0:S], y.t[:, 0:S], AF.Square), reads=[y], writes=[sq5[k]])
                    pA, pB = p5[(it * 2) % 6], p5[(it * 2 + 1) % 6]
                    op("pe", lambda e: e.matmul(pA.t[:, 0:S], bones.t[:], y.t[:, 0:S], start=True, stop=True), reads=[bones, y], writes=[pA])
                    yield
                    op("pe", lambda e: e.matmul(pB.t[:, 0:S], bones.t[:], sq5[k].t[:, 0:S], start=True, stop=True), reads=[bones, sq5[k]], writes=[pB])
                    m, ms, vr = mt5[k], ms5[k], vr5[k]
                    op("act", lambda e: e.activation(m.t[:, 0:S], pA.t[:, 0:S], AF.Copy, scale=1.0 / 64.0), reads=[pA], writes=[m])
                    yield
                    op("dve", lambda e: e.tensor_tensor(y.t[:, 0:S], y.t[:, 0:S], m.t[:, 0:S], ALU.subtract), reads=[y, m], writes=[y])
                    op("pool", lambda e: e.tensor_tensor(ms.t[:, 0:S], m.t[:, 0:S], m.t[:, 0:S], ALU.mult), reads=[m], writes=[ms])
                    yield
                    op("dve", lambda e: e.scalar_tensor_tensor(vr.t[:, 0:S], pB.t[:, 0:S], 1.0 / 64.0, ms.t[:, 0:S], ALU.mult, ALU.subtract),
                       reads=[pB, ms], writes=[vr])
                    yield
                    op("act", lambda e: e.activation(vr.t[:, 0:S], vr.t[:, 0:S], AF.Sqrt, bias=64e-5, scale=1.0), reads=[vr], writes=[vr])
                    yield
                    op("dve", lambda e: e.reciprocal(vr.t[:, 0:S], vr.t[:, 0:S]), reads=[vr], writes=[vr])
                    yield
                    op("pool", lambda e: e.tensor_tensor(y.t[:, 0:S], y.t[:, 0:S], vr.t[:, 0:S], ALU.mult), reads=[y, vr], writes=[y])
                    yield
                    op("act", lambda e: e.activation(y.t[:, 0:S], y.t[:, 0:S], AF.Identity, bias=pk.t[:, O_GB + hp:O_GB + hp + 1],
                                                     scale=pk.t[:, O_GG + hp:O_GG + hp + 1]), reads=[y, pk], writes=[y])
                    yield
                    op("pool", lambda e: e.tensor_tensor(y.t[:, 0:S], y.t[:, 0:S], bo[k].t[:, 0:S], ALU.add), reads=[y, bo[k]], writes=[y])
                    yield
                    op("dve", lambda e: e.tensor_tensor(ob[k].t[:, 0:S], y.t[:, 0:S], gg[k].t[:, 0:S], ALU.mult), reads=[y, gg[k]], writes=[ob[k]])
                    yield
                    dma("sp", MIXT[b, 512 + hp * 128:512 + (hp + 1) * 128, g0:g0 + S], ob[k].t[:, 0:S], reads=[ob[k]], sembuf=ob[k])

                wst4 = [c.sb("wst4_%d" % i, [128, 640]) for i in range(3)]
                wl = wload_gen(wst4)
                tasks5 = []
                for b in range(NB):
                    for (g0, S) in [(0, TC)] + [(TC + s, S5) for s in range(0, T, S5)]:
                        for hp in range(4):
                            tasks5.append((b, g0, S, hp))
                live = []
                ti_ = 0
                while ti_ < len(tasks5) or live:
                    while len(live) < NS and ti_ < len(tasks5):
                        (b, g0, S, hp) = tasks5[ti_]
                        live.append(p5_task(b, g0, S, hp, ti_ % NS, ti_))
                        ti_ += 1
                    for g in list(live):
                        try:
                            next(g)
                        except StopIteration:
                            live.remove(g)
                    if wl is not None:
                        try:
                            next(wl)
                        except StopIteration:
                            wl = None
                if wl is not None:
                    for _ in wl:
                        pass

            with (c.phase() if upto >= 6 else contextlib.nullcontext(False)) as _ph6:
              if upto >= 6:
                wob = c.sb("wob", [128, 8, D], BF16)
                wst = [c.sb("wst%d" % i, [128, 512]) for i in range(2)]
                load_w_bf16(wob, wout_in[l], 8, D, wst, 512)
                po = [c.ps("po%d" % i, [128, 512]) for i in range(4)]
                GT = make_gt(0, po)
                mxs = [c.sb("mx%d" % i, [128, 8, 256], BF16) for i in range(2)]
                xts = [c.sb("xa%d" % i, [128, D]) for i in range(2)]
                tmp = [c.sb("tmpa%d" % i, [128, D]) for i in range(2)]
                nblk = 0
                nt_ = 0
                for b in range(NB):
                    seqs = [] if last else [(ctx_in if l == 0 else CXs, CXs, TC, 0, NB, False)]
                    seqs.append((x_in if l == 0 else Xs, Xs, T, TC, b, True))
                    for (src, dst, Ts, off, r, latent) in seqs:
                        for s0 in range(0, Ts, 256):
                            S = min(256, Ts - s0)
                            mx = mxs[nblk % 2]
                            nblk += 1
                            dma(q2(), mx.t[:, :, 0:S], MIXT[b].rearrange("(kc p) t -> p kc t", p=128)[:, :, off + s0:off + s0 + S],
                                writes=[mx], sembuf=mx)
                            for ti in range(S // 128):
                                xt = xts[nt_ % 2]
                                tm_ = tmp[nt_ % 2]
                                nt_ += 1
                                rows = x_rows(src[b], l, s0 // 128 + ti, latent)
                                for (p0, npart, ap) in rows:
                                    dma(q2(), xt.t[p0:p0 + npart, :], ap, writes=[xt], sembuf=xt)
                                for n2 in range(2):
                                    p = po[(nt_ * 2 + n2) % 4]
                                    for kc in range(8):
                                        op("pe", lambda e: e.matmul(p.t[:], mx.t[:, kc, ti * 128:(ti + 1) * 128], wob.t[:, kc, n2 * 512:(n2 + 1) * 512],
                                                                    start=(kc == 0), stop=(kc == 7)), reads=[mx, wob], writes=[p])
                                    op("dve", lambda e: e.tensor_tensor(tm_.t[:, n2 * 512:(n2 + 1) * 512], p.t[:], GT[r].t[:, n2 * 512:(n2 + 1) * 512], ALU.mult),
                                       reads=[p, GT[r]], writes=[tm_])
                                op("pool", lambda e: e.tensor_tensor(xt.t[:], xt.t[:], tm_.t[:], ALU.add), reads=[xt, tm_], writes=[xt])
                                for (p0, npart, ap) in x_rows(dst[b], l, s0 // 128 + ti, latent):
                                    dma(q2(), ap, xt.t[p0:p0 + npart, :], reads=[xt], sembuf=xt)

            with (c.phase() if upto >= 7 else contextlib.nullcontext(False)) as _ph7:
              if upto >= 7:
                pd = [c.ps("pd%d" % i, [128, 512]) for i in range(3)]
                GT = make_gt(1, pd)
                NBK = 256
                fgb = None
                if last:
                    fgb = c.sb("fgb", [128, D])
                    dma("sp", fgb.t[:], fg_in, writes=[fgb], sembuf=fgb)
                xts = [c.sb("xb%d" % i, [128, D]) for i in range(4)]
                xns = [c.sb("xnb%d" % i, [128, D], BF16) for i in range(2)]
                sss = [c.sb("ssb%d" % i, [128, 1]) for i in range(2)]
                rss = [c.sb("rsb%d" % i, [128, 1]) for i in range(2)]
                hTs = [c.sb("h2T%d" % i, [128, 8, NBK], BF16) for i in range(2)]
                actT = [c.sb("actT%d" % i, [128, NFC, NBK], BF16) for i in range(1)]
                sgs = [c.sb("sg%d" % i, [128, NBK]) for i in range(1)]
                ftmps = [c.sb("ftmp%d" % i, [128, 512]) for i in range(1)]
                pTs = [c.ps("pTb%d" % i, [128, D], BF16) for i in range(1)]
                pg = [c.ps("pg%d" % i, [128, 512]) for i in range(4)]
                nf = 0
                blks = []
                for b in range(NB):
                    seqs = [] if last else [(CXs, TC, NB, False)]
                    seqs.append((Xs, T, b, True))
                    for (src, Ts, r, latent) in seqs:
                        for s0 in range(0, Ts, NBK):
                            blks.append((b, src, r, latent, s0, min(NBK, Ts - s0)))

                def prep6(bi):
                    (b, src, r, latent, s0, S) = blks[bi]
                    hT = hTs[bi % 2]
                    xl = []
                    for ti in range(S // 128):
                        xt = xts[(bi * 2 + ti) % 4]
                        k = ti % 2
                        xl.append(xt)
                        for (p0, npart, ap) in x_rows(src[b], l, s0 // 128 + ti, latent):
                            dma("sp", xt.t[p0:p0 + npart, :], ap, writes=[xt], sembuf=xt)
                        norm_transpose(xt, r, G2, SH2, hT, ti * 128, sss[k], rss[k], xns[k], pTs[0])
                    return xl
                xl_next = prep6(0)
                if True:
                    if True:
                        for bi in range(len(blks)):
                            (b, src, r, latent, s0, S) = blks[bi]
                            hT = hTs[bi % 2]
                            aT = actT[0]
                            xl = xl_next
                            for fc in range(NFC):
                                nf += 1
                                p_g, p_u = pg[(nf * 2) % 4], pg[(nf * 2 + 1) % 4]
                                for kc in range(8):
                                    op("pe", lambda e: e.matmul(p_g.t[:, 0:S], wgb.t[:, kc, fc * 128:(fc + 1) * 128], hT.t[:, kc, 0:S],
                                                                start=(kc == 0), stop=(kc == 7)), reads=[wgb, hT], writes=[p_g])
                                for kc in range(8):
                                    op("pe", lambda e: e.matmul(p_u.t[:, 0:S], wub.t[:, kc, fc * 128:(fc + 1) * 128], hT.t[:, kc, 0:S],
                                                                start=(kc == 0), stop=(kc == 7)), reads=[wub, hT], writes=[p_u])
                                sg = sgs[0]
                                op("act", lambda e: e.activation(sg.t[:, 0:S], p_g.t[:, 0:S], AF.Silu), reads=[p_g], writes=[sg])
                                op("dve", lambda e: e.tensor_tensor(aT.t[:, fc, 0:S], sg.t[:, 0:S], p_u.t[:, 0:S], ALU.mult), reads=[sg, p_u], writes=[aT])
                                if fc == 12 and bi + 1 < len(blks):
                                    xl_next = prep6(bi + 1)
                            for ti in range(S // 128):
                                xt = xl[ti]
                                for n2 in range(2):
                                    nf += 1
                                    p = pd[nf % 3]
                                    for fc in range(NFC):
                                        op("pe", lambda e: e.matmul(p.t[:], aT.t[:, fc, ti * 128:(ti + 1) * 128], wdb.t[:, fc, n2 * 512:(n2 + 1) * 512],
                                                                    start=(fc == 0), stop=(fc == NFC - 1)), reads=[aT, wdb], writes=[p])
                                    sl = slice(n2 * 512, (n2 + 1) * 512)
                                    fx = ftmps[0]
                                    op("dve", lambda e: e.tensor_tensor(fx.t[:], p.t[:], GT[r].t[:, sl], ALU.mult), reads=[p, GT[r]], writes=[fx])
                                    op("pool", lambda e: e.tensor_tensor(xt.t[:, sl], xt.t[:, sl], fx.t[:], ALU.add), reads=[xt, fx], writes=[xt])
                                if last:
                                    k = ti % 2
                                    op("pool", lambda e: e.memset(sss[k].t[:], 0.0), writes=[sss[k]])
                                    op("act", lambda e: e.activation(xns[k].t[:], xt.t[:], AF.Square, scale=1.0 / 32.0, accum_out=sss[k].t[:, 0:1]),
                                       reads=[xt, sss[k]], writes=[xns[k], sss[k]])
                                    rstd_of(sss[k], rss[k])
                                    op("dve", lambda e: e.scalar_tensor_tensor(xt.t[:], xt.t[:], rss[k].t[:, 0:1], fgb.t[:], ALU.mult, ALU.mult),
                                       reads=[xt, rss[k], fgb], writes=[xt])
                                    for (p0, npart, ap) in x_rows(out[b], l, s0 // 128 + ti, True):
                                        dma("sp", ap, xt.t[p0:p0 + npart, :], reads=[xt], sembuf=xt)
                                else:
                                    for (p0, npart, ap) in x_rows(src[b], l, s0 // 128 + ti, latent):
                                        dma("sp", ap, xt.t[p0:p0 + npart, :], reads=[xt], sembuf=xt)
            ffn_st.close()
        c.barrier()
        print("instructions:", c.ninst, "dma sems:", len(c.dpool))
    return nc


def _pack(inp, L):
    pk = np.zeros((L, 128, NPK), np.float32)

    def fm(v, n):
        return np.asarray(v, np.float32).reshape(L, n, 128).transpose(0, 2, 1)
    pk[:, :, O_ADAB:O_ADAB + 48] = fm(inp["ada_b"], 48)
    pk[:, :, O_N1G:O_N1G + 8] = fm(inp["norm1_g"], 8)
    pk[:, :, O_N2G:O_N2G + 8] = fm(inp["norm2_g"], 8)
    pk[:, :, O_CW:O_CW + 124] = np.asarray(inp["conv_w"], np.float32).reshape(L, CK, 4, 128).transpose(0, 3, 2, 1).reshape(L, 128, 124)
    pk[:, :, O_CB:O_CB + 4] = fm(inp["conv_b"], 4)
    pk[:, :, O_CG:O_CG + 4] = fm(inp["cnorm_g"], 4)
    pk[:, :, O_CBB:O_CBB + 4] = fm(inp["cnorm_b"], 4)
    sw = np.zeros((L, 3, 1920), np.float32)
    sw[:, :, :1824] = np.asarray(inp["shift_w"], np.float32)
    pk[:, :, O_SW:O_SW + 45] = sw.reshape(L, 3, 15, 128).transpose(0, 3, 2, 1).reshape(L, 128, 45)
    pk[:, :, O_DB0:O_DB0 + 8] = np.asarray(inp["decay_b0"], np.float32).reshape(L, 2, 4, 128).transpose(0, 3, 1, 2).reshape(L, 128, 8)
    pk[:, :, O_IB0:O_IB0 + 8] = np.asarray(inp["iclr_b0"], np.float32).reshape(L, 2, 4, 128).transpose(0, 3, 1, 2).reshape(L, 128, 8)
    pk[:, :, O_KK:O_KK + 4] = fm(inp["k_k"], 4)
    pk[:, :, O_KA:O_KA + 4] = fm(inp["k_a"], 4)
    pk[:, :, O_RK:O_RK + 4] = fm(inp["r_k"], 4)
    pk[:, :, O_GG:O_GG + 4] = fm(inp["gn_g"], 4)
    pk[:, :, O_GB:O_GB + 4] = fm(inp["gn_b"], 4)
    return pk


def make_in_maps(inp, NB, ncores):
    L = inp["ada_w"].shape[0]
    f = lambda a: np.ascontiguousarray(np.asarray(a, np.float32))
    pk = _pack(inp, L)
    shared = {"pk": pk, "ada_b": f(inp["ada_b"]), "ada_w": f(inp["ada_w"]), "w_in": f(inp["w_in"]),
              "decay_up": f(inp["decay_up"]), "iclr_up": f(inp["iclr_up"]), "g_up": f(inp["g_up"]),
              "w_out": f(inp["w_out"]), "ffn_wg": f(inp["ffn_wg"]), "ffn_wu": f(inp["ffn_wu"]), "ffn_wd": f(inp["ffn_wd"]),
              "fg": np.ascontiguousarray(np.broadcast_to(f(inp["final_g"])[None, :], (128, D)))}
    x = f(inp["x"]); ctx = f(inp["ctx"]); cc = f(inp["c"]); c_ctx = f(inp["c_ctx"])
    maps = []
    for i in range(ncores):
        rows = np.concatenate([cc[i * NB:(i + 1) * NB], c_ctx[None, :]], axis=0)
        cT = np.ascontiguousarray(rows.reshape(NB + 1, 8, 128).transpose(2, 1, 0))
        m = dict(shared)
        m.update({"x": np.ascontiguousarray(x[i * NB:(i + 1) * NB]), "ctx": np.ascontiguousarray(ctx[i * NB:(i + 1) * NB]), "cT": cT})
        maps.append(m)
    return maps


def kernel(**inputs):
    NB, NCORES = 2, 8
    B, T, _ = inputs["x"].shape
    TC = inputs["ctx"].shape[1]
    L = inputs["ada_w"].shape[0]
    nc = build(NB, T, TC, L)
    maps = make_in_maps(inputs, NB, NCORES)
    res = run_bass_kernel_spmd(nc, maps, core_ids=list(range(NCORES)))
    return np.concatenate([np.asarray(r["out"], np.float32) for r in res.results], axis=0)
```

```python
import contextlib
import os
import numpy as np
import concourse.bass as bass
import concourse.mybir as mybir
from concourse.bass_utils import run_bass_kernel_spmd

F32 = mybir.dt.float32
BF16 = mybir.dt.bfloat16
F32R = mybir.dt.float32r
AF = mybir.ActivationFunctionType
ALU = mybir.AluOpType

D = 1024
INC = 2848
DFF = 2816
NFC = 22
CK = 31
LAM = -0.606531
INV_BF16 = False
NPK = 281
O_ADAB, O_N1G, O_N2G, O_CW, O_CB, O_CG, O_CBB, O_SW, O_DB0, O_IB0, O_KK, O_KA, O_RK, O_GG, O_GB = (
    0, 48, 56, 64, 188, 192, 196, 200, 245, 253, 261, 265, 269, 273, 277)


class Buf:
    __slots__ = ("name", "t", "lw", "rd", "semi")

    def __init__(self, name, t=None):
        self.name = name
        self.t = t
        self.lw = {}
        self.rd = {}
        self.semi = None


class Ctx:
    def __init__(self, nc, gstack):
        self.nc = nc
        self.gstack = gstack
        self.stack = gstack
        self.engs = {"pe": nc.tensor, "dve": nc.vector, "act": nc.scalar, "pool": nc.gpsimd, "sp": nc.sync}
        self.esem = {}
        self.ecnt = {}
        self.known = {k: {} for k in self.engs}
        for k in self.engs:
            self.esem[k] = gstack.enter_context(nc.semaphore("e_" + k))
            self.ecnt[k] = 0
        self.dpool = []
        self.dnext = 0
        self.ninst = 0
        self.uid = 0

    def sb(self, name, shape, dt=F32):
        self.uid += 1
        return Buf(name, self.stack.enter_context(self.nc.sbuf_tensor("%s_%d" % (name, self.uid), list(shape), dt)))

    def ps(self, name, shape, dt=F32):
        self.uid += 1
        return Buf(name, self.stack.enter_context(self.nc.psum_tensor("%s_%d" % (name, self.uid), list(shape), dt)))

    def _need(self, eng, toks):
        kn = self.known[eng]
        best = {}
        for sem, val in toks:
            k = id(sem)
            if kn.get(k, 0) >= val:
                continue
            if k not in best or best[k][1] < val:
                best[k] = (sem, val)
        for sem, val in best.values():
            self.engs[eng].wait_ge(sem, val)
            kn[id(sem)] = val

    def _deps(self, eng, reads, writes):
        toks = []
        own = self.esem[eng]
        for b in reads:
            for t in b.lw.values():
                if t[0] is own and eng == "pe":
                    continue
                toks.append(t)
        for b in writes:
            for t in b.lw.values():
                if t[0] is not own:
                    toks.append(t)
            for t in b.rd.values():
                if t[0] is not own:
                    toks.append(t)
        self._need(eng, toks)

    def op(self, eng, fn, reads=(), writes=()):
        self._deps(eng, reads, writes)
        ins = fn(self.engs[eng])
        self.ecnt[eng] += 1
        tok = (self.esem[eng], self.ecnt[eng])
        ins.then_inc(self.esem[eng], 1)
        k = id(tok[0])
        for b in writes:
            b.lw[k] = tok
        for b in reads:
            b.rd[k] = tok
        self.ninst += 1
        return ins

    def dma(self, q, out_ap, in_ap, reads=(), writes=(), sembuf=None, slow=False):
        self._deps(q, reads, writes)
        sbf = sembuf
        if sbf.semi is None:
            if self.dnext >= len(self.dpool):
                self.dpool.append([self.gstack.enter_context(self.nc.semaphore("d_%d" % len(self.dpool))), 0])
            sbf.semi = self.dnext
            self.dnext += 1
        ent = self.dpool[sbf.semi]
        ins = self.engs[q].dma_start(out=out_ap, in_=in_ap, allow_slow_non_contiguous=True) if slow else self.engs[q].dma_start(out=out_ap, in_=in_ap)
        ent[1] += 16
        ins.then_inc(ent[0], 16)
        tok = (ent[0], ent[1])
        k = id(tok[0])
        for b in writes:
            b.lw[k] = tok
        for b in reads:
            b.rd[k] = tok
        self.ninst += 1
        return ins

    def presem(self, sbf):
        self.dpool.append([self.gstack.enter_context(self.nc.semaphore("d_%d" % len(self.dpool))), 0])
        sbf.semi = len(self.dpool) - 1
        self.dnext = len(self.dpool)

    def barrier(self):
        toks = [(self.esem[k], self.ecnt[k]) for k in self.engs if self.ecnt[k] > 0]
        toks += [(e[0], e[1]) for e in self.dpool if e[1] > 0]
        for k in self.engs:
            self._need(k, toks)

    @contextlib.contextmanager
    def phase(self):
        old = self.stack
        with contextlib.ExitStack() as st:
            self.stack = st
            self.dnext_save = self.dnext
            yield
            self.barrier()
        self.stack = old
        self.dnext = self.dnext_save


def build(NB, T, TC, L, dbg=False, upto=99):
    R = NB + 1
    TT = TC + T
    ROWS = T // 64
    NCHT = TT // 64
    nc = bass.Bass("TRN2", target_bir_lowering=False)
    KI = "ExternalInput"
    SK = "ExternalOutput" if dbg else "Internal"

    def dr(name, shape, dt=F32, kind=KI):
        return nc.dram_tensor(name, list(shape), dt, kind=kind).ap()

    x_in = dr("x", [NB, T, D])
    ctx_in = dr("ctx", [NB, TC, D])
    cT_in = dr("cT", [128, 8, R])
    pk_in = dr("pk", [L, 128, NPK])
    adab_in = dr("ada_b", [L, 6144])
    adaw_in = dr("ada_w", [L, D, 6144])
    win_in = dr("w_in", [L, D, INC])
    dup_in = dr("decay_up", [L, 2, 64, 512])
    iup_in = dr("iclr_up", [L, 2, 64, 512])
    gup_in = dr("g_up", [L, 160, 512])
    wout_in = dr("w_out", [L, D, D])
    wg_in = dr("ffn_wg", [L, D, DFF])
    wu_in = dr("ffn_wu", [L, D, DFF])
    wd_in = dr("ffn_wd", [L, DFF, D])
    fg_in = dr("fg", [128, D])
    out = dr("out", [NB, T, D], kind="ExternalOutput")
    Xs = dr("Xs", [NB, T, D], kind=SK)
    CXs = dr("CXs", [NB, TC, D], kind=SK)
    UT = dr("UT", [NB, INC, TT], kind=SK)
    MIXT = dr("MIXT", [NB, D, TT], BF16, kind=SK)
    FM = dr("FM", [NB, 2, 4, 512, TT], kind=SK)
    TM = dr("TM", [NB, 2, 2, TT, 512], kind=SK)
    VT = dr("VT", [NB, TT, 512], kind=SK)
    PC = dr("PC", [NB, 2, 512, NCHT], kind=SK)
    BON = dr("BON", [NB, 512, TT], kind=SK)
    GG = dr("GG", [NB, 512, TT], kind=SK)
    YT = dr("YT", [NB, 2, 512, TT], kind=SK)

    gst = contextlib.ExitStack()
    with gst:
        c = Ctx(nc, gst)
        op, dma = c.op, c.dma
        rr = [0]

        def q2():
            rr[0] += 1
            return ("sp", "pool")[rr[0] % 2]

        identf = c.sb("identf", [128, 128])
        identb = c.sb("identb", [128, 128], BF16)
        bones = c.sb("bones", [128, 128])
        MK = [c.sb("mk%d" % d, [64, 2, 64]) for d in range(2)]
        MNT = [c.sb("mnt%d" % d, [64, 64]) for d in range(2)]
        rmask = c.sb("rmask", [128, 512])
        ones1 = c.sb("ones1", [1, 128])
        sel = c.sb("sel", [R, R, 128])
        silc = c.sb("silc", [128, 8, R])
        G1 = c.sb("G1", [128, 8, R]); SH1 = c.sb("SH1", [128, 8, R])
        G2 = c.sb("G2", [128, 8, R]); SH2 = c.sb("SH2", [128, 8, R])
        gtrow = c.sb("gtrow", [R, 2, D])
        pk = c.sb("pk", [128, NPK])
        omka = c.sb("omka", [128, 4])

        def asel(buf, ap, pattern, base, cm):
            op("pool", lambda e: e.affine_select(ap, ap, pattern=pattern, compare_op=ALU.is_ge, fill=0.0,
                                                 base=base, channel_multiplier=cm), reads=[buf], writes=[buf])

        tmpi = c.sb("tmpi", [128, 128])
        op("pool", lambda e: e.memset(identf.t[:], 1.0), writes=[identf])
        asel(identf, identf.t[:], [[1, 128]], 0, -1)
        asel(identf, identf.t[:], [[-1, 128]], 0, 1)
        op("dve", lambda e: e.tensor_copy(identb.t[:], identf.t[:]), reads=[identf], writes=[identb])
        op("pool", lambda e: e.memset(bones.t[:], 0.0), writes=[bones])
        op("pool", lambda e: e.memset(bones.t[0:64, 0:64], 1.0), writes=[bones])
        op("pool", lambda e: e.memset(bones.t[64:128, 64:128], 1.0), writes=[bones])
        for d in range(2):
            op("pool", lambda e: e.memset(MK[d].t[:], 1.0), writes=[MK[d]])
            op("pool", lambda e: e.memset(MNT[d].t[:], 1.0), writes=[MNT[d]])
        asel(MK[0], MK[0].t[:, 0, :], [[1, 64]], -1, -1)
        asel(MK[0], MK[0].t[:, 1, :], [[1, 64]], 0, -1)
        asel(MK[1], MK[1].t[:, 0, :], [[-1, 64]], -1, 1)
        asel(MK[1], MK[1].t[:, 1, :], [[-1, 64]], 0, 1)
        asel(MNT[0], MNT[0].t[:], [[-1, 64]], -1, 1)
        asel(MNT[1], MNT[1].t[:], [[1, 64]], -1, -1)
        op("pool", lambda e: e.memset(rmask.t[:], 1.0), writes=[rmask])
        op("pool", lambda e: e.memset(rmask.t[:].rearrange("p (c t) -> p c t", t=64)[:, :, 0:1], 0.0), writes=[rmask])
        op("pool", lambda e: e.memset(ones1.t[:], 1.0), writes=[ones1])
        op("dve", lambda e: e.tensor_copy(sel.t[:], identf.t[0:R, 0:R].unsqueeze(2).broadcast_to([R, R, 128])),
           reads=[identf], writes=[sel])
        c.presem(pk)
        dma("sp", silc.t[:], cT_in, writes=[silc], sembuf=silc)
        op("act", lambda e: e.activation(silc.t[:], silc.t[:], AF.Silu), reads=[silc], writes=[silc])

        def x_rows(src_b, l, i, latent):
            if (not latent) or l % 2 == 0:
                return [(0, 128, src_b[i * 128:(i + 1) * 128, :])]
            ncol = 128 // ROWS
            v = src_b.rearrange("(r c) d -> c r d", c=64)
            return [(cl * ROWS, ROWS, v[i * ncol + cl]) for cl in range(ncol)]

        def rstd_of(ss, rs):
            op("act", lambda e: e.activation(rs.t[:], ss.t[:], AF.Sqrt, bias=1e-6, scale=1.0), reads=[ss], writes=[rs])
            op("dve", lambda e: e.reciprocal(rs.t[:], rs.t[:]), reads=[rs], writes=[rs])

        for l in range(L):
            last = l == L - 1
            with (c.phase() if upto >= 0 else contextlib.nullcontext(False)) as _ph0:
              if upto >= 0:
                dma("sp", pk.t[:], pk_in[l], writes=[pk], sembuf=pk)
                adab = c.sb("adab", [1, 6144])
                dma("pool", adab.t[:], adab_in[l:l + 1, :], writes=[adab], sembuf=adab)
                aw = [c.sb("aw%d" % i, [128, 8, 1024]) for i in range(2)]
                modT = c.ps("modT", [128, 48, R])
                mrow = c.ps("mrow", [R, 2, 512])
                modsb = c.sb("modsb", [128, 48, R])
                awv = adaw_in[l].rearrange("(kc p) n -> p kc n", p=128)
                for jb in range(6):
                    a = aw[jb % 2]
                    for kc in range(8):
                        dma(q2(), a.t[:, kc, :], awv[:, kc, jb * 1024:(jb + 1) * 1024], writes=[a], sembuf=a)
                    for j in range(8):
                        for kc in range(8):
                            op("pe", lambda e: e.matmul(modT.t[:, jb * 8 + j, :], a.t[:, kc, j * 128:(j + 1) * 128],
                                                        silc.t[:, kc, :], start=(kc == 0), stop=(kc == 7)),
                               reads=[a, silc], writes=[modT])
                    if jb in (2, 5):
                        wh = 0 if jb == 2 else 1
                        for nt in range(2):
                            for kc in range(8):
                                op("pe", lambda e: e.matmul(mrow.t[:, nt, :], silc.t[:, kc, :],
                                                            a.t[:, kc, nt * 512:(nt + 1) * 512], start=(kc == 0), stop=False),
                                   reads=[a, silc], writes=[mrow])
                            op("pe", lambda e: e.matmul(mrow.t[:, nt, :], ones1.t[0:1, 0:R],
                                                        adab.t[0:1, jb * 1024 + nt * 512: jb * 1024 + (nt + 1) * 512],
                                                        start=False, stop=True), reads=[adab, ones1], writes=[mrow])
                        op("dve", lambda e: e.tensor_copy(gtrow.t[:, wh, :], mrow.t[:].rearrange("p a b -> p (a b)")),
                           reads=[mrow], writes=[gtrow])
                op("dve", lambda e: e.tensor_tensor(modsb.t[:], modT.t[:],
                                                    pk.t[:, O_ADAB:O_ADAB + 48].unsqueeze(2).broadcast_to([128, 48, R]), ALU.add),
                   reads=[modT, pk], writes=[modsb])
                for (Gx, SHx, o_sh, o_sc, o_g) in ((G1, SH1, 0, 8, O_N1G), (G2, SH2, 24, 32, O_N2G)):
                    op("dve", lambda e: e.tensor_copy(SHx.t[:], modsb.t[:, o_sh:o_sh + 8, :]), reads=[modsb], writes=[SHx])
                    op("dve", lambda e: e.tensor_scalar(Gx.t[:], modsb.t[:, o_sc:o_sc + 8, :], 1.0, None, ALU.add),
                       reads=[modsb], writes=[Gx])
                    op("dve", lambda e: e.tensor_tensor(Gx.t[:], Gx.t[:],
                                                        pk.t[:, o_g:o_g + 8].unsqueeze(2).broadcast_to([128, 8, R]), ALU.mult),
                       reads=[Gx, pk], writes=[Gx])
                op("dve", lambda e: e.tensor_scalar(omka.t[:], pk.t[:, O_KA:O_KA + 4], -1.0, 1.0, ALU.mult, ALU.add),
                   reads=[pk], writes=[omka])

            def make_gt(which, bcp):
                res = []
                for r in range(R):
                    g = c.sb("gt%d" % r, [128, D])
                    for nt in range(2):
                        p = bcp[nt]
                        op("pe", lambda e: e.matmul(p.t[:], sel.t[0:R, r, :], gtrow.t[0:R, which, nt * 512:(nt + 1) * 512],
                                                    start=True, stop=True), reads=[sel, gtrow], writes=[p])
                        op("act", lambda e: e.copy(g.t[:, nt * 512:(nt + 1) * 512], p.t[:]), reads=[p], writes=[g])
                    res.append(g)
                return res

            def norm_transpose(xt, r, Gx, SHx, hT, col0, ss, rs, xn, pT):
                op("pool", lambda e: e.memset(ss.t[:], 0.0), writes=[ss])
                op("act", lambda e: e.activation(xn.t[:], xt.t[:], AF.Square, scale=1.0 / 32.0, accum_out=ss.t[:, 0:1]),
                   reads=[xt, ss], writes=[xn, ss])
                rstd_of(ss, rs)
                op("act", lambda e: e.activation(xn.t[:], xt.t[:], AF.Copy, scale=rs.t[:, 0:1]), reads=[xt, rs], writes=[xn])
                for kc in range(8):
                    op("pe", lambda e: e.transpose(pT.t[:, kc * 128:(kc + 1) * 128], xn.t[:, kc * 128:(kc + 1) * 128], identb.t[:]),
                       reads=[xn, identb], writes=[pT])
                for kc in range(8):
                    if kc % 2 == 0:
                        op("act", lambda e: e.activation(hT.t[:, kc, col0:col0 + 128], pT.t[:, kc * 128:(kc + 1) * 128], AF.Identity,
                                                         bias=SHx.t[:, kc, r:r + 1], scale=Gx.t[:, kc, r:r + 1]),
                           reads=[pT, SHx, Gx], writes=[hT])
                    else:
                        op("dve", lambda e: e.tensor_scalar(hT.t[:, kc, col0:col0 + 128], pT.t[:, kc * 128:(kc + 1) * 128],
                                                            Gx.t[:, kc, r:r + 1], SHx.t[:, kc, r:r + 1], ALU.mult, ALU.add),
                           reads=[pT, SHx, Gx], writes=[hT])

            def load_w_bf16(dst, src_ap, nk, ncols, wst, cwid=1424):
                cnt = 0
                for kc in range(nk):
                    for c0 in range(0, ncols, cwid):
                        w = min(cwid, ncols - c0)
                        s = wst[cnt % len(wst)]
                        dma(q2(), s.t[:, 0:w], src_ap[kc * 128:(kc + 1) * 128, c0:c0 + w], writes=[s], sembuf=s)
                        e_ = ("pool", "dve", "act")[cnt % 3]
                        if e_ == "act":
                            op("act", lambda e: e.copy(dst.t[:, kc, c0:c0 + w], s.t[:, 0:w]), reads=[s], writes=[dst])
                        else:
                            op(e_, lambda e: e.tensor_copy(dst.t[:, kc, c0:c0 + w], s.t[:, 0:w]), reads=[s], writes=[dst])
                        cnt += 1

            with (c.phase() if upto >= 1 else contextlib.nullcontext(False)) as _ph1:
              if upto >= 1:
                wb = c.sb("winb", [128, 8, INC], BF16)
                wst = [c.sb("wst%d" % i, [128, 1424]) for i in range(2)]
                load_w_bf16(wb, win_in[l], 8, INC, wst)
                SEG = min(T, 2048)
                hTs = [c.sb("hT%d" % i, [128, 8, SEG], BF16) for i in range(2)]
                xts = [c.sb("xt%d" % i, [128, D]) for i in range(2)]
                xns = [c.sb("xn%d" % i, [128, D], BF16) for i in range(2)]
                sss = [c.sb("ss%d" % i, [128, 1]) for i in range(2)]
                rss = [c.sb("rs%d" % i, [128, 1]) for i in range(2)]
                pTs = [c.ps("pT%d" % i, [128, D], BF16) for i in range(2)]
                pms = [c.ps("pm%d" % i, [128, 512]) for i in range(4)]
                stg = [c.sb("stg%d" % i, [128, SEG]) for i in range(2)]
                segl = []
                for b in range(NB):
                    xsrc = (x_in if l == 0 else Xs)[b]
                    csrc = (ctx_in if l == 0 else CXs)[b]
                    for (src, Ts, off, r, latent) in ((csrc, TC, 0, NB, False), (xsrc, T, TC, b, True)):
                        for s0 in range(0, Ts, SEG):
                            segl.append((b, src, off, r, latent, s0, min(SEG, Ts - s0)))
                cnt1 = {"tile": 0, "mm": 0}

                def prep_gen(si):
                    (b, src, off, r, latent, s0, S) = segl[si]
                    hT = hTs[si % 2]
                    for i in range(S // 128):
                        k = cnt1["tile"] % 2
                        cnt1["tile"] += 1
                        for (p0, npart, ap) in x_rows(src, l, s0 // 128 + i, latent):
                            dma("sp", xts[k].t[p0:p0 + npart, :], ap, writes=[xts[k]], sembuf=xts[k])
                        norm_transpose(xts[k], r, G1, SH1, hT, i * 128, sss[k], rss[k], xns[k], pTs[k])
                        yield

                def mm_gen(si):
                    (b, src, off, r, latent, s0, S) = segl[si]
                    hT = hTs[si % 2]
                    for j in range(23):
                        cw = 128 if j < 22 else 32
                        sg = stg[j % 2]
                        for nt in range(S // 512 if S >= 512 else 1):
                            n = min(512, S)
                            pm = pms[cnt1["mm"] % 4]
                            cnt1["mm"] += 1
                            for kc in range(8):
                                op("pe", lambda e: e.matmul(pm.t[0:cw, 0:n], wb.t[:, kc, j * 128:j * 128 + cw],
                                                            hT.t[:, kc, nt * 512:nt * 512 + n], start=(kc == 0), stop=(kc == 7)),
                                   reads=[wb, hT], writes=[pm])
                            if cnt1["mm"] % 2:
                                op("act", lambda e: e.copy(sg.t[0:cw, nt * 512:nt * 512 + n], pm.t[0:cw, 0:n]), reads=[pm], writes=[sg])
                            else:
                                op("dve", lambda e: e.tensor_copy(sg.t[0:cw, nt * 512:nt * 512 + n], pm.t[0:cw, 0:n]), reads=[pm], writes=[sg])
                            yield
                        dma("sp", UT[b, j * 128:j * 128 + cw, off + s0:off + s0 + S], sg.t[0:cw, 0:S], reads=[sg], sembuf=sg)

                for _ in prep_gen(0):
                    pass
                for si in range(len(segl)):
                    mg = mm_gen(si)
                    pg_ = prep_gen(si + 1) if si + 1 < len(segl) else None
                    nst = 0
                    for _ in mg:
                        nst += 1
                        if pg_ is not None and nst % 4 == 0:
                            try:
                                next(pg_)
                            except StopIteration:
                                pg_ = None
                    if pg_ is not None:
                        for _ in pg_:
                            pass

            with (c.phase() if upto >= 2 else contextlib.nullcontext(False)) as _ph2:
              if upto >= 2:
                zps = [c.sb("zp%d" % i, [128, TT + 60], F32R) for i in range(2)]
                zer = c.sb("zer", [128, 32])
                op("pool", lambda e: e.memset(zer.t[:], 0.0), writes=[zer])
                for zp in zps:
                    for (z0, zn) in ((0, 15), (15 + TC, 30), (TC + 45 + T, 15)):
                        op("pool", lambda e: e.tensor_copy(zp.t[:, z0:z0 + zn], zer.t[:, 0:zn]), reads=[zer], writes=[zp])
                dg = c.sb("dg", [128, 4 * CK, 128], F32R)
                for i in range(4 * CK):
                    e_ = ("dve", "pool")[i % 2]
                    op(e_, lambda e: e.tensor_scalar(dg.t[:, i, :], identf.t[:], pk.t[:, O_CW + i:O_CW + i + 1], None, ALU.mult),
                       reads=[identf, pk], writes=[dg])
                gates = [c.sb("gate%d" % i, [128, TT]) for i in range(1)]
                vals = [c.sb("val%d" % i, [128, TT]) for i in range(1)]
                mixst = c.sb("mixst", [128, TT], BF16)
                NT2 = 3

                def mk2(name):
                    return [c.sb("%s%d" % (name, i), [128, 512]) for i in range(NT2)]
                accs, sqts, mts, dtts, msqs, vars_ = mk2("acc"), mk2("sqt"), mk2("mt"), mk2("dtt"), mk2("msq"), mk2("var")
                pcv = [c.ps("pcv%d" % i, [128, 512]) for i in range(3)]
                pAB = [c.ps("pAB%d" % i, [128, 512]) for i in range(4)]
                tiles = [(0 + s, 0 + s, min(512, TC - s)) for s in range(0, TC, 512)]
                tiles += [(TC + 30 + s, TC + s, 512) for s in range(0, T, 512)]
                itn = 0
                ibc = 0
                its2 = [(b, cc) for b in range(NB) for cc in range(4)]

                def prep2(i2):
                    (b, cc) = its2[i2]
                    zp = zps[i2 % 2]
                    gate, val = gates[0], vals[0]
                    dma("sp", gate.t[:], UT[b, 512 + cc * 128:512 + (cc + 1) * 128, :], writes=[gate], sembuf=gate)
                    dma("sp", val.t[:], UT[b, cc * 128:(cc + 1) * 128, :], writes=[val], sembuf=val)
                    op("act", lambda e: e.activation(gate.t[:], gate.t[:], AF.Sigmoid), reads=[gate], writes=[gate])
                    op("pool", lambda e: e.tensor_tensor(zp.t[:, 15:15 + TC], val.t[:, 0:TC], gate.t[:, 0:TC], ALU.mult),
                       reads=[val, gate], writes=[zp])
                    hT_ = T // 2
                    op("dve", lambda e: e.tensor_tensor(zp.t[:, TC + 45:TC + 45 + hT_], val.t[:, TC:TC + hT_], gate.t[:, TC:TC + hT_], ALU.mult),
                       reads=[val, gate], writes=[zp])
                    op("pool", lambda e: e.tensor_tensor(zp.t[:, TC + 45 + hT_:TC + 45 + T], val.t[:, TC + hT_:TT], gate.t[:, TC + hT_:TT], ALU.mult),
                       reads=[val, gate], writes=[zp])
                prep2(0)
                if True:
                    for i2 in range(len(its2)):
                        (b, cc) = its2[i2]
                        zp = zps[i2 % 2]
                        tcount = 0
                        for (a0, o0, n) in tiles:
                            tcount += 1
                            if tcount == min(3, len(tiles)) and i2 + 1 < len(its2):
                                prep2(i2 + 1)
                            itn += 1
                            k = itn % NT2
                            acc, sqt, mt, dtt, msq, var = accs[k], sqts[k], mts[k], dtts[k], msqs[k], vars_[k]
                            pc_ = pcv[itn % 3]
                            for j in range(CK):
                                op("pe", lambda e: e.matmul(pc_.t[:, 0:n], dg.t[:, cc * CK + j, :], zp.t[:, a0 + j:a0 + j + n],
                                                            start=(j == 0), stop=(j == CK - 1)), reads=[dg, zp], writes=[pc_])
                            op("act", lambda e: e.activation(acc.t[:, 0:n], pc_.t[:, 0:n], AF.Identity, bias=pk.t[:, O_CB + cc:O_CB + cc + 1], scale=1.0),
                               reads=[pc_, pk], writes=[acc])
                            op("act", lambda e: e.activation(sqt.t[:, 0:n], acc.t[:, 0:n], AF.Square), reads=[acc], writes=[sqt])
                            pA, pB = pAB[(itn * 2) % 4], pAB[(itn * 2 + 1) % 4]
                            op("pe", lambda e: e.matmul(pA.t[:, 0:n], bones.t[:], acc.t[:, 0:n], start=True, stop=True),
                               reads=[bones, acc], writes=[pA])
                            op("pe", lambda e: e.matmul(pB.t[:, 0:n], bones.t[:], sqt.t[:, 0:n], start=True, stop=True),
                               reads=[bones, sqt], writes=[pB])
                            op("act", lambda e: e.activation(mt.t[:, 0:n], pA.t[:, 0:n], AF.Copy, scale=1.0 / 64.0), reads=[pA], writes=[mt])
                            op("dve", lambda e: e.tensor_tensor(dtt.t[:, 0:n], acc.t[:, 0:n], mt.t[:, 0:n], ALU.subtract),
                               reads=[acc, mt], writes=[dtt])
                            op("pool", lambda e: e.tensor_tensor(msq.t[:, 0:n], mt.t[:, 0:n], mt.t[:, 0:n], ALU.mult), reads=[mt], writes=[msq])
                            op("dve", lambda e: e.scalar_tensor_tensor(var.t[:, 0:n], pB.t[:, 0:n], 1.0 / 64.0, msq.t[:, 0:n],
                                                                       ALU.mult, ALU.subtract), reads=[pB, msq], writes=[var])
                            op("dve", lambda e: e.tensor_scalar(var.t[:, 0:n], var.t[:, 0:n], 0.0, None, ALU.max), reads=[var], writes=[var])
                            op("act", lambda e: e.activation(var.t[:, 0:n], var.t[:, 0:n], AF.Sqrt, bias=1e-5, scale=1.0), reads=[var], writes=[var])
                            op("dve", lambda e: e.reciprocal(var.t[:, 0:n], var.t[:, 0:n]), reads=[var], writes=[var])
                            op("pool", lambda e: e.tensor_tensor(dtt.t[:, 0:n], dtt.t[:, 0:n], var.t[:, 0:n], ALU.mult), reads=[dtt, var], writes=[dtt])
                            op("act", lambda e: e.activation(mixst.t[:, o0:o0 + n], dtt.t[:, 0:n], AF.Silu,
                                                             bias=pk.t[:, O_CBB + cc:O_CBB + cc + 1], scale=pk.t[:, O_CG + cc:O_CG + cc + 1]),
                               reads=[dtt, pk], writes=[mixst])
                        dma("sp", MIXT[b, cc * 128:(cc + 1) * 128, :], mixst.t[:], reads=[mixst], sembuf=mixst)

            with (c.phase() if upto >= 3 else contextlib.nullcontext(False)) as _ph3:
              if upto >= 3:
                dup = c.sb("dup", [64, 2, 512]); iup = c.sb("iup", [128, 2, 512])
                gup1 = c.sb("gup1", [128, 512]); gup2 = c.sb("gup2", [32, 512])
                for d in range(2):
                    dma("sp", dup.t[:, d, :], dup_in[l, d], writes=[dup], sembuf=dup)
                    dma("sp", iup.t[64:128, d, :], iup_in[l, d], writes=[iup], sembuf=iup)
                dma("pool", gup1.t[:], gup_in[l, 0:128, :], writes=[gup1], sembuf=gup1)
                dma("pool", gup2.t[:], gup_in[l, 128:160, :], writes=[gup2], sembuf=gup2)
                S3 = 512
                NS = 2
                pp = [c.ps("pp%d" % i, [128, 512]) for i in range(8)]
                pi = [0]

                def nps():
                    pi[0] += 1
                    return pp[pi[0] % 8]

                def mk(name, shape=None, dt=F32):
                    return [c.sb("%s%d" % (name, i), shape or [128, S3], dt) for i in range(NS)]
                inb = {k_: mk("in" + k_, [128, S3 + 2]) for k_ in ("r", "k", "v", "wa", "g1", "g2")}
                B_ = {k_: mk(k_) for k_ in ("r", "k", "v", "wa", "g1", "g2", "gst", "rk", "bon", "kkr", "sq", "rn", "kk",
                                             "sig", "a", "t1", "kd", "bd", "cs", "ex", "rem", "cb", "E1", "E2", "E3", "E4",
                                             "kh", "bh")}
                for k_ in ("o0", "o1", "o2", "o3"):
                    B_[k_] = mk(k_, None, F32)
                tms = {k_: mk("tm" + k_, [128, 4, 128]) for k_ in ("v", "kh", "bh")}
                pcs = mk("pcs", [128, 8])
                it = 0

                def shift(dst, ib, ch, S, g0, first, lastseg, b, rows=128):
                    f0 = 1024 + ch * 128
                    lo = g0 - (0 if first else 1)
                    hi = g0 + S + (0 if lastseg else 1)
                    if first:
                        op("pool", lambda e: e.memset(ib.t[0:rows, 0:1], 0.0), writes=[ib])
                    if lastseg:
                        op("pool", lambda e: e.memset(ib.t[0:rows, S + 1:S + 2], 0.0), writes=[ib])
                    dma("sp", ib.t[0:rows, (1 if first else 0):(1 if first else 0) + hi - lo], UT[b, f0:f0 + rows, lo:hi], writes=[ib], sembuf=ib)
                    sw = O_SW + ch * 3
                    op("act", lambda e: e.activation(dst.t[0:rows, 0:S], ib.t[0:rows, 0:S], AF.Copy, scale=pk.t[0:rows, sw:sw + 1]),
                       reads=[ib, pk], writes=[dst])
                    for j in (1, 2):
                        op("dve", lambda e: e.scalar_tensor_tensor(dst.t[0:rows, 0:S], ib.t[0:rows, j:j + S], pk.t[0:rows, sw + j:sw + j + 1],
                                                                   dst.t[0:rows, 0:S], ALU.mult, ALU.add), reads=[ib, pk, dst], writes=[dst])

                def hp_task(b, g0, S, first, lastseg, hp, ks, wa, g1, g2):
                    NCS = S // 64
                    pend = []

                    def st(dst_ap, src_ap, buf):
                        pend.append((dst_ap, src_ap, buf))

                    def flush():
                        for (dst_ap, src_ap, buf) in pend:
                            dma("sp", dst_ap, src_ap, reads=[buf], sembuf=buf)
                        del pend[:]
                    X = {k_: v_[ks] for k_, v_ in B_.items()}
                    r_, k_b, v_ = X["r"], X["k"], X["v"]
                    shift(r_, inb["r"][ks], hp, S, g0, first, lastseg, b)
                    yield
                    shift(k_b, inb["k"][ks], 4 + hp, S, g0, first, lastseg, b)
                    yield
                    shift(v_, inb["v"][ks], 8 + hp, S, g0, first, lastseg, b)
                    fs = slice(hp * 128, (hp + 1) * 128)
                    p = nps()
                    op("pe", lambda e: e.matmul(p.t[:, 0:S], gup1.t[:, fs], g1.t[:, 0:S], start=True, stop=False), reads=[gup1, g1], writes=[p])
                    op("pe", lambda e: e.matmul(p.t[:, 0:S], gup2.t[0:32, fs], g2.t[0:32, 0:S], start=False, stop=True), reads=[gup2, g2], writes=[p])
                    yield
                    gs = X["gst"]
                    op("act", lambda e: e.copy(gs.t[:, 0:S], p.t[:, 0:S]), reads=[p], writes=[gs])
                    st(GG[b, fs, g0:g0 + S], gs.t[:, 0:S], gs)
                    rk = X["rk"]
                    op("dve", lambda e: e.scalar_tensor_tensor(rk.t[:, 0:S], r_.t[:, 0:S], pk.t[:, O_RK + hp:O_RK + hp + 1], k_b.t[:, 0:S],
                                                               ALU.mult, ALU.mult), reads=[r_, k_b, pk], writes=[rk])
                    kkr, sq, rn, kk = X["kkr"], X["sq"], X["rn"], X["kk"]
                    op("pool", lambda e: e.tensor_scalar(kkr.t[:, 0:S], k_b.t[:, 0:S], pk.t[:, O_KK + hp:O_KK + hp + 1], None, ALU.mult),
                       reads=[k_b, pk], writes=[kkr])
                    yield
                    flush()
                    p1 = nps()
                    op("pe", lambda e: e.matmul(p1.t[:, 0:S], bones.t[:], rk.t[:, 0:S], start=True, stop=True), reads=[bones, rk], writes=[p1])
                    op("act", lambda e: e.activation(sq.t[:, 0:S], kkr.t[:, 0:S], AF.Square), reads=[kkr], writes=[sq])
                    yield
                    bon = X["bon"]
                    op("dve", lambda e: e.tensor_tensor(bon.t[:, 0:S], p1.t[:, 0:S], v_.t[:, 0:S], ALU.mult), reads=[p1, v_], writes=[bon])
                    st(BON[b, fs, g0:g0 + S], bon.t[:, 0:S], bon)
                    p2 = nps()
                    op("pe", lambda e: e.matmul(p2.t[:, 0:S], bones.t[:], sq.t[:, 0:S], start=True, stop=True), reads=[bones, sq], writes=[p2])
                    yield
                    flush()
                    op("act", lambda e: e.activation(rn.t[:, 0:S], p2.t[:, 0:S], AF.Sqrt), reads=[p2], writes=[rn])
                    yield
                    op("dve", lambda e: e.tensor_scalar(rn.t[:, 0:S], rn.t[:, 0:S], 1e-12, None, ALU.max), reads=[rn], writes=[rn])
                    yield
                    op("dve", lambda e: e.reciprocal(rn.t[:, 0:S], rn.t[:, 0:S]), reads=[rn], writes=[rn])
                    yield
                    op("pool", lambda e: e.tensor_tensor(kk.t[:, 0:S], kkr.t[:, 0:S], rn.t[:, 0:S], ALU.mult), reads=[kkr, rn], writes=[kk])

                    def transp_out(src, tmb, dst_ap):
                        p = nps()
                        nb_ = S // 128
                        for tb in range(nb_):
                            op("pe", lambda e: e.transpose(p.t[:, tb * 128:(tb + 1) * 128], src.t[:, tb * 128:(tb + 1) * 128], identf.t[:]),
                               reads=[src, identf], writes=[p])
                        yield
                        op("act", lambda e: e.copy(tmb.t[:, 0:nb_, :], p.t[:, 0:nb_ * 128].rearrange("p (a f) -> p a f", f=128)),
                           reads=[p], writes=[tmb])
                        st(dst_ap.rearrange("(tb p) f -> p tb f", p=128), tmb.t[:, 0:nb_, :], tmb)
                    yield from transp_out(v_, tms["v"][ks], VT[b, g0:g0 + S, fs])
                    for d in range(2):
                        sig, a_, t1, kd, bd = X["sig"], X["a"], X["t1"], X["kd"], X["bd"]
                        pa_ = nps()
                        op("pe", lambda e: e.matmul(pa_.t[:, 0:S], dup.t[0:64, d, fs], wa.t[0:64, 0:S], start=True, stop=True), reads=[dup, wa], writes=[pa_])
                        pb_ = nps()
                        op("pe", lambda e: e.matmul(pb_.t[:, 0:S], iup.t[64:128, d, fs], wa.t[64:128, 0:S], start=True, stop=True), reads=[iup, wa], writes=[pb_])
                        yield
                        flush()
                        op("act", lambda e: e.activation(sig.t[:, 0:S], pa_.t[:, 0:S], AF.Sigmoid, bias=pk.t[:, O_DB0 + d * 4 + hp:O_DB0 + d * 4 + hp + 1]),
                           reads=[pa_, pk], writes=[sig])
                        op("act", lambda e: e.activation(a_.t[:, 0:S], pb_.t[:, 0:S], AF.Sigmoid, bias=pk.t[:, O_IB0 + d * 4 + hp:O_IB0 + d * 4 + hp + 1]),
                           reads=[pb_, pk], writes=[a_])
                        yield
                        cs, ex, rem, cb = X["cs"], X["ex"], X["rem"], X["cb"]
                        op("dve", lambda e: e.tensor_tensor_scan(cs.t[:, 0:S], rmask.t[:, 0:S], sig.t[:, 0:S], 0.0, ALU.mult, ALU.add),
                           reads=[rmask, sig], writes=[cs])
                        op("pool", lambda e: e.tensor_tensor(bd.t[:, 0:S], kk.t[:, 0:S], a_.t[:, 0:S], ALU.mult), reads=[kk, a_], writes=[bd])
                        yield
                        op("act", lambda e: e.activation(t1.t[:, 0:S], a_.t[:, 0:S], AF.Identity, bias=omka.t[:, hp:hp + 1],
                                                         scale=pk.t[:, O_KA + hp:O_KA + hp + 1]), reads=[a_, pk, omka], writes=[t1])
                        op("pool", lambda e: e.tensor_tensor(ex.t[:, 0:S], cs.t[:, 0:S], sig.t[:, 0:S], ALU.subtract), reads=[cs, sig], writes=[ex])
                        csv = cs.t[:, 0:S].rearrange("p (c t) -> p c t", t=64)
                        pc_ = pcs[ks]
                        op("act", lambda e: e.activation(pc_.t[:, 0:NCS], csv[:, :, 63], AF.Exp, scale=LAM), reads=[cs], writes=[pc_])
                        st(PC[b, d, fs, g0 // 64:g0 // 64 + NCS], pc_.t[:, 0:NCS], pc_)
                        yield
                        flush()
                        op("dve", lambda e: e.tensor_tensor(rem.t[:, 0:S].rearrange("p (c t) -> p c t", t=64),
                                                            csv[:, :, 63:64].broadcast_to([128, NCS, 64]), csv, ALU.subtract),
                           reads=[cs], writes=[rem])
                        op("pool", lambda e: e.tensor_tensor(kd.t[:, 0:S], t1.t[:, 0:S], k_b.t[:, 0:S], ALU.mult), reads=[t1, k_b], writes=[kd])
                        yield
                        if d == 0:
                            incl, excl, aft = cs, ex, rem
                        else:
                            op("pool", lambda e: e.tensor_tensor(cb.t[:, 0:S], rem.t[:, 0:S], sig.t[:, 0:S], ALU.add), reads=[rem, sig], writes=[cb])
                            incl, excl, aft = cb, rem, ex
                            yield
                        E1, E2, E3, E4 = X["E1"], X["E2"], X["E3"], X["E4"]
                        op("act", lambda e: e.activation(E3.t[:, 0:S], incl.t[:, 0:S], AF.Exp, scale=-LAM), reads=[incl], writes=[E3])
                        op("act", lambda e: e.activation(E4.t[:, 0:S], aft.t[:, 0:S], AF.Exp, scale=LAM), reads=[aft], writes=[E4])
                        yield
                        op("act", lambda e: e.activation(E2.t[:, 0:S], excl.t[:, 0:S], AF.Exp, scale=LAM), reads=[excl], writes=[E2])
                        op("act", lambda e: e.activation(E1.t[:, 0:S], incl.t[:, 0:S], AF.Exp, scale=LAM), reads=[incl], writes=[E1])
                        kh, bh = X["kh"], X["bh"]
                        op("dve", lambda e: e.tensor_tensor(kh.t[:, 0:S], kd.t[:, 0:S], E4.t[:, 0:S], ALU.mult), reads=[kd, E4], writes=[kh])
                        op("pool", lambda e: e.tensor_tensor(bh.t[:, 0:S], bd.t[:, 0:S], E4.t[:, 0:S], ALU.mult), reads=[bd, E4], writes=[bh])
                        yield
                        outs = ((X["o0"], kd, E3), (X["o1"], bd, E3), (X["o2"], kk, E2), (X["o3"], r_, E1))
                        for qi, (o_, a1, a2) in enumerate(outs):
                            op(("dve", "pool")[qi % 2], lambda e: e.tensor_tensor(o_.t[:, 0:S], a1.t[:, 0:S], a2.t[:, 0:S], ALU.mult),
                               reads=[a1, a2], writes=[o_])
                            st(FM[b, d, qi, fs, g0:g0 + S], o_.t[:, 0:S], o_)
                            if qi == 1:
                                yield
                        yield from transp_out(kh, tms["kh"][ks], TM[b, d, 0, g0:g0 + S, fs])
                        flush()
                        yield from transp_out(bh, tms["bh"][ks], TM[b, d, 1, g0:g0 + S, fs])
                    yield
                    flush()

                tasks = []
                segno = 0
                for b in range(NB):
                    segs = [(0, TC, True, True)]
                    segs += [(TC + s, S3, s == 0, s + S3 == T) for s in range(0, T, S3)]
                    for (g0, S, first, lastseg) in segs:
                        for hp in range(4):
                            tasks.append((b, g0, S, first, lastseg, hp, segno % NS))
                        segno += 1
                live = []
                ti_ = 0
                while ti_ < len(tasks) or live:
                    while len(live) < NS and ti_ < len(tasks):
                        (b, g0, S, first, lastseg, hp, k0) = tasks[ti_]
                        wa, g1, g2 = B_["wa"][k0], B_["g1"][k0], B_["g2"][k0]
                        if hp == 0:
                            shift(wa, inb["wa"][k0], 12, S, g0, first, lastseg, b)
                            shift(g1, inb["g1"][k0], 13, S, g0, first, lastseg, b)
                            shift(g2, inb["g2"][k0], 14, S, g0, first, lastseg, b, rows=32)
                            op("act", lambda e: e.activation(wa.t[0:64, 0:S], wa.t[0:64, 0:S], AF.Tanh), reads=[wa], writes=[wa])
                            op("act", lambda e: e.activation(g1.t[:, 0:S], g1.t[:, 0:S], AF.Sigmoid), reads=[g1], writes=[g1])
                            op("act", lambda e: e.activation(g2.t[0:32, 0:S], g2.t[0:32, 0:S], AF.Sigmoid), reads=[g2], writes=[g2])
                        live.append(hp_task(b, g0, S, first, lastseg, hp, ti_ % NS, wa, g1, g2))
                        ti_ += 1
                    for g in list(live):
                        try:
                            next(g)
                        except StopIteration:
                            live.remove(g)

            with (c.phase() if upto >= 4 else contextlib.nullcontext(False)) as _ph4:
              if upto >= 4:
                pbk = [c.ps("pb%d" % i, [128, 512]) for i in range(8)]
                cn = {"pb": 0, "tp": 0, "e": 0, "ht": 0}

                def npb():
                    cn["pb"] += 1
                    return pbk[cn["pb"] % 8]
                tp = [c.sb("tp%d" % i, [64, 8, 64], F32R) for i in range(8 if INV_BF16 else 12)]

                def ntp():
                    cn["tp"] += 1
                    return tp[cn["tp"] % len(tp)]
                tpq = [c.sb("tpq%d" % i, [64, 8, 64], BF16) for i in range(12)] if INV_BF16 else tp
                cn["tpq"] = 0

                def ntq():
                    if not INV_BF16:
                        return ntp()
                    cn["tpq"] += 1
                    return tpq[cn["tpq"] % len(tpq)]
                htmp = [c.sb("htmp%d" % i, [128, 4, 128]) for i in range(2)]
                bmask = c.sb("bmask", [128, 2])
                op("pool", lambda e: e.memset(bmask.t[:], 0.0), writes=[bmask])
                op("pool", lambda e: e.memset(bmask.t[0:64, 0:1], 1.0), writes=[bmask])
                op("pool", lambda e: e.memset(bmask.t[64:128, 1:2], 1.0), writes=[bmask])

                def V8(p):
                    return p.t[0:64, :].rearrange("p (h t) -> p h t", t=64)

                def V4(p):
                    return p.t[0:64, :].rearrange("p (h q t) -> p h q t", q=2, t=64)

                def VH(p):
                    return p.t[:, :].rearrange("p (a f) -> p a f", f=128)

                def evac(dst_ap, src_ap, reads, writes, scale=None):
                    cn["e"] += 1
                    if cn["e"] % 2 and scale is None:
                        op("dve", lambda e: e.tensor_copy(dst_ap, src_ap), reads=reads, writes=writes)
                    else:
                        op("act", lambda e: e.activation(dst_ap, src_ap, AF.Copy, scale=(1.0 if scale is None else scale)), reads=reads, writes=writes)
                strs = []
                for b in range(NB):
                    for d in range(2):
                        cl = list(range(0, TC, 64))
                        ll = list(range(TC, TT, 64))
                        i_ = len(strs)
                        strs.append(dict(
                            b=b, d=d, chunks=(cl + ll) if d == 0 else (cl[::-1] + ll[::-1]),
                            fm32=c.sb("fm32_%d" % i_, [128, 4, 4, 64]), tm32=c.sb("tm32_%d" % i_, [64, 2, 512]), vt32=c.sb("vt32_%d" % i_, [64, 512]),
                            pc=[c.sb("pc%d_%d" % (i_, k), [128, 4, 1]) for k in range(2)],
                            tm=c.sb("tm_%d" % i_, [64, 2, 512], F32R), vt=c.sb("vt_%d" % i_, [64, 512], F32R),
                            yst=c.sb("yst_%d" % i_, [64, 8, 64]), H=c.sb("H_%d" % i_, [128, 4, 128]), Hb=c.sb("Hb_%d" % i_, [128, 4, 128], F32R),
                            bd=c.sb("bd_%d" % i_, [128, 4, 4, 2, 64], F32R), AK=c.sb("AK_%d" % i_, [64, 8, 2, 64], F32R),
                            AB=c.sb("AB_%d" % i_, [64, 8, 3, 64], F32R),
                            XPb=(c.sb("XPb_%d" % i_, [64, 8, 2, 64], BF16) if INV_BF16 else None)))

                def load(S_, n):
                    g0 = S_["chunks"][n]
                    b, d = S_["b"], S_["d"]
                    dma("sp", S_["fm32"].t[:], FM[b, d].rearrange("q (hp p) t -> p q hp t", p=128)[:, :, :, g0:g0 + 64],
                        writes=[S_["fm32"]], sembuf=S_["fm32"])
                    dma("sp", S_["tm32"].t[:], TM[b, d, :, g0:g0 + 64, :].rearrange("q p f -> p q f"), writes=[S_["tm32"]], sembuf=S_["tm32"])
                    dma("sp", S_["vt32"].t[:], VT[b, g0:g0 + 64, :], writes=[S_["vt32"]], sembuf=S_["vt32"])
                    pcb = S_["pc"][n % 2]
                    dma("sp", pcb.t[:], PC[b, d].rearrange("(hp p) c -> p hp c", p=128)[:, :, g0 // 64:g0 // 64 + 1], writes=[pcb], sembuf=pcb, slow=True)

                def store_y(S_, n):
                    g0 = S_["chunks"][n]
                    dma("sp", YT[S_["b"], S_["d"]].rearrange("(h v) t -> v h t", v=64)[:, :, g0:g0 + 64], S_["yst"].t[:],
                        reads=[S_["yst"]], sembuf=S_["yst"])

                def chunk_gen(S_, n):
                    d = S_["d"]
                    bd, AK, AB, tm, vt, H, Hb, yst = S_["bd"], S_["AK"], S_["AB"], S_["tm"], S_["vt"], S_["H"], S_["Hb"], S_["yst"]
                    pcb = S_["pc"][n % 2]
                    fm32, tm32, vt32 = S_["fm32"], S_["tm32"], S_["vt32"]
                    if n > 0:
                        store_y(S_, n - 1)
                    op("pool", lambda e: e.tensor_tensor(
                        bd.t[:].rearrange("p a b q t -> p (a b) q t"),
                        fm32.t[:].rearrange("p a b t -> p (a b) t").unsqueeze(2).broadcast_to([128, 16, 2, 64]),
                        bmask.t[:].unsqueeze(1).unsqueeze(3).broadcast_to([128, 16, 2, 64]), ALU.mult), reads=[fm32, bmask], writes=[bd])
                    op("act", lambda e: e.copy(tm.t[:], tm32.t[:]), reads=[tm32], writes=[tm])
                    op("dve", lambda e: e.tensor_copy(vt.t[:], vt32.t[:]), reads=[vt32], writes=[vt])
                    yield

                    def BD(qty, h):
                        return bd.t[:, qty, h // 2, h % 2, :]
                    if n + 1 < len(S_["chunks"]):
                        load(S_, n + 1)
                    for (dst, qty) in ((AK, 0), (AB, 1)):
                        dsl = slice(0, 2) if qty == 0 else slice(1, 3)
                        pa = [npb(), npb()]
                        for h in range(8):
                            op("pe", lambda e: e.matmul(V4(pa[h // 4])[:, h % 4, :, :], BD(qty, h), bd.t[:, 2:4, h // 2, h % 2, :], start=True, stop=True),
                               reads=[bd], writes=[pa[h // 4]])
                        for hh in range(2):
                            op("dve", lambda e: e.tensor_tensor(dst.t[:, hh * 4:hh * 4 + 4, dsl, :], V4(pa[hh]),
                                                                MK[d].t[:].unsqueeze(1).broadcast_to([64, 4, 2, 64]), ALU.mult),
                               reads=[pa[hh], MK[d]], writes=[dst])
                        yield
                    pn = npb()
                    for h in range(8):
                        op("pe", lambda e: e.matmul(V8(pn)[:, h, :], BD(2, h), BD(1, h), start=True, stop=True), reads=[bd], writes=[pn])
                    Q = ntq()
                    op("dve", lambda e: e.tensor_tensor(Q.t[:], V8(pn), MNT[d].t[:].unsqueeze(1).broadcast_to([64, 8, 64]), ALU.mult),
                       reads=[pn, MNT[d]], writes=[Q])
                    XT = S_["XPb"] if INV_BF16 else AB
                    op("pool", lambda e: e.tensor_tensor(XT.t[:, :, 0, :], identf.t[0:64, 0:64].unsqueeze(1).broadcast_to([64, 8, 64]), AB.t[:, :, 1, :], ALU.subtract),
                       reads=[identf, AB], writes=[XT])
                    if INV_BF16:
                        op("act", lambda e: e.copy(XT.t[:, :, 1, :], AB.t[:, :, 1, :]), reads=[AB], writes=[XT])
                    yield
                    pP = npb()
                    pQ = npb()
                    for h in range(8):
                        op("pe", lambda e: e.matmul(V8(pP)[:, h, :], Q.t[:, h, :], XT.t[:, h, 1, :], start=True, stop=True), reads=[Q, XT], writes=[pP])
                    for h in range(8):
                        op("pe", lambda e: e.matmul(V8(pQ)[:, h, :], XT.t[:, h, 1, :], Q.t[:, h, :], start=True, stop=True), reads=[Q, XT], writes=[pQ])
                    evac(XT.t[:, :, 1, :], V8(pP), [pP], [XT])
                    Qn = ntq()
                    evac(Qn.t[:], V8(pQ), [pQ], [Qn])
                    Q = Qn
                    yield
                    for r_ in range(1, 5):
                        pa = [npb(), npb()]
                        nq = 2 if r_ < 4 else 1
                        for h in range(8):
                            op("pe", lambda e: e.matmul(V4(pa[h // 4])[:, h % 4, 0:nq, :], Q.t[:, h, :], XT.t[:, h, 0:nq, :], start=True, stop=True),
                               reads=[Q, XT], writes=[pa[h // 4]])
                        pQ = npb()
                        for h in range(8):
                            op("pe", lambda e: e.matmul(V8(pQ)[:, h, :], XT.t[:, h, 1, :], Q.t[:, h, :], start=True, stop=True), reads=[Q, XT], writes=[pQ])
                        for hh in range(2):
                            op("dve", lambda e: e.tensor_tensor(XT.t[:, hh * 4:hh * 4 + 4, 0, :], V4(pa[hh])[:, :, 0, :], XT.t[:, hh * 4:hh * 4 + 4, 0, :], ALU.add),
                               reads=[pa[hh], XT], writes=[XT])
                            if r_ < 4:
                                evac(XT.t[:, hh * 4:hh * 4 + 4, 1, :], V4(pa[hh])[:, :, 1, :], [pa[hh]], [XT])
                        Qn = ntq()
                        evac(Qn.t[:], V8(pQ), [pQ], [Qn])
                        Q = Qn
                        yield
                    pX = npb()
                    for h in range(8):
                        op("pe", lambda e: e.matmul(V8(pX)[:, h, :], Q.t[:, h, :], XT.t[:, h, 0, :], start=True, stop=True), reads=[Q, XT], writes=[pX])
                    op("dve", lambda e: e.tensor_tensor(XT.t[:, :, 0, :], V8(pX), XT.t[:, :, 0, :], ALU.add), reads=[pX, XT], writes=[XT])
                    if INV_BF16:
                        op("act", lambda e: e.copy(AB.t[:, :, 0, :], XT.t[:, :, 0, :]), reads=[XT], writes=[AB])
                    yield
                    pW = npb()
                    for h in range(8):
                        hp, qq = h // 2, h % 2
                        op("pe", lambda e: e.matmul(V8(pW)[:, h, :], BD(2, h), Hb.t[:, hp, qq * 64:qq * 64 + 64], start=True, stop=False),
                           reads=[bd, Hb], writes=[pW])
                        op("pe", lambda e: e.matmul(V8(pW)[:, h, :], AK.t[:, h, 0, :], vt.t[:, h * 64:(h + 1) * 64], start=False, stop=True),
                           reads=[AK, vt], writes=[pW])
                    W1 = ntp()
                    evac(W1.t[:], V8(pW), [pW], [W1])
                    yield
                    pU = npb()
                    for h in range(8):
                        op("pe", lambda e: e.matmul(V8(pU)[:, h, :], AB.t[:, h, 0, :], W1.t[:, h, :], start=True, stop=True), reads=[AB, W1], writes=[pU])
                    Un = ntp()
                    evac(Un.t[:], V8(pU), [pU], [Un], scale=-1.0)
                    yield
                    pY = npb()
                    for h in range(8):
                        hp, qq = h // 2, h % 2
                        op("pe", lambda e: e.matmul(V8(pY)[:, h, :], Hb.t[:, hp, qq * 64:qq * 64 + 64], BD(3, h), start=True, stop=False),
                           reads=[bd, Hb], writes=[pY])
                        op("pe", lambda e: e.matmul(V8(pY)[:, h, :], vt.t[:, h * 64:(h + 1) * 64], AK.t[:, h, 1, :], start=False, stop=False),
                           reads=[AK, vt], writes=[pY])
                        op("pe", lambda e: e.matmul(V8(pY)[:, h, :], Un.t[:, h, :], AB.t[:, h, 2, :], start=False, stop=True),
                           reads=[AB, Un], writes=[pY])
                    evac(yst.t[:], V8(pY), [pY], [yst])
                    pH = npb()
                    Unf = Un.t[:].rearrange("p h t -> p (h t)")
                    for hp in range(4):
                        fs = slice(hp * 128, (hp + 1) * 128)
                        op("pe", lambda e: e.matmul(VH(pH)[:, hp, :], tm.t[:, 0, fs], vt.t[:, fs], start=True, stop=False), reads=[tm, vt], writes=[pH])
                        op("pe", lambda e: e.matmul(VH(pH)[:, hp, :], tm.t[:, 1, fs], Unf[:, fs], start=False, stop=True), reads=[tm, Un], writes=[pH])
                    cn["ht"] += 1
                    ht = htmp[cn["ht"] % 2]
                    op("pool", lambda e: e.tensor_tensor(ht.t[:], H.t[:], pcb.t[:].broadcast_to([128, 4, 128]), ALU.mult), reads=[H, pcb], writes=[ht])
                    op("dve", lambda e: e.tensor_tensor(H.t[:], ht.t[:], VH(pH), ALU.add), reads=[ht, pH], writes=[H])
                    op("act", lambda e: e.copy(Hb.t[:], H.t[:]), reads=[H], writes=[Hb])
                    yield

                for S_ in strs:
                    op("pool", lambda e: e.memset(S_["H"].t[:], 0.0), writes=[S_["H"]])
                    op("act", lambda e: e.copy(S_["Hb"].t[:], S_["H"].t[:]), reads=[S_["H"]], writes=[S_["Hb"]])
                    load(S_, 0)
                nchunks = len(strs[0]["chunks"])
                for n in range(nchunks):
                    gens = [chunk_gen(S_, n) for S_ in strs]
                    while gens:
                        for g in list(gens):
                            try:
                                next(g)
                            except StopIteration:
                                gens.remove(g)
                for S_ in strs:
                    store_y(S_, nchunks - 1)

            ffn_st = contextlib.ExitStack()
            _prev = c.stack
            c.stack = ffn_st
            wgb = c.sb("wgb", [128, 8, DFF], BF16)
            wub = c.sb("wub", [128, 8, DFF], BF16)
            wdb = c.sb("wdb", [128, NFC, D], BF16)
            c.stack = _prev

            def wload_gen(wst4):
                steps = []
                for (dst, src, nk, ncols) in ((wgb, wg_in[l], 8, DFF), (wub, wu_in[l], 8, DFF), (wdb, wd_in[l], NFC, D)):
                    for kc in range(nk):
                        for c0 in range(0, ncols, 640):
                            steps.append((dst, src, kc, c0, min(640, ncols - c0)))
                pend = []

                def cast(i, dst, kc, c0, w, s_):
                    if i % 2:
                        op("act", lambda e: e.copy(dst.t[:, kc, c0:c0 + w], s_.t[:, 0:w]), reads=[s_], writes=[dst])
                    else:
                        op("pool", lambda e: e.tensor_copy(dst.t[:, kc, c0:c0 + w], s_.t[:, 0:w]), reads=[s_], writes=[dst])
                for i, (dst, src, kc, c0, w) in enumerate(steps):
                    s_ = wst4[i % len(wst4)]
                    dma("sp", s_.t[:, 0:w], src[kc * 128:(kc + 1) * 128, c0:c0 + w], writes=[s_], sembuf=s_)
                    pend.append((i, dst, kc, c0, w, s_))
                    if len(pend) > 2:
                        cast(*pend.pop(0))
                    yield
                while pend:
                    cast(*pend.pop(0))
                    yield

            with (c.phase() if upto >= 5 else contextlib.nullcontext(False)) as _ph5:
              if upto >= 5:
                S5 = 512
                NS = 3

                def mk5(name, dt=F32):
                    return [c.sb("%s%d" % (name, i), [128, S5], dt) for i in range(NS)]
                yf, yb, bo, gg, sq5, mt5, ms5, vr5, ob = mk5("yf"), mk5("yb"), mk5("bo"), mk5("gg"), mk5("sq5"), mk5("mt5"), mk5("ms5"), mk5("vr5"), mk5("ob", BF16)
                p5 = [c.ps("p5_%d" % i, [128, 512]) for i in range(6)]

                def p5_task(b, g0, S, hp, k, it):
                    fs = slice(hp * 128, (hp + 1) * 128)
                    dma("sp", yf[k].t[:, 0:S], YT[b, 0, fs, g0:g0 + S], writes=[yf[k]], sembuf=yf[k])
                    dma("sp", yb[k].t[:, 0:S], YT[b, 1, fs, g0:g0 + S], writes=[yb[k]], sembuf=yb[k])
                    dma("sp", bo[k].t[:, 0:S], BON[b, fs, g0:g0 + S], writes=[bo[k]], sembuf=bo[k])
                    dma("sp", gg[k].t[:, 0:S], GG[b, fs, g0:g0 + S], writes=[gg[k]], sembuf=gg[k])
                    yield
                    y = yf[k]
                    op("dve", lambda e: e.tensor_tensor(y.t[:, 0:S], y.t[:, 0:S], yb[k].t[:, 0:S], ALU.add), reads=[y, yb[k]], writes=[y])
                    yield
                    op("act", lambda e: e.activation(sq5[k].t[:, 0:S], y.t[:, 0:S], AF.Square), reads=[y], writes=[sq5[k]])
                    pA, pB = p5[(it * 2) % 6], p5[(it * 2 + 1) % 6]
                    op("pe", lambda e: e.matmul(pA.t[:, 0:S], bones.t[:], y.t[:, 0:S], start=True, stop=True), reads=[bones, y], writes=[pA])
                    yield
                    op("pe", lambda e: e.matmul(pB.t[:, 0:S], bones.t[:], sq5[k].t[:, 0:S], start=True, stop=True), reads=[bones, sq5[k]], writes=[pB])
                    m, ms, vr = mt5[k], ms5[k], vr5[k]
                    op("act", lambda e: e.activation(m.t[:, 0:S], pA.t[:, 0:S], AF.Copy, scale=1.0 / 64.0), reads=[pA], writes=[m])
                    yield
                    op("dve", lambda e: e.tensor_tensor(y.t[:, 0:S], y.t[:, 0:S], m.t[:, 0:S], ALU.subtract), reads=[y, m], writes=[y])
                    op("pool", lambda e: e.tensor_tensor(ms.t[:, 0:S], m.t[:, 0:S], m.t[:, 0:S], ALU.mult), reads=[m], writes=[ms])
                    yield
                    op("dve", lambda e: e.scalar_tensor_tensor(vr.t[:, 0:S], pB.t[:, 0:S], 1.0 / 64.0, ms.t[:, 0:S], ALU.mult, ALU.subtract),
                       reads=[pB, ms], writes=[vr])
                    yield
                    op("dve", lambda e: e.tensor_scalar(vr.t[:, 0:S], vr.t[:, 0:S], 0.0, None, ALU.max), reads=[vr], writes=[vr])
                    yield
                    op("act", lambda e: e.activation(vr.t[:, 0:S], vr.t[:, 0:S], AF.Sqrt, bias=64e-5, scale=1.0), reads=[vr], writes=[vr])
                    yield
                    op("dve", lambda e: e.reciprocal(vr.t[:, 0:S], vr.t[:, 0:S]), reads=[vr], writes=[vr])
                    yield
                    op("pool", lambda e: e.tensor_tensor(y.t[:, 0:S], y.t[:, 0:S], vr.t[:, 0:S], ALU.mult), reads=[y, vr], writes=[y])
                    yield
                    op("act", lambda e: e.activation(y.t[:, 0:S], y.t[:, 0:S], AF.Identity, bias=pk.t[:, O_GB + hp:O_GB + hp + 1],
                                                     scale=pk.t[:, O_GG + hp:O_GG + hp + 1]), reads=[y, pk], writes=[y])
                    yield
                    op("pool", lambda e: e.tensor_tensor(y.t[:, 0:S], y.t[:, 0:S], bo[k].t[:, 0:S], ALU.add), reads=[y, bo[k]], writes=[y])
                    yield
                    op("dve", lambda e: e.tensor_tensor(ob[k].t[:, 0:S], y.t[:, 0:S], gg[k].t[:, 0:S], ALU.mult), reads=[y, gg[k]], writes=[ob[k]])
                    yield
                    dma("sp", MIXT[b, 512 + hp * 128:512 + (hp + 1) * 128, g0:g0 + S], ob[k].t[:, 0:S], reads=[ob[k]], sembuf=ob[k])

                wst4 = [c.sb("wst4_%d" % i, [128, 640]) for i in range(3)]
                wl = wload_gen(wst4)
                tasks5 = []
                for b in range(NB):
                    for (g0, S) in [(0, TC)] + [(TC + s, S5) for s in range(0, T, S5)]:
                        for hp in range(4):
                            tasks5.append((b, g0, S, hp))
                live = []
                ti_ = 0
                while ti_ < len(tasks5) or live:
                    while len(live) < NS and ti_ < len(tasks5):
                        (b, g0, S, hp) = tasks5[ti_]
                        live.append(p5_task(b, g0, S, hp, ti_ % NS, ti_))
                        ti_ += 1
                    for g in list(live):
                        try:
                            next(g)
                        except StopIteration:
                            live.remove(g)
                    if wl is not None:
                        try:
                            next(wl)
                        except StopIteration:
                            wl = None
                if wl is not None:
                    for _ in wl:
                        pass

            with (c.phase() if upto >= 6 else contextlib.nullcontext(False)) as _ph6:
              if upto >= 6:
                wob = c.sb("wob", [128, 8, D], BF16)
                wst = [c.sb("wst%d" % i, [128, 512]) for i in range(2)]
                load_w_bf16(wob, wout_in[l], 8, D, wst, 512)
                po = [c.ps("po%d" % i, [128, 512]) for i in range(4)]
                GT = make_gt(0, po)
                mxs = [c.sb("mx%d" % i, [128, 8, 256], BF16) for i in range(2)]
                xts = [c.sb("xa%d" % i, [128, D]) for i in range(2)]
                tmp = [c.sb("tmpa%d" % i, [128, D]) for i in range(2)]
                nblk = 0
                nt_ = 0
                for b in range(NB):
                    seqs = [] if last else [(ctx_in if l == 0 else CXs, CXs, TC, 0, NB, False)]
                    seqs.append((x_in if l == 0 else Xs, Xs, T, TC, b, True))
                    for (src, dst, Ts, off, r, latent) in seqs:
                        for s0 in range(0, Ts, 256):
                            S = min(256, Ts - s0)
                            mx = mxs[nblk % 2]
                            nblk += 1
                            dma(q2(), mx.t[:, :, 0:S], MIXT[b].rearrange("(kc p) t -> p kc t", p=128)[:, :, off + s0:off + s0 + S],
                                writes=[mx], sembuf=mx)
                            for ti in range(S // 128):
                                xt = xts[nt_ % 2]
                                tm_ = tmp[nt_ % 2]
                                nt_ += 1
                                rows = x_rows(src[b], l, s0 // 128 + ti, latent)
                                for (p0, npart, ap) in rows:
                                    dma(q2(), xt.t[p0:p0 + npart, :], ap, writes=[xt], sembuf=xt)
                                for n2 in range(2):
                                    p = po[(nt_ * 2 + n2) % 4]
                                    for kc in range(8):
                                        op("pe", lambda e: e.matmul(p.t[:], mx.t[:, kc, ti * 128:(ti + 1) * 128], wob.t[:, kc, n2 * 512:(n2 + 1) * 512],
                                                                    start=(kc == 0), stop=(kc == 7)), reads=[mx, wob], writes=[p])
                                    op("dve", lambda e: e.tensor_tensor(tm_.t[:, n2 * 512:(n2 + 1) * 512], p.t[:], GT[r].t[:, n2 * 512:(n2 + 1) * 512], ALU.mult),
                                       reads=[p, GT[r]], writes=[tm_])
                                op("pool", lambda e: e.tensor_tensor(xt.t[:], xt.t[:], tm_.t[:], ALU.add), reads=[xt, tm_], writes=[xt])
                                for (p0, npart, ap) in x_rows(dst[b], l, s0 // 128 + ti, latent):
                                    dma(q2(), ap, xt.t[p0:p0 + npart, :], reads=[xt], sembuf=xt)

            with (c.phase() if upto >= 7 else contextlib.nullcontext(False)) as _ph7:
              if upto >= 7:
                pd = [c.ps("pd%d" % i, [128, 512]) for i in range(3)]
                GT = make_gt(1, pd)
                NBK = 256
                fgb = None
                if last:
                    fgb = c.sb("fgb", [128, D])
                    dma("sp", fgb.t[:], fg_in, writes=[fgb], sembuf=fgb)
                xts = [c.sb("xb%d" % i, [128, D]) for i in range(4)]
                xns = [c.sb("xnb%d" % i, [128, D], BF16) for i in range(2)]
                sss = [c.sb("ssb%d" % i, [128, 1]) for i in range(2)]
                rss = [c.sb("rsb%d" % i, [128, 1]) for i in range(2)]
                hTs = [c.sb("h2T%d" % i, [128, 8, NBK], BF16) for i in range(2)]
                actT = [c.sb("actT%d" % i, [128, NFC, NBK], BF16) for i in range(1)]
                sgs = [c.sb("sg%d" % i, [128, NBK]) for i in range(1)]
                ftmps = [c.sb("ftmp%d" % i, [128, 512]) for i in range(1)]
                pTs = [c.ps("pTb%d" % i, [128, D], BF16) for i in range(1)]
                pg = [c.ps("pg%d" % i, [128, 512]) for i in range(4)]
                nf = 0
                blks = []
                for b in range(NB):
                    seqs = [] if last else [(CXs, TC, NB, False)]
                    seqs.append((Xs, T, b, True))
                    for (src, Ts, r, latent) in seqs:
                        for s0 in range(0, Ts, NBK):
                            blks.append((b, src, r, latent, s0, min(NBK, Ts - s0)))

                def prep6(bi):
                    (b, src, r, latent, s0, S) = blks[bi]
                    hT = hTs[bi % 2]
                    xl = []
                    for ti in range(S // 128):
                        xt = xts[(bi * 2 + ti) % 4]
                        k = ti % 2
                        xl.append(xt)
                        for (p0, npart, ap) in x_rows(src[b], l, s0 // 128 + ti, latent):
                            dma("sp", xt.t[p0:p0 + npart, :], ap, writes=[xt], sembuf=xt)
                        norm_transpose(xt, r, G2, SH2, hT, ti * 128, sss[k], rss[k], xns[k], pTs[0])
                    return xl
                xl_next = prep6(0)
                if True:
                    if True:
                        for bi in range(len(blks)):
                            (b, src, r, latent, s0, S) = blks[bi]
                            hT = hTs[bi % 2]
                            aT = actT[0]
                            xl = xl_next
                            for fc in range(NFC):
                                nf += 1
                                p_g, p_u = pg[(nf * 2) % 4], pg[(nf * 2 + 1) % 4]
                                for kc in range(8):
                                    op("pe", lambda e: e.matmul(p_g.t[:, 0:S], wgb.t[:, kc, fc * 128:(fc + 1) * 128], hT.t[:, kc, 0:S],
                                                                start=(kc == 0), stop=(kc == 7)), reads=[wgb, hT], writes=[p_g])
                                for kc in range(8):
                                    op("pe", lambda e: e.matmul(p_u.t[:, 0:S], wub.t[:, kc, fc * 128:(fc + 1) * 128], hT.t[:, kc, 0:S],
                                                                start=(kc == 0), stop=(kc == 7)), reads=[wub, hT], writes=[p_u])
                                sg = sgs[0]
                                op("act", lambda e: e.activation(sg.t[:, 0:S], p_g.t[:, 0:S], AF.Silu), reads=[p_g], writes=[sg])
                                op("dve", lambda e: e.tensor_tensor(aT.t[:, fc, 0:S], sg.t[:, 0:S], p_u.t[:, 0:S], ALU.mult), reads=[sg, p_u], writes=[aT])
                                if fc == 12 and bi + 1 < len(blks):
                                    xl_next = prep6(bi + 1)
                            for ti in range(S // 128):
                                xt = xl[ti]
                                for n2 in range(2):
                                    nf += 1
                                    p = pd[nf % 3]
                                    for fc in range(NFC):
                                        op("pe", lambda e: e.matmul(p.t[:], aT.t[:, fc, ti * 128:(ti + 1) * 128], wdb.t[:, fc, n2 * 512:(n2 + 1) * 512],
                                                                    start=(fc == 0), stop=(fc == NFC - 1)), reads=[aT, wdb], writes=[p])
                                    sl = slice(n2 * 512, (n2 + 1) * 512)
                                    fx = ftmps[0]
                                    op("dve", lambda e: e.tensor_tensor(fx.t[:], p.t[:], GT[r].t[:, sl], ALU.mult), reads=[p, GT[r]], writes=[fx])
                                    op("pool", lambda e: e.tensor_tensor(xt.t[:, sl], xt.t[:, sl], fx.t[:], ALU.add), reads=[xt, fx], writes=[xt])
                                if last:
                                    k = ti % 2
                                    op("pool", lambda e: e.memset(sss[k].t[:], 0.0), writes=[sss[k]])
                                    op("act", lambda e: e.activation(xns[k].t[:], xt.t[:], AF.Square, scale=1.0 / 32.0, accum_out=sss[k].t[:, 0:1]),
                                       reads=[xt, sss[k]], writes=[xns[k], sss[k]])
                                    rstd_of(sss[k], rss[k])
                                    op("dve", lambda e: e.scalar_tensor_tensor(xt.t[:], xt.t[:], rss[k].t[:, 0:1], fgb.t[:], ALU.mult, ALU.mult),
                                       reads=[xt, rss[k], fgb], writes=[xt])
                                    for (p0, npart, ap) in x_rows(out[b], l, s0 // 128 + ti, True):
                                        dma("sp", ap, xt.t[p0:p0 + npart, :], reads=[xt], sembuf=xt)
                                else:
                                    for (p0, npart, ap) in x_rows(src[b], l, s0 // 128 + ti, latent):
                                        dma("sp", ap, xt.t[p0:p0 + npart, :], reads=[xt], sembuf=xt)
            ffn_st.close()
        c.barrier()
        print("instructions:", c.ninst, "dma sems:", len(c.dpool))
    return nc


def _pack(inp, L):
    pk = np.zeros((L, 128, NPK), np.float32)

    def fm(v, n):
        return np.asarray(v, np.float32).reshape(L, n, 128).transpose(0, 2, 1)
    pk[:, :, O_ADAB:O_ADAB + 48] = fm(inp["ada_b"], 48)
    pk[:, :, O_N1G:O_N1G + 8] = fm(inp["norm1_g"], 8)
    pk[:, :, O_N2G:O_N2G + 8] = fm(inp["norm2_g"], 8)
    pk[:, :, O_CW:O_CW + 124] = np.asarray(inp["conv_w"], np.float32).reshape(L, CK, 4, 128).transpose(0, 3, 2, 1).reshape(L, 128, 124)
    pk[:, :, O_CB:O_CB + 4] = fm(inp["conv_b"], 4)
    pk[:, :, O_CG:O_CG + 4] = fm(inp["cnorm_g"], 4)
    pk[:, :, O_CBB:O_CBB + 4] = fm(inp["cnorm_b"], 4)
    sw = np.zeros((L, 3, 1920), np.float32)
    sw[:, :, :1824] = np.asarray(inp["shift_w"], np.float32)
    pk[:, :, O_SW:O_SW + 45] = sw.reshape(L, 3, 15, 128).transpose(0, 3, 2, 1).reshape(L, 128, 45)
    pk[:, :, O_DB0:O_DB0 + 8] = np.asarray(inp["decay_b0"], np.float32).reshape(L, 2, 4, 128).transpose(0, 3, 1, 2).reshape(L, 128, 8)
    pk[:, :, O_IB0:O_IB0 + 8] = np.asarray(inp["iclr_b0"], np.float32).reshape(L, 2, 4, 128).transpose(0, 3, 1, 2).reshape(L, 128, 8)
    pk[:, :, O_KK:O_KK + 4] = fm(inp["k_k"], 4)
    pk[:, :, O_KA:O_KA + 4] = fm(inp["k_a"], 4)
    pk[:, :, O_RK:O_RK + 4] = fm(inp["r_k"], 4)
    pk[:, :, O_GG:O_GG + 4] = fm(inp["gn_g"], 4)
    pk[:, :, O_GB:O_GB + 4] = fm(inp["gn_b"], 4)
    return pk


def make_in_maps(inp, NB, ncores):
    L = inp["ada_w"].shape[0]
    f = lambda a: np.ascontiguousarray(np.asarray(a, np.float32))
    pk = _pack(inp, L)
    shared = {"pk": pk, "ada_b": f(inp["ada_b"]), "ada_w": f(inp["ada_w"]), "w_in": f(inp["w_in"]),
              "decay_up": f(inp["decay_up"]), "iclr_up": f(inp["iclr_up"]), "g_up": f(inp["g_up"]),
              "w_out": f(inp["w_out"]), "ffn_wg": f(inp["ffn_wg"]), "ffn_wu": f(inp["ffn_wu"]), "ffn_wd": f(inp["ffn_wd"]),
              "fg": np.ascontiguousarray(np.broadcast_to(f(inp["final_g"])[None, :], (128, D)))}
    x = f(inp["x"]); ctx = f(inp["ctx"]); cc = f(inp["c"]); c_ctx = f(inp["c_ctx"])
    maps = []
    for i in range(ncores):
        rows = np.concatenate([cc[i * NB:(i + 1) * NB], c_ctx[None, :]], axis=0)
        cT = np.ascontiguousarray(rows.reshape(NB + 1, 8, 128).transpose(2, 1, 0))
        m = dict(shared)
        m.update({"x": np.ascontiguousarray(x[i * NB:(i + 1) * NB]), "ctx": np.ascontiguousarray(ctx[i * NB:(i + 1) * NB]), "cT": cT})
        maps.append(m)
    return maps


def kernel(**inputs):
    NB, NCORES = 2, 8
    B, T, _ = inputs["x"].shape
    TC = inputs["ctx"].shape[1]
    L = inputs["ada_w"].shape[0]
    nc = build(NB, T, TC, L)
    maps = make_in_maps(inputs, NB, NCORES)
    res = run_bass_kernel_spmd(nc, maps, core_ids=list(range(NCORES)))
    return np.concatenate([np.asarray(r["out"], np.float32) for r in res.results], axis=0)
```
